# Optimizing a Trainium2 kernel written in Bass

```python
import jax, jax.numpy as jnp
from jax import lax
import numpy as np

D_MODEL = 2048
BATCH = 4
SEQ = 8192
DEPTH = 1

PLE_DIM = 256
HEAD_DIM = 128
ATTN_HEADS = 8
RET_HEADS = 8
ATTN_WIDTH = ATTN_HEADS * HEAD_DIM
RET_WIDTH = RET_HEADS * HEAD_DIM
MIX_WIDTH = ATTN_WIDTH + RET_WIDTH
IN_COLS = 3 * ATTN_WIDTH + 4 * RET_WIDTH
D_FF = -(-(8 * D_MODEL) // (3 * 256)) * 256
MOBA_BLOCK = 256
MOBA_TOPK = 3
MOBA_Q_CHUNK = 32
RET_CHUNK = 128
ROPE_THETA = 10000.0
RMS_EPS = 1e-6
NEG_INF = -1e30

kernel_name = "hymba_moba_retnet_hybrid_layer"


def rms_norm(x, g):
    xf = x.astype(jnp.float32)
    y = xf * lax.rsqrt(jnp.mean(xf * xf, axis=-1, keepdims=True) + RMS_EPS)
    return (y * g.astype(jnp.float32)).astype(x.dtype)


def rotate_half_split(x, pos, inv_freq):
    ang = pos[:, None] * inv_freq[None, :]
    cos = jnp.cos(ang).astype(x.dtype)
    sin = jnp.sin(ang).astype(x.dtype)
    x1, x2 = jnp.split(x, 2, axis=-1)
    return jnp.concatenate([x1 * cos - x2 * sin, x2 * cos + x1 * sin], axis=-1)


def moba_attention(q, k, v):
    B, H, T, d = q.shape
    BS = MOBA_BLOCK
    QC = MOBA_Q_CHUNK
    NB = -(-T // BS)
    Tp = NB * BS
    NC = T // QC
    K_SEL = min(MOBA_TOPK, max(NB - 1, 1))
    scale = 1.0 / np.sqrt(d)

    pad = ((0, 0), (0, 0), (0, Tp - T), (0, 0))
    kb = jnp.pad(k, pad).reshape(B, H, NB, BS, d)
    vb = jnp.pad(v, pad).reshape(B, H, NB, BS, d)

    k_mean = jnp.mean(kb.astype(jnp.float32), axis=3)
    gate = jnp.einsum('bhtd,bhnd->bhtn', q.astype(jnp.float32), k_mean)
    q_block = jnp.arange(T) // BS
    past = jnp.arange(NB)[None, :] < q_block[:, None]
    gate = jnp.where(past[None, None], gate, NEG_INF)
    _, sel_idx = lax.top_k(gate, K_SEL)

    b_idx = jnp.arange(B)[:, None, None, None]
    h_idx = jnp.arange(H)[None, :, None, None]

    def chunk_step(args):
        c, q_c, idx_c = args
        t = c * QC + jnp.arange(QC)
        kg = kb[b_idx, h_idx, idx_c].reshape(B, H, QC, K_SEL * BS, d)
        vg = vb[b_idx, h_idx, idx_c].reshape(B, H, QC, K_SEL * BS, d)
        valid = jnp.arange(K_SEL)[None, :] < (t // BS)[:, None]
        valid = jnp.repeat(valid, BS, axis=1)
        s_sel = jnp.einsum('bhqd,bhqnd->bhqn', q_c, kg).astype(jnp.float32) * scale
        s_sel = jnp.where(valid[None, None], s_sel, NEG_INF)
        b0 = (c * QC) // BS
        k_own = lax.dynamic_index_in_dim(kb, b0, axis=2, keepdims=False)
        v_own = lax.dynamic_index_in_dim(vb, b0, axis=2, keepdims=False)
        key_pos = b0 * BS + jnp.arange(BS)
        causal = key_pos[None, :] <= t[:, None]
        s_own = jnp.einsum('bhqd,bhkd->bhqk', q_c, k_own).astype(jnp.float32) * scale
        s_own = jnp.where(causal[None, None], s_own, NEG_INF)
        probs = jax.nn.softmax(jnp.concatenate([s_sel, s_own], axis=-1), axis=-1)
        p_sel = probs[..., :K_SEL * BS].astype(v.dtype)
        p_own = probs[..., K_SEL * BS:].astype(v.dtype)
        return (jnp.einsum('bhqn,bhqnd->bhqd', p_sel, vg)
                + jnp.einsum('bhqk,bhkd->bhqd', p_own, v_own))

    q_ch = jnp.moveaxis(q.reshape(B, H, NC, QC, d), 2, 0)
    idx_ch = jnp.moveaxis(sel_idx.reshape(B, H, NC, QC, K_SEL), 2, 0)
    out = lax.map(chunk_step, (jnp.arange(NC), q_ch, idx_ch))
    return jnp.moveaxis(out, 0, 2).reshape(B, H, T, d)


def retention_chunkwise(q, k, v):
    B, H, T, d = q.shape
    dv = v.shape[-1]
    C = RET_CHUNK
    N = T // C
    log_g = jnp.log(1.0 - jnp.power(2.0, -5.0 - jnp.arange(H, dtype=jnp.float32)))
    j = jnp.arange(C, dtype=jnp.float32)
    rel = j[:, None] - j[None, :]
    decay = jnp.where(rel >= 0, jnp.exp(log_g[:, None, None] * jnp.maximum(rel, 0.0)), 0.0)

    qc = q.reshape(B, H, N, C, d)
    kc = k.reshape(B, H, N, C, d)
    vc = v.reshape(B, H, N, C, dv)

    s = jnp.einsum('bhncd,bhnmd->bhncm', qc, kc) * decay[None, :, None]
    intra = jnp.einsum('bhncm,bhnme->bhnce', s, vc)

    k_dec = kc * jnp.exp(log_g[:, None] * (C - 1.0 - j)[None, :])[None, :, None, :, None]
    U = jnp.einsum('bhncd,bhnce->nbhde', k_dec, vc)
    g_chunk = jnp.exp(log_g * C)[None, :, None, None]

    def step(S, U_n):
        return g_chunk * S + U_n, S

    _, S_prev = lax.scan(step, jnp.zeros((B, H, d, dv), jnp.float32), U)
    q_dec = qc * jnp.exp(log_g[:, None] * (j + 1.0)[None, :])[None, :, None, :, None]
    cross = jnp.einsum('bhncd,nbhde->bhnce', q_dec, S_prev)
    return (intra + cross).reshape(B, H, T, dv)


def setup_inputs(seed: int = 0) -> dict:
    key = jax.random.key(seed)
    ks = jax.random.split(key, 20)
    f32 = jnp.float32
    nrm = lambda k, shape, fan: jax.random.normal(k, shape, f32) * (fan ** -0.5)
    gain = lambda k, shape: 1.0 + 0.05 * jax.random.normal(k, shape, f32)
    return {
        "x": jax.random.normal(ks[0], (BATCH, SEQ, D_MODEL), f32),
        "p": jax.random.normal(ks[1], (DEPTH, BATCH, SEQ, PLE_DIM), f32),
        "g_mix": gain(ks[2], (DEPTH, D_MODEL)),
        "w_in": nrm(ks[3], (DEPTH, D_MODEL, IN_COLS), D_MODEL),
        "q_norm": gain(ks[4], (DEPTH, HEAD_DIM)),
        "k_norm": gain(ks[5], (DEPTH, HEAD_DIM)),
        "g_ret": gain(ks[6], (DEPTH, RET_WIDTH)),
        "w_o": nrm(ks[7], (DEPTH, MIX_WIDTH, D_MODEL), MIX_WIDTH),
        "g_ffn": gain(ks[8], (DEPTH, D_MODEL)),
        "w_gate": nrm(ks[9], (DEPTH, D_MODEL, D_FF), D_MODEL),
        "w_up": nrm(ks[10], (DEPTH, D_MODEL, D_FF), D_MODEL),
        "w_down": nrm(ks[11], (DEPTH, D_FF, D_MODEL), D_FF),
        "g_ple": gain(ks[12], (DEPTH, D_MODEL)),
        "w_ple_gate": nrm(ks[13], (DEPTH, D_MODEL, D_MODEL), D_MODEL),
        "b_ple_gate": 0.01 * jax.random.normal(ks[14], (DEPTH, D_MODEL), f32),
        "w_ple_proj": nrm(ks[15], (DEPTH, PLE_DIM, D_MODEL), PLE_DIM),
    }


def reference(x, p, g_mix, w_in, q_norm, k_norm, g_ret, w_o, g_ffn, w_gate, w_up,
              w_down, g_ple, w_ple_gate, b_ple_gate, w_ple_proj):
    B, T, _ = x.shape
    pos = jnp.arange(T, dtype=jnp.float32)
    inv_freq_attn = jnp.power(ROPE_THETA, -jnp.arange(0, HEAD_DIM, 2, dtype=jnp.float32) / HEAD_DIM)
    inv_freq_ret = jnp.power(ROPE_THETA, -jnp.linspace(0.0, 1.0, HEAD_DIM // 2, dtype=jnp.float32))
    split_at = np.cumsum([ATTN_WIDTH] * 3 + [RET_WIDTH] * 3).tolist()

    def heads(t, n):
        return t.reshape(B, T, n, HEAD_DIM).transpose(0, 2, 1, 3)

    for i in range(DEPTH):
        h = rms_norm(x, g_mix[i])
        z = h @ w_in[i]
        aq, ak, av, rq, rk, rv, rg = jnp.split(z, split_at, axis=-1)

        aq = rotate_half_split(heads(rms_norm(aq.reshape(B, T, ATTN_HEADS, HEAD_DIM), q_norm[i]).reshape(B, T, ATTN_WIDTH), ATTN_HEADS), pos, inv_freq_attn)
        ak = rotate_half_split(heads(rms_norm(ak.reshape(B, T, ATTN_HEADS, HEAD_DIM), k_norm[i]).reshape(B, T, ATTN_WIDTH), ATTN_HEADS), pos, inv_freq_attn)
        a_out = moba_attention(aq, ak, heads(av, ATTN_HEADS))
        a_out = a_out.transpose(0, 2, 1, 3).reshape(B, T, ATTN_WIDTH)

        rq_h = rotate_half_split(heads(rq, RET_HEADS), pos, inv_freq_ret).astype(jnp.float32)
        rk_h = (rotate_half_split(heads(rk, RET_HEADS), pos, inv_freq_ret).astype(jnp.float32)
                * (HEAD_DIM ** -0.5))
        r = retention_chunkwise(rq_h, rk_h, heads(rv, RET_HEADS).astype(jnp.float32))
        r = r * lax.rsqrt(jnp.mean(r * r, axis=-1, keepdims=True) + RMS_EPS)
        r = r.transpose(0, 2, 1, 3).reshape(B, T, RET_WIDTH) * g_ret[i].astype(jnp.float32)
        r_out = (jax.nn.silu(rg.astype(jnp.float32)) * r).astype(x.dtype)

        x = x + jnp.concatenate([a_out, r_out], axis=-1) @ w_o[i]

        h2 = rms_norm(x, g_ffn[i])
        x = x + (jax.nn.silu(h2 @ w_gate[i]) * (h2 @ w_up[i])) @ w_down[i]

        gate = jax.nn.sigmoid(rms_norm(x, g_ple[i]) @ w_ple_gate[i] + b_ple_gate[i])
        x = x + (p[i] @ w_ple_proj[i]) * gate
    return x
```

```python
import numpy as np
import contextlib
import ml_dtypes
import concourse.bass as bass
import concourse.mybir as mybir
from concourse.bass_utils import run_bass_kernel_spmd
from concourse.alu_op_type import AluOpType as ALU

F32 = mybir.dt.float32
BF16 = mybir.dt.bfloat16
AF = mybir.ActivationFunctionType
AX = mybir.AxisListType

D = 2048
KD = 16
HD = 128
H = 8
INC = 7168
DFF = 5632
KF = 44
PLE = 256
EPS = 1e-6
BIG = 30000.0
NEGP = -1.0e9

N_DMA_SEMS = 32
N_SW_SEMS = 8
ENGS = ("pe", "act", "dve", "pool", "sp")


class Buf:
    __slots__ = ("w", "r", "pr", "name")

    def __init__(self, name=""):
        self.w = {}
        self.r = {}
        self.pr = set()
        self.name = name


class Op:
    __slots__ = ("eng", "fn", "deps", "idx", "pos", "inc", "tick", "is_dma", "dsem", "dval")


class Sched:
    def __init__(self, nc, stack):
        self.nc = nc
        self.sem = {e: stack.enter_context(nc.semaphore("s_" + e)) for e in ENGS if e != "sp"}
        self.dsems = [stack.enter_context(nc.semaphore("d%d" % i)) for i in range(N_DMA_SEMS)]
        self.drr_sw = 0
        self.tickbase = {e: 0 for e in self.sem}
        self.dcount = [0] * N_DMA_SEMS
        self.drr = 0
        self.waited = {e: {} for e in ENGS}
        self.ops = []
        self.pos = {e: 0 for e in ENGS}
        self.phase_dma = {}
        self.touched = set()
        self.lastop = {}

    def op(self, eng, fn, reads=(), writes=(), pwrites=(), dma=False):
        o = Op()
        o.eng = eng
        o.fn = fn
        o.idx = len(self.ops)
        o.is_dma = dma
        o.inc = False
        o.tick = 0
        o.pos = self.pos[eng]
        self.pos[eng] += 1
        deps = set()
        for b in reads:
            deps.update(b.w.values())
        for b in writes:
            deps.update(b.r.values())
            deps.update(b.w.values())
        for b in pwrites:
            deps.update(b.r.values())
            if b.r:
                deps.update(b.w.values())
            else:
                deps.update(b.pr)
        deps.discard(o.idx)
        o.deps = deps
        if dma:
            if eng == "pool":
                s = N_DMA_SEMS - N_SW_SEMS + self.drr_sw
                self.drr_sw = (self.drr_sw + 1) % N_SW_SEMS
            else:
                s = self.drr
                self.drr = (self.drr + 1) % (N_DMA_SEMS - N_SW_SEMS)
            if s in self.phase_dma:
                deps.add(self.phase_dma[s])
            self.dcount[s] += 1
            o.dsem = s
            o.dval = 16 * self.dcount[s]
            self.phase_dma[s] = o.idx
        key = ("dma", o.idx) if dma else eng
        if not dma:
            self.lastop[eng] = o.idx
        for b in reads:
            self.touched.add(b)
            b.r[key] = o.idx
        for b in writes:
            self.touched.add(b)
            b.pr = set(b.r.values()) | set(b.w.values())
            b.w = {key: o.idx}
            b.r = {}
        for b in pwrites:
            self.touched.add(b)
            if b.r:
                b.pr = set(b.r.values()) | set(b.w.values())
                b.w = {key: o.idx}
                b.r = {}
            else:
                b.w[key] = o.idx
        self.ops.append(o)
        return o

    def _needs(self, c, p):
        if p.is_dma:
            return True
        if p.eng == c.eng:
            if c.eng == "pe":
                return False
            if c.is_dma:
                return True
            if c.eng == "pool":
                return True
            return p.pos >= c.pos - 3
        return True

    def emit(self):
        nc = self.nc
        ops = self.ops
        f = Op()
        f.eng = "sp"; f.fn = None; f.idx = len(ops); f.is_dma = False
        f.inc = False; f.tick = 0; f.pos = self.pos["sp"]
        f.deps = set(self.phase_dma.values())
        for e, i in self.lastop.items():
            if e != "sp":
                f.deps.add(i)
        ops.append(f)
        for o in ops:
            for d in o.deps:
                p = ops[d]
                if (not p.is_dma) and self._needs(o, p):
                    p.inc = True
        cnt = dict(self.tickbase)
        for o in ops:
            if (not o.is_dma) and o.inc:
                cnt[o.eng] += 1
                o.tick = cnt[o.eng]
        per = {e: [] for e in ENGS}
        for o in ops:
            per[o.eng].append(o)
        waited = self.waited
        sem = self.sem
        dsems = self.dsems

        def run(e, eng):
            wd = waited[e]
            for o in per[e]:
                for d in sorted(o.deps):
                    p = ops[d]
                    if not self._needs(o, p):
                        continue
                    if p.is_dma:
                        k = ("d", p.dsem)
                        if wd.get(k, 0) < p.dval:
                            eng.wait_ge(dsems[p.dsem], p.dval)
                            wd[k] = p.dval
                    else:
                        k = p.eng
                        if wd.get(k, 0) < p.tick:
                            eng.wait_ge(sem[p.eng], p.tick)
                            wd[k] = p.tick
                if o.fn is None:
                    continue
                inst = o.fn(eng)
                if o.is_dma:
                    inst.then_inc(dsems[o.dsem], 16)
                elif o.inc:
                    inst.then_inc(sem[o.eng], 1)

        with nc.Block() as block:
            @block.tensor
            def _(eng):
                run("pe", eng)

            @block.scalar
            def _(eng):
                run("act", eng)

            @block.vector
            def _(eng):
                run("dve", eng)

            @block.gpsimd
            def _(eng):
                run("pool", eng)

            @block.sync
            def _(eng):
                run("sp", eng)

        self.tickbase = cnt
        self.ops = []
        self.pos = {e: 0 for e in ENGS}
        self.phase_dma = {}
        for b in self.touched:
            b.w = {}
            b.r = {}
            b.pr = set()
        self.touched = set()
        self.lastop = {}
        return {e: len(per[e]) for e in ENGS}


def bc_mid(t, row, off, n_mid, n_in):
    return bass.AP(t, off, [[row, 128], [0, n_mid], [1, n_in]])


def bc_in(t, row, off, n_mid, n_in):
    return bass.AP(t, off, [[row, 128], [1, n_mid], [0, n_in]])


def build(NBH, debug=False):
    TH = NBH * 256
    NT = TH // 128
    NG = TH // 512
    NBK = 2 * NBH
    TT = 2 * TH
    scale = 1.0 / np.sqrt(HD)

    nc = bass.Bass("TRN2", target_bir_lowering=False)
    din = lambda n, s, d=F32: nc.dram_tensor(n, list(s), d, kind="ExternalInput")
    x_pre = din("x_pre", [TH, D]); x_own = din("x_own", [TH, D]); p_own = din("p_own", [TH, PLE])
    w_in = din("w_in", [D, INC]); w_o = din("w_o", [D, D]); w_gate = din("w_gate", [D, DFF])
    w_up = din("w_up", [D, DFF]); w_down = din("w_down", [DFF, D]); w_pg = din("w_pg", [D, D])
    w_pp = din("w_pp", [PLE, D])
    g_mix = din("g_mix", [D]); g_ffn = din("g_ffn", [D]); g_ple = din("g_ple", [D])
    q_norm = din("q_norm", [HD]); k_norm = din("k_norm", [HD]); g_ret = din("g_ret", [H * HD])
    b_pg = din("b_pg", [D])
    rope_a_pre = din("rope_a_pre", [TH, 128]); rope_a_own = din("rope_a_own", [TH, 128])
    rope_r_pre = din("rope_r_pre", [TH, 128]); rope_r_own = din("rope_r_own", [TH, 128])
    c_kdec = din("c_kdec", [128, H]); c_qdec = din("c_qdec", [128, H]); c_gch = din("c_gch", [128, H])
    c_decT = din("c_decT", [128, H * 128])
    c_past = din("c_past", [128, NBH * NBK]); c_diag = din("c_diag", [128, NBH * NBK])
    c_mask = din("c_mask", [128, 4 * 512], BF16)
    c_onehot = din("c_onehot", [128, NBK * 128], BF16)
    c_identb = din("c_identb", [128, 128], BF16); c_identf = din("c_identf", [128, 128])
    out = nc.dram_tensor("out", [TH, D], F32, kind="ExternalOutput")

    dscr = lambda n, s, d=BF16: nc.dram_tensor(n, list(s), d)
    wb_in = dscr("wb_in", [D, INC]); wb_o = dscr("wb_o", [D, D]); wb_g = dscr("wb_g", [D, DFF])
    wb_u = dscr("wb_u", [D, DFF]); wb_d = dscr("wb_d", [DFF, D]); wb_pg = dscr("wb_pg", [D, D])
    wb_pp = dscr("wb_pp", [PLE, D])
    kT_s = dscr("kT_s", [H, 128, TT]); v_s = dscr("v_s", [TT, H * HD]); qT_s = dscr("qT_s", [H, 128, TH])
    rqT_s = dscr("rqT_s", [H, 128, TH]); rkT_s = dscr("rkT_s", [H, 128, TH])
    rkd_s = dscr("rkd_s", [TH, H * HD]); rv_s = dscr("rv_s", [TH, H * HD]); rgs_s = dscr("rgs_s", [TH, H * HD])
    mixT_s = dscr("mixT_s", [D, TH])
    dbg = {}
    if debug:
        for n, s in (("d_qT", [H, 128, TH]), ("d_kT", [H, 128, TT]), ("d_v", [TT, H * HD]), ("d_mixT", [D, TH]),
                     ("d_rqT", [H, 128, TH]), ("d_rkd", [TH, H * HD])):
            dbg[n] = nc.dram_tensor(n, s, BF16, kind="ExternalOutput")

    with contextlib.ExitStack() as gst:
        S = Sched(nc, gst)
        gsb = lambda n, s, d=F32: gst.enter_context(nc.sbuf_tensor(n, list(s), d))
        identb = gsb("identb", [128, 128], BF16); identf = gsb("identf", [128, 128])
        gmix_t = gsb("gmix_t", [128, KD]); gffn_t = gsb("gffn_t", [128, KD]); gple_t = gsb("gple_t", [128, KD])
        qn_t = gsb("qn_t", [128, HD]); kn_t = gsb("kn_t", [128, HD])
        kdec_t = gsb("kdec_t", [128, H]); qdec_t = gsb("qdec_t", [128, H]); gch_t = gsb("gch_t", [128, H])
        mhalf = gsb("mhalf", [128, 8])
        kmT = gsb("kmT", [128, H * NBK])
        Sst = gsb("Sst", [128, H * 128])
        B_const = Buf("const"); B_kmT = Buf("kmT"); B_S = Buf("S")
        B_w = {n: Buf(n) for n in ("in", "o", "g", "u", "d", "pg", "pp")}
        B_scr = {n: Buf(n) for n in ("kT", "v", "qT", "rqT", "rkT", "rkd", "rv", "rgs", "mixT")}

        def cast_w(src, dst, R, C, buf):
            for r0 in range(0, R, 1024):
                rr = min(1024, R - r0)
                for c0 in range(0, C, 2048):
                    cc = min(2048, C - c0)
                    S.op("pool", lambda e, r0=r0, rr=rr, c0=c0, cc=cc: e.dma_start(
                        out=dst[r0:r0 + rr, c0:c0 + cc], in_=src[r0:r0 + rr, c0:c0 + cc]),
                        pwrites=[buf], dma=True)

        cast_w(w_in, wb_in, D, INC, B_w["in"])
        ld = lambda o_, i_, **kw: S.op("sp", lambda e: e.dma_start(out=o_, in_=i_, **kw), pwrites=[B_const], dma=True)
        ld(identb[:, :], c_identb[:, :]); ld(identf[:, :], c_identf[:, :])
        for t_, g_ in ((gmix_t, g_mix), (gffn_t, g_ffn), (gple_t, g_ple)):
            ld(t_[:, :], g_.ap().rearrange("(k p) -> p k", p=128), allow_slow_non_contiguous=True)
        ld(qn_t[:, :], q_norm.ap().partition_broadcast(128)); ld(kn_t[:, :], k_norm.ap().partition_broadcast(128))
        ld(kdec_t[:, :], c_kdec[:, :]); ld(qdec_t[:, :], c_qdec[:, :]); ld(gch_t[:, :], c_gch[:, :])
        S.op("pool", lambda e: e.memset(mhalf[:, :], -0.5), pwrites=[B_const])
        S.op("pool", lambda e: e.memset(Sst[:, :], 0.0), writes=[B_S])
        S.op("pool", lambda e: e.memset(kmT[:, :], 0.0), writes=[B_kmT])
        st0 = S.emit()

        with contextlib.ExitStack() as st:
            sb = lambda n, s, d=F32: st.enter_context(nc.sbuf_tensor(n, list(s), d))
            psb = lambda n, s, d=F32: st.enter_context(nc.psum_tensor(n, list(s), d))
            NSL = 3
            wsl = [sb("wsl%d" % i, [128, KD, 512], BF16) for i in range(NSL)]; B_wsl = [Buf() for _ in range(NSL)]
            xt = [sb("xt%d" % i, [128, D]) for i in range(2)]; B_xt = [Buf() for _ in range(2)]
            xn = [sb("xn%d" % i, [128, D], BF16) for i in range(2)]; B_xn = [Buf() for _ in range(2)]
            st8 = [sb("st8_%d" % i, [128, 8]) for i in range(2)]; B_st8 = [Buf() for _ in range(2)]
            hT = [sb("hT%d" % i, [128, KD, 512], BF16) for i in range(2)]; B_hT = [Buf() for _ in range(2)]
            NTMP = 6
            tmp = [[sb("tmp%d_%d" % (i, j), [128, 512]) for j in range(3)] for i in range(NTMP)]
            B_tmp = [[Buf() for j in range(3)] for i in range(NTMP)]
            sm = [sb("sm%d" % i, [128, 8]) for i in range(NTMP)]; B_sm = [Buf() for _ in range(NTMP)]
            NOB = 8
            ob = [sb("ob%d" % i, [128, 512], BF16) for i in range(NOB)]; B_ob = [Buf() for _ in range(NOB)]
            NSTG = 3
            stage = [sb("stage%d" % i, [128, 4, 512], BF16) for i in range(NSTG)]; B_stage = [Buf() for _ in range(NSTG)]
            kdh = sb("kdh", [128, 4, H * HD], BF16); B_kdh = [Buf() for _ in range(4)]
            NVO = 4
            vob = [sb("vob%d" % i, [128, 512], BF16) for i in range(NVO)]; B_vob = [Buf() for _ in range(NVO)]
            NKO = 8
            kdo = [sb("kdo%d" % i, [128, 512], BF16) for i in range(NKO)]; B_kdo = [Buf() for _ in range(NKO)]
            ropa = [sb("ropa%d" % i, [128, 4, 128]) for i in range(2)]; B_ropa = [Buf() for _ in range(2)]
            ropr = [sb("ropr%d" % i, [128, 4, 128]) for i in range(2)]; B_ropr = [Buf() for _ in range(2)]
            NACC = 4
            pacc = [psb("pacc%d" % i, [128, 512]) for i in range(NACC)]; B_pacc = [Buf() for _ in range(NACC)]
            ptx = psb("ptx", [128, 8, 128], BF16); B_ptx = Buf()
            ptq = psb("ptq", [128, 8, 128], BF16); B_ptq = Buf()
            pU = [psb("pU%d" % i, [128, 512]) for i in range(2)]; B_pU = [Buf() for _ in range(2)]

            cast_w(w_o, wb_o, D, D, B_w["o"]); cast_w(w_pg, wb_pg, D, D, B_w["pg"]); cast_w(w_pp, wb_pp, PLE, D, B_w["pp"])

            ctr = {"tile": 0, "slab": 0, "pp": 0, "acc": 0, "vo": 0, "stg": 0, "ko": 0, "ob": 0}

            def rope(T_, BT_, tab, Btab, outb, Bout, tt):
                zc = T_[0]; Bz = BT_[0]
                z1 = bass.AP(zc, 0, [[512, 128], [128, 4], [1, 64]])
                z2 = bass.AP(zc, 64, [[512, 128], [128, 4], [1, 64]])
                cs = bc_mid(tab, 512, tt * 128, 4, 64); sn = bc_mid(tab, 512, tt * 128 + 64, 4, 64)
                v4 = lambda t_, off: bass.AP(t_, off, [[512, 128], [64, 4], [1, 64]])
                a, b_, c_, d_ = v4(T_[1], 0), v4(T_[1], 256), v4(T_[2], 0), v4(T_[2], 256)
                o1 = bass.AP(outb, 0, [[512, 128], [128, 4], [1, 64]])
                o2 = bass.AP(outb, 64, [[512, 128], [128, 4], [1, 64]])
                S.op("dve", lambda e: e.tensor_tensor(out=a, in0=z1, in1=cs, op=ALU.mult), reads=[Bz, Btab], writes=[BT_[1]])
                S.op("pool", lambda e: e.tensor_tensor(out=b_, in0=z2, in1=sn, op=ALU.mult), reads=[Bz, Btab], pwrites=[BT_[1]])
                S.op("pool", lambda e: e.tensor_tensor(out=c_, in0=z2, in1=cs, op=ALU.mult), reads=[Bz, Btab], writes=[BT_[2]])
                S.op("dve", lambda e: e.tensor_tensor(out=d_, in0=z1, in1=sn, op=ALU.mult), reads=[Bz, Btab], pwrites=[BT_[2]])
                S.op("dve", lambda e: e.tensor_tensor(out=o1, in0=a, in1=b_, op=ALU.subtract), reads=[BT_[1]], pwrites=[Bout])
                S.op("pool", lambda e: e.tensor_tensor(out=o2, in0=c_, in1=d_, op=ALU.add), reads=[BT_[2]], pwrites=[Bout])

            seq = [(g, False) for g in range(NG)] + [(g, True) for g in range(NG)]

            def prologue_steps(q):
                g, own = seq[q]
                hb = q % 2
                xsrc = x_own if own else x_pre
                ra_src = rope_a_own if own else rope_a_pre
                rr_src = rope_r_own if own else rope_r_pre
                steps = []
                for tt in range(4):
                    t = g * 4 + tt
                    ti = ctr["tile"] % 2; ctr["tile"] += 1

                    def sa(tt=tt, t=t, ti=ti):
                        if tt == 0:
                            S.op("sp", lambda e: e.dma_start(out=ropa[hb][:, :, :], in_=ra_src[g * 512:(g + 1) * 512, :].rearrange("(t p) c -> p t c", p=128)), writes=[B_ropa[hb]], dma=True)
                            S.op("sp", lambda e: e.dma_start(out=ropr[hb][:, :, :], in_=rr_src[g * 512:(g + 1) * 512, :].rearrange("(t p) c -> p t c", p=128)), writes=[B_ropr[hb]], dma=True)
                        S.op("sp", lambda e: e.dma_start(out=xt[ti][:, :], in_=xsrc[t * 128:(t + 1) * 128, :]), writes=[B_xt[ti]], dma=True)
                        S.op("act", lambda e: e.activation(out=xn[ti][:, :], in_=xt[ti][:, :], func=AF.Square, accum_out=st8[ti][:, 0:1]),
                             reads=[B_xt[ti]], writes=[B_xn[ti], B_st8[ti]])
                        S.op("pool", lambda e: e.tensor_scalar(out=st8[ti][:, 1:2], in0=st8[ti][:, 0:1], scalar1=1.0 / D, scalar2=EPS, op0=ALU.mult, op1=ALU.add),
                             reads=[B_st8[ti]], writes=[B_st8[ti]])
                        S.op("pool", lambda e: e.tensor_tensor(out=st8[ti][:, 2:3], in0=st8[ti][:, 1:2], in1=mhalf[:, 0:1], op=ALU.pow),
                             reads=[B_st8[ti], B_const], writes=[B_st8[ti]])
                        S.op("act", lambda e: e.activation(out=xn[ti][:, :], in_=xt[ti][:, :], func=AF.Copy, scale=st8[ti][:, 2:3]),
                             reads=[B_xt[ti], B_st8[ti]], writes=[B_xn[ti]])

                    def sb_(tt=tt, ti=ti):
                        for k4 in range(4):
                            for kk in range(4):
                                k = k4 * 4 + kk
                                S.op("pe", lambda e, k=k, kk=kk: e.transpose(out=ptx[:, kk, :], in_=xn[ti][:, k * 128:(k + 1) * 128], identity=identb[:, :]),
                                     reads=[B_xn[ti], B_const], pwrites=[B_ptx])
                            for kk in range(4):
                                k = k4 * 4 + kk
                                S.op("dve", lambda e, k=k, kk=kk: e.tensor_scalar(out=hT[hb][:, k, tt * 128:(tt + 1) * 128], in0=ptx[:, kk, :],
                                                                               scalar1=gmix_t[:, k:k + 1], scalar2=None, op0=ALU.mult),
                                     reads=[B_ptx, B_const], pwrites=[B_hT[hb]])
                    steps.append(sa); steps.append(sb_)
                return steps

            units = []
            extra = {}
            chunk_list = []

            def slab_load(j):
                q, c = chunk_list[j]
                si = j % NSL
                S.op("sp", lambda e: e.dma_start(out=wsl[si][:, :, :], in_=wb_in.ap()[:, c * 512:(c + 1) * 512].rearrange("(k p) n -> p k n", p=128)),
                     reads=[B_w["in"]], writes=[B_wsl[si]], dma=True)

            for q, (g, own) in enumerate(seq):
                for c in (list(range(14)) if own else [2, 3, 4, 5, 8, 9, 10, 11]):
                    chunk_list.append((q, c))
            first_unit_of_group = {}
            for j, (q, c) in enumerate(chunk_list):
                g, own = seq[q]
                hb = q % 2
                tokoff = TH if own else 0
                typ = c // 2
                hc = c % 2
                si = j % NSL
                if q not in first_unit_of_group:
                    first_unit_of_group[q] = len(units)
                extra.setdefault(len(units), []).append(lambda j=j: slab_load(j + 2) if j + 2 < len(chunk_list) else None)
                need_stage = typ in (0, 1, 3) or (typ == 4 and own)
                if need_stage:
                    sidx = ctr["stg"] % NSTG; ctr["stg"] += 1
                for tt in range(4):
                    t = g * 4 + tt
                    tok0 = tokoff + t * 128
                    ai = ctr["acc"] % NACC; ctr["acc"] += 1
                    qk = typ in (0, 1, 3, 4)
                    attn = typ in (0, 1)
                    if qk:
                        pi = ctr["pp"] % NTMP; ctr["pp"] += 1
                        T_, BT_ = tmp[pi], B_tmp[pi]
                        oi = ctr["ob"] % NOB; ctr["ob"] += 1
                    need_vo = (not qk) or (typ == 4 and own)
                    if not qk:
                        vi = ctr["vo"] % NVO; ctr["vo"] += 1
                    elif need_vo:
                        vi = ctr["ko"] % NKO; ctr["ko"] += 1
                    A = B = C = Dd = None

                    def A(tt=tt, si=si, ai=ai, hb=hb, typ=typ, qk=qk, T_=(T_ if qk else None), BT_=(BT_ if qk else None), vi=(vi if need_vo else None)):
                        for k in range(KD):
                            S.op("pe", lambda e, k=k: e.matmul(pacc[ai][:, :], lhsT=hT[hb][:, k, tt * 128:(tt + 1) * 128], rhs=wsl[si][:, k, :], start=(k == 0), stop=(k == KD - 1)),
                                 reads=[B_hT[hb], B_wsl[si]], pwrites=[B_pacc[ai]])
                        if qk:
                            S.op("act", lambda e: e.activation(out=T_[0][:, :], in_=pacc[ai][:, :], func=AF.Copy), reads=[B_pacc[ai]], writes=[BT_[0]])
                        elif typ == 6:
                            S.op("act", lambda e: e.activation(out=vob[vi][:, :], in_=pacc[ai][:, :], func=AF.Silu), reads=[B_pacc[ai]], writes=[B_vob[vi]])
                        else:
                            S.op("act", lambda e: e.activation(out=vob[vi][:, :], in_=pacc[ai][:, :], func=AF.Copy), reads=[B_pacc[ai]], writes=[B_vob[vi]])

                    if not qk:
                        def B(tt=tt, t=t, tok0=tok0, hc=hc, typ=typ, own=own, vi=vi):
                            if typ == 2:
                                S.op("sp", lambda e: e.dma_start(out=v_s[tok0:tok0 + 128, hc * 512:(hc + 1) * 512], in_=vob[vi][:, :]), reads=[B_vob[vi]], pwrites=[B_scr["v"]], dma=True)
                            elif typ == 6:
                                S.op("sp", lambda e: e.dma_start(out=rgs_s[t * 128:(t + 1) * 128, hc * 512:(hc + 1) * 512], in_=vob[vi][:, :]), reads=[B_vob[vi]], pwrites=[B_scr["rgs"]], dma=True)
                            elif own:
                                S.op("sp", lambda e: e.dma_start(out=rv_s[t * 128:(t + 1) * 128, hc * 512:(hc + 1) * 512], in_=vob[vi][:, :]), reads=[B_vob[vi]], pwrites=[B_scr["rv"]], dma=True)
                            else:
                                for hh in range(4):
                                    h = hc * 4 + hh
                                    S.op("pe", lambda e, h=h, hh=hh: e.matmul(pU[hc][:, hh * 128:(hh + 1) * 128], lhsT=kdh[:, tt, h * 128:(h + 1) * 128],
                                                                          rhs=vob[vi][:, hh * 128:(hh + 1) * 128], start=True, stop=True),
                                         reads=[B_kdh[tt], B_vob[vi]], pwrites=[B_pU[hc]])
                                sv = bass.AP(Sst, hc * 512, [[H * 128, 128], [128, 4], [1, 128]])
                                S.op("pool", lambda e: e.tensor_tensor(out=sv, in0=sv, in1=bc_in(gch_t, H, hc * 4, 4, 128), op=ALU.mult), reads=[B_S, B_const], writes=[B_S])
                                S.op("dve", lambda e: e.tensor_tensor(out=Sst[:, hc * 512:(hc + 1) * 512], in0=Sst[:, hc * 512:(hc + 1) * 512], in1=pU[hc][:, :], op=ALU.add),
                                     reads=[B_S, B_pU[hc]], writes=[B_S])
                        units.append([A, B, None, None])
                        continue

                    tab, Btab = (ropa[hb], B_ropa[hb]) if attn else (ropr[hb], B_ropr[hb])
                    if attn:
                        def B(pi=pi, T_=T_, BT_=BT_, typ=typ):
                            gn = qn_t if typ == 0 else kn_t
                            zc, sq = T_[0], T_[1]
                            S.op("pool", lambda e: e.tensor_tensor(out=sq[:, :], in0=zc[:, :], in1=zc[:, :], op=ALU.mult), reads=[BT_[0]], writes=[BT_[1]])
                            S.op("dve", lambda e: e.tensor_reduce(out=sm[pi][:, 0:4], in_=sq[:, :].rearrange("p (h d) -> p h d", h=4), axis=AX.X, op=ALU.add),
                                 reads=[BT_[1]], writes=[B_sm[pi]])
                            S.op("pool", lambda e: e.tensor_scalar(out=sm[pi][:, 0:4], in0=sm[pi][:, 0:4], scalar1=1.0 / HD, scalar2=EPS, op0=ALU.mult, op1=ALU.add),
                                 reads=[B_sm[pi]], writes=[B_sm[pi]])
                            S.op("pool", lambda e: e.tensor_tensor(out=sm[pi][:, 4:8], in0=sm[pi][:, 0:4], in1=mhalf[:, 0:4], op=ALU.pow), reads=[B_sm[pi], B_const], writes=[B_sm[pi]])
                            z3 = zc[:, :].rearrange("p (h d) -> p h d", h=4)
                            S.op("dve", lambda e: e.tensor_tensor(out=z3, in0=z3, in1=bc_in(sm[pi], 8, 4, 4, 128), op=ALU.mult), reads=[BT_[0], B_sm[pi]], writes=[BT_[0]])
                            S.op("pool", lambda e: e.tensor_tensor(out=z3, in0=z3, in1=bc_mid(gn, HD, 0, 4, 128), op=ALU.mult), reads=[BT_[0], B_const], writes=[BT_[0]])

                    def C(pi=oi, T_=T_, BT_=BT_, tab=tab, Btab=Btab, tt=tt, typ=typ, own=own, hc=hc, vi=(vi if need_vo else None)):
                        rope(T_, BT_, tab, Btab, ob[pi], B_ob[pi], tt)
                        if typ == 4:
                            o3 = ob[pi][:, :].rearrange("p (h d) -> p h d", h=4)
                            if own:
                                S.op("pool", lambda e: e.tensor_tensor(out=kdo[vi][:, :].rearrange("p (h d) -> p h d", h=4), in0=o3, in1=bc_in(kdec_t, H, hc * 4, 4, 128), op=ALU.mult),
                                     reads=[B_ob[pi], B_const], writes=[B_kdo[vi]])
                            else:
                                S.op("pool", lambda e: e.tensor_tensor(out=kdh[:, tt, hc * 512:(hc + 1) * 512].rearrange("p (h d) -> p h d", h=4), in0=o3,
                                                                     in1=bc_in(kdec_t, H, hc * 4, 4, 128), op=ALU.mult),
                                     reads=[B_ob[pi], B_const], pwrites=[B_kdh[tt]])

                    if need_stage:
                        def Dd(pi=oi, tt=tt, t=t, typ=typ, own=own, hc=hc, sidx=sidx, g=g, tokoff=tokoff, vi=(vi if need_vo else None)):
                            if typ == 4 and own:
                                S.op("sp", lambda e: e.dma_start(out=rkd_s[t * 128:(t + 1) * 128, hc * 512:(hc + 1) * 512], in_=kdo[vi][:, :]), reads=[B_kdo[vi]], pwrites=[B_scr["rkd"]], dma=True)
                            for hh in range(4):
                                S.op("pe", lambda e, hh=hh: e.transpose(out=ptq[:, hh, :], in_=ob[pi][:, hh * 128:(hh + 1) * 128], identity=identb[:, :]),
                                     reads=[B_ob[pi], B_const], pwrites=[B_ptq])
                            S.op("act", lambda e: e.activation(out=stage[sidx][:, :, tt * 128:(tt + 1) * 128], in_=ptq[:, 0:4, :], func=AF.Copy), reads=[B_ptq], pwrites=[B_stage[sidx]])
                            if tt == 3:
                                dst, bname, toff = {0: (qT_s, "qT", 0), 1: (kT_s, "kT", tokoff), 3: (rqT_s, "rqT", 0), 4: (rkT_s, "rkT", 0)}[typ]
                                c0 = toff + g * 512
                                S.op("sp", lambda e: e.dma_start(out=dst[hc * 4:(hc + 1) * 4, :, c0:c0 + 512].rearrange("h d t -> d h t"), in_=stage[sidx][:, :, :]),
                                     reads=[B_stage[sidx]], pwrites=[B_scr[bname]], dma=True)
                                if typ == 1:
                                    blk0 = (tokoff + g * 512) // 256
                                    kv = bass.AP(kmT, hc * 4 * NBK + blk0, [[H * NBK, 128], [NBK, 4], [1, 2]])
                                    S.op("dve", lambda e: e.tensor_reduce(out=kv, in_=stage[sidx][:, :, :].rearrange("p h (b t) -> p h b t", b=2), axis=AX.X, op=ALU.add),
                                         reads=[B_stage[sidx]], pwrites=[B_kmT])
                    units.append([A, (B if attn else None), C, (Dd if need_stage else None)])

            for q in range(len(seq)):
                u0 = first_unit_of_group[q]
                u1 = first_unit_of_group[q + 1] if q + 1 < len(seq) else len(units)
                if q + 1 < len(seq):
                    steps = prologue_steps(q + 1)
                    n = u1 - u0
                    for i_, stp in enumerate(steps):
                        extra.setdefault(u0 + 10 + (i_ * (n - 15)) // 8, []).append(stp)
            ctr["tile"] = 0
            pre0 = prologue_steps(0)
            slab_load(0)
            slab_load(1)
            for stp in pre0:
                stp()
            SKEW = (0, 1, 4, 9)
            NU = len(units)
            for i in range(NU + SKEW[3]):
                if i < NU:
                    for fn in extra.get(i, []):
                        fn()
                for s_ in range(4):
                    j = i - SKEW[s_]
                    if 0 <= j < NU and units[j][s_] is not None:
                        units[j][s_]()

            if debug:
                S.op("sp", lambda e: e.dma_start(out=dbg["d_qT"][:, :, :], in_=qT_s[:, :, :]), reads=[B_scr["qT"]], dma=True)
                S.op("sp", lambda e: e.dma_start(out=dbg["d_kT"][:, :, :], in_=kT_s[:, :, :]), reads=[B_scr["kT"]], dma=True)
                S.op("sp", lambda e: e.dma_start(out=dbg["d_v"][:, :], in_=v_s[:, :]), reads=[B_scr["v"]], dma=True)
                S.op("sp", lambda e: e.dma_start(out=dbg["d_rqT"][:, :, :], in_=rqT_s[:, :, :]), reads=[B_scr["rqT"]], dma=True)
                S.op("sp", lambda e: e.dma_start(out=dbg["d_rkd"][:, :], in_=rkd_s[:, :]), reads=[B_scr["rkd"]], dma=True)
            st1 = S.emit()
        print("phase0", st0, "phase1", st1, flush=True)
        import os as _os
        if _os.environ.get("STOP_AFTER") != "1":
            build_rest(nc, S, locals())
    return nc


def build_rest(nc, S, L):
    NBH, TH, NT, NG, NBK, TT, scale, debug, dbg = (L[k] for k in ("NBH", "TH", "NT", "NG", "NBK", "TT", "scale", "debug", "dbg"))
    identb, identf, gffn_t, gple_t, kdec_t, qdec_t, gch_t, mhalf, kmT, Sst = (L[k] for k in (
        "identb", "identf", "gffn_t", "gple_t", "kdec_t", "qdec_t", "gch_t", "mhalf", "kmT", "Sst"))
    B_const, B_kmT, B_S, B_w, B_scr, cast_w = (L[k] for k in ("B_const", "B_kmT", "B_S", "B_w", "B_scr", "cast_w"))
    scale = float(scale)

    with contextlib.ExitStack() as st:
        sb = lambda n, s, d=F32: st.enter_context(nc.sbuf_tensor(n, list(s), d))
        psb = lambda n, s, d=F32: st.enter_context(nc.psum_tensor(n, list(s), d))
        cast_jobs = []
        for src_, dst_, R_, C_, bn_ in ((L["w_gate"], L["wb_g"], D, DFF, "g"), (L["w_up"], L["wb_u"], D, DFF, "u"), (L["w_down"], L["wb_d"], DFF, D, "d")):
            for r0 in range(0, R_, 1024):
                rr = min(1024, R_ - r0)
                for c0 in range(0, C_, 2048):
                    cc = min(2048, C_ - c0)
                    cast_jobs.append(lambda src_=src_, dst_=dst_, r0=r0, rr=rr, c0=c0, cc=cc, bn_=bn_: S.op(
                        "pool", lambda e: e.dma_start(out=dst_[r0:r0 + rr, c0:c0 + cc], in_=src_[r0:r0 + rr, c0:c0 + cc]), pwrites=[B_w[bn_]], dma=True))
        kTh = [sb("kTh%d" % i, [128, TT], BF16) for i in range(2)]; B_kTh = [Buf() for _ in range(2)]
        V1 = [sb("V1_%d" % i, [128, 2 * NT, 129], BF16) for i in range(2)]; B_V1 = [Buf() for _ in range(2)]
        qTh = [sb("qTh%d" % i, [128, TH], BF16) for i in range(2)]; B_qTh = [Buf() for _ in range(2)]
        aTst = [sb("aTst%d" % i, [128, TH], BF16) for i in range(2)]; B_aTst = [Buf() for _ in range(2)]
        kmb = sb("kmb", [128, H * NBK], BF16); B_kmb = Buf()
        pastt = sb("pastt", [128, NBH * NBK]); diagt = sb("diagt", [128, NBH * NBK])
        maskc = sb("maskc", [128, 4 * 512], BF16); oneh = sb("oneh", [128, NBK * 128], BF16)
        B_c2 = Buf()
        NPT = 5
        PT2 = [sb("PT2_%d" % i, [128, 1024], BF16) for i in range(NPT)]; B_PT = [Buf() for _ in range(NPT)]
        gs = sb("gs", [128, 4 * NBK]); B_gs = Buf()
        m8 = sb("m8", [128, 4 * 8]); B_m8 = Buf()
        selb = sb("selb", [128, 4 * NBK]); B_selb = Buf()
        NP = NBH // 2
        selbTa = [sb("selbTa%d" % i, [128, NP * 512], BF16) for i in range(2)]; B_selbTa = [[Buf() for _ in range(NP)] for _ in range(2)]
        PaD = [sb("PaD%d" % i, [128, 1024]) for i in range(2)]; B_PaD = [Buf() for _ in range(2)]
        PaP = [sb("PaP%d" % i, [128, 1024]) for i in range(2)]; B_PaP = [Buf() for _ in range(2)]
        psm = [sb("psm%d" % i, [128, 512]) for i in range(3)]; B_psm = [Buf() for _ in range(3)]
        rcr = sb("rcr", [1, 512]); B_rcr = Buf()
        bcs = sb("bcs", [128, 512]); B_bcs = Buf()
        onesc = sb("onesc", [128, 1], BF16); onesr = sb("onesr", [1, 128], BF16)
        phl = [sb("phl%d" % i, [128, 512], BF16) for i in range(2)]; B_phl = Buf()
        rhl = [sb("rhl%d" % i, [1, 512], BF16) for i in range(2)]; B_rhl = Buf()
        pSTw = [psb("pSTw%d" % i, [128, 1024]) for i in range(2)]; B_pST = [Buf() for _ in range(2)]
        pacc_o = [psb("pacco%d" % i, [128, 512]) for i in range(2)]; B_pacc_o = [Buf() for _ in range(2)]
        pm1 = psb("pm1", [128, 512]); B_pm1 = Buf()
        pm2 = psb("pm2", [128, 512]); B_pm2 = Buf()
        pm2b = pm2[:, :].bitcast(BF16)

        ldc = lambda o_, i_: S.op("sp", lambda e: e.dma_start(out=o_, in_=i_), pwrites=[B_c2], dma=True)
        ldc(pastt[:, :], L["c_past"][:, :]); ldc(diagt[:, :], L["c_diag"][:, :]); ldc(maskc[:, :], L["c_mask"][:, :]); ldc(oneh[:, :], L["c_onehot"][:, :])
        S.op("dve", lambda e: e.tensor_copy(out=kmb[:, :], in_=kmT[:, :]), reads=[B_kmT], writes=[B_kmb])
        for i in range(2):
            for p_ in range(NP):
                S.op("pool", lambda e, i=i, p_=p_: e.memset(selbTa[i][:, p_ * 512:(p_ + 1) * 512], 0.0), writes=[B_selbTa[i][p_]])
        S.op("pool", lambda e: e.memset(onesc[:, :], 1.0), pwrites=[B_c2])
        S.op("pool", lambda e: e.memset(onesr[:, :], 1.0), pwrites=[B_c2])

        def head_loads(h):
            hb = h % 2
            S.op("sp", lambda e: e.dma_start(out=qTh[hb][:, :], in_=L["qT_s"][h, :, :]), reads=[B_scr["qT"]], writes=[B_qTh[hb]], dma=True)
            S.op("sp", lambda e: e.dma_start(out=kTh[hb][:, :], in_=L["kT_s"][h, :, :]), reads=[B_scr["kT"]], writes=[B_kTh[hb]], dma=True)
            nvs = max(1, (2 * NT) // 16)
            for vs in range(nvs):
                t0_, t1_ = vs * (2 * NT // nvs), (vs + 1) * (2 * NT // nvs)
                S.op("sp", lambda e, t0_=t0_, t1_=t1_: e.dma_start(out=V1[hb][:, t0_:t1_, 0:128],
                                                                 in_=L["v_s"][t0_ * 128:t1_ * 128, h * 128:(h + 1) * 128].rearrange("(t p) d -> p t d", p=128)),
                     reads=[B_scr["v"]], writes=[B_V1[hb]] if vs == 0 else [], pwrites=[] if vs == 0 else [B_V1[hb]], dma=True)

        def sel_steps(h):
            hb = h % 2
            steps = []
            row = NBH * NBK
            v4 = lambda t_: bass.AP(t_, 0, [[4 * NBK, 128], [2 * NBK, 2], [NBK, 2], [1, NBK]])
            g3 = lambda t_: bass.AP(t_, 0, [[4 * NBK, 128], [NBK, 4], [1, NBK]])
            pm1v = bass.AP(pm1, 0, [[512, 128], [2 * NBK, 2], [NBK, 2], [1, NBK]])
            thr = bass.AP(m8, 2, [[32, 128], [8, 4], [0, NBK]])
            for p in range(NP):
                pastv = bass.AP(pastt, 2 * p * NBK, [[row, 128], [NBK, 2], [0, 2], [1, NBK]])
                diagv = bass.AP(diagt, 2 * p * NBK, [[row, 128], [NBK, 2], [0, 2], [1, NBK]])

                def s1(p=p):
                    for qt in range(4):
                        tq0 = p * 512 + qt * 128
                        S.op("pe", lambda e, qt=qt, tq0=tq0: e.matmul(pm1[:, qt * NBK:(qt + 1) * NBK], lhsT=qTh[hb][:, tq0:tq0 + 128], rhs=kmb[:, h * NBK:(h + 1) * NBK],
                                                                  start=True, stop=True), reads=[B_qTh[hb], B_kmb], pwrites=[B_pm1])

                def s2(pastv=pastv):
                    S.op("dve", lambda e: e.tensor_tensor(out=v4(gs), in0=pm1v, in1=pastv, op=ALU.add), reads=[B_pm1, B_c2], writes=[B_gs])
                    for qt in range(4):
                        S.op("dve", lambda e, qt=qt: e.max(out=m8[:, qt * 8:(qt + 1) * 8], in_=gs[:, qt * NBK:(qt + 1) * NBK]), reads=[B_gs], pwrites=[B_m8])

                def s3(pastv=pastv, diagv=diagv):
                    S.op("dve", lambda e: e.tensor_tensor(out=g3(selb), in0=g3(gs), in1=thr, op=ALU.is_ge), reads=[B_gs, B_m8], writes=[B_selb])
                    S.op("dve", lambda e: e.tensor_scalar(out=selb[:, :], in0=selb[:, :], scalar1=BIG, scalar2=-BIG, op0=ALU.mult, op1=ALU.add), reads=[B_selb], writes=[B_selb])
                    S.op("dve", lambda e: e.tensor_tensor(out=v4(selb), in0=v4(selb), in1=pastv, op=ALU.min), reads=[B_selb, B_c2], writes=[B_selb])
                    S.op("dve", lambda e: e.tensor_tensor(out=v4(selb), in0=v4(selb), in1=diagv, op=ALU.max), reads=[B_selb, B_c2], writes=[B_selb])

                def s4():
                    for qt in range(4):
                        S.op("pe", lambda e, qt=qt: e.transpose(out=pm2[0:NBK, qt * 128:(qt + 1) * 128], in_=selb[:, qt * NBK:(qt + 1) * NBK], identity=identf[:, :]),
                             reads=[B_selb, B_const], pwrites=[B_pm2])

                def s5(p=p):
                    S.op("dve", lambda e: e.tensor_copy(out=selbTa[hb][0:NBK, p * 512:(p + 1) * 512], in_=pm2[0:NBK, :]), reads=[B_pm2], writes=[B_selbTa[hb][p]])

                def s45(s4=s4, s5=s5):
                    s4(); s5()
                steps += [s1, s2, s3, s45]
            return steps

        U = []
        import os as _os
        HLIM = int(_os.environ.get("A_HEADS", H))
        for h in range(HLIM):
            hb = h % 2
            for p in range(NP):
                keys = [(kt, kt // 2, None) for kt in range(NT)]
                for ko in range(4 * p + 4):
                    keys.append((NT + ko, NBH + ko // 2, (ko - 4 * p) if ko >= 4 * p else None))
                nk2 = len(keys) // 2
                for u in range(nk2):
                    U.append(dict(h=h, hb=hb, p=p, keys=keys[2 * u:2 * u + 2], first=(u == 0), last=(u == nk2 - 1), idx=len(U), uinpair=u, pairidx=h * NP + p))

        def do_ST(un):
            h, hb, p = un["h"], un["hb"], un["p"]
            si = un["idx"] % 2; pj = un["idx"] % NPT
            for half, (ktile, n, pidx) in enumerate(un["keys"]):
                osl = slice(half * 512, (half + 1) * 512)
                S.op("pe", lambda e, ktile=ktile, osl=osl: e.matmul(pSTw[si][:, osl], lhsT=kTh[hb][:, ktile * 128:(ktile + 1) * 128], rhs=qTh[hb][:, p * 512:(p + 1) * 512],
                                                                start=True, stop=False), reads=[B_kTh[hb], B_qTh[hb]], pwrites=[B_pST[si]])
                S.op("pe", lambda e, n=n, osl=osl: e.matmul(pSTw[si][:, osl], lhsT=oneh[:, n * 128:(n + 1) * 128], rhs=selbTa[hb][:, p * 512:(p + 1) * 512], start=False, stop=True),
                     reads=[B_c2, B_selbTa[hb][p]], pwrites=[B_pST[si]])
            S.op("act", lambda e: e.activation(out=PT2[pj][:, :], in_=pSTw[si][:, :], func=AF.Exp, scale=scale), reads=[B_pST[si]], writes=[B_PT[pj]])
            for half, (ktile, n, pidx) in enumerate(un["keys"]):
                if pidx is not None:
                    osl = slice(half * 512, (half + 1) * 512)
                    S.op("pool", lambda e, osl=osl, pidx=pidx: e.tensor_tensor(out=PT2[pj][:, osl], in0=PT2[pj][:, osl], in1=maskc[:, pidx * 512:(pidx + 1) * 512], op=ALU.mult),
                         reads=[B_PT[pj], B_c2], writes=[B_PT[pj]])

        deferred = {}

        def do_PV(un):
            h, hb, p = un["h"], un["hb"], un["p"]
            pj = un["idx"] % NPT
            pb = un["pairidx"] % 2
            for half, (ktile, n, pidx) in enumerate(un["keys"]):
                S.op("pe", lambda e, half=half, ktile=ktile: e.matmul(pacc_o[pb][:, :], lhsT=V1[hb][:, ktile, 0:128], rhs=PT2[pj][:, half * 512:(half + 1) * 512],
                                                                start=(un["first"] and half == 0), stop=(un["last"] and half == 1)),
                     reads=[B_PT[pj], B_V1[hb]], pwrites=[B_pacc_o[pb]])
            u = un["uinpair"]
            eng, Pa, BPa = ("dve", PaD[pb], B_PaD[pb]) if u % 2 == 0 else ("pool", PaP[pb], B_PaP[pb])
            if u < 2:
                S.op(eng, lambda e: e.tensor_copy(out=Pa[:, :], in_=PT2[pj][:, :]), reads=[B_PT[pj]], writes=[BPa])
            else:
                S.op(eng, lambda e: e.tensor_tensor(out=Pa[:, :], in0=Pa[:, :], in1=PT2[pj][:, :], op=ALU.add), reads=[B_PT[pj], BPa], writes=[BPa])
            if un["last"]:
                def ep1(pb=pb):
                    S.op("dve", lambda e: e.tensor_tensor(out=psm[0][:, :], in0=PaD[pb][:, 0:512], in1=PaD[pb][:, 512:1024], op=ALU.add), reads=[B_PaD[pb]], writes=[B_psm[0]])
                    S.op("pool", lambda e: e.tensor_tensor(out=psm[1][:, :], in0=PaP[pb][:, 0:512], in1=PaP[pb][:, 512:1024], op=ALU.add), reads=[B_PaP[pb]], writes=[B_psm[1]])
                    S.op("dve", lambda e: e.tensor_tensor(out=psm[2][:, :], in0=psm[0][:, :], in1=psm[1][:, :], op=ALU.add), reads=[B_psm[0], B_psm[1]], writes=[B_psm[2]])
                    S.op("dve", lambda e: e.tensor_copy(out=phl[0][:, :], in_=psm[2][:, :]), reads=[B_psm[2]], writes=[B_phl])
                    S.op("dve", lambda e: e.tensor_tensor(out=phl[1][:, :], in0=psm[2][:, :], in1=phl[0][:, :], op=ALU.subtract), reads=[B_psm[2], B_phl], pwrites=[B_phl])

                def ep2(pb=pb, hb=hb, p=p, h=h):
                    S.op("pe", lambda e: e.matmul(pm2[0:1, 0:512], lhsT=onesc[:, 0:1], rhs=phl[0][:, :], start=True, stop=False), reads=[B_phl, B_c2], pwrites=[B_pm2])
                    S.op("pe", lambda e: e.matmul(pm2[0:1, 0:512], lhsT=onesc[:, 0:1], rhs=phl[1][:, :], start=False, stop=True), reads=[B_phl, B_c2], pwrites=[B_pm2])
                    S.op("dve", lambda e: e.reciprocal(out=rcr[:, :], in_=pm2[0:1, 0:512]), reads=[B_pm2], writes=[B_rcr])
                    S.op("dve", lambda e: e.tensor_copy(out=rhl[0][:, :], in_=rcr[:, :]), reads=[B_rcr], writes=[B_rhl])
                    S.op("dve", lambda e: e.tensor_tensor(out=rhl[1][:, :], in0=rcr[:, :], in1=rhl[0][:, :], op=ALU.subtract), reads=[B_rcr, B_rhl], pwrites=[B_rhl])
                    S.op("pe", lambda e: e.matmul(pm2[:, 0:512], lhsT=onesr[0:1, :], rhs=rhl[0][0:1, :], start=True, stop=False), reads=[B_rhl, B_c2], pwrites=[B_pm2])
                    S.op("pe", lambda e: e.matmul(pm2[:, 0:512], lhsT=onesr[0:1, :], rhs=rhl[1][0:1, :], start=False, stop=True), reads=[B_rhl, B_c2], pwrites=[B_pm2])
                    S.op("act", lambda e: e.activation(out=bcs[:, :], in_=pm2[:, 0:512], func=AF.Copy), reads=[B_pm2], writes=[B_bcs])
                    S.op("dve", lambda e: e.tensor_tensor(out=aTst[hb][:, p * 512:(p + 1) * 512], in0=pacc_o[pb][:, :], in1=bcs[:, :], op=ALU.mult),
                         reads=[B_pacc_o[pb], B_bcs], pwrites=[B_aTst[hb]])
                    if p == NP - 1:
                        S.op("sp", lambda e: e.dma_start(out=L["mixT_s"][h * 128:(h + 1) * 128, :], in_=aTst[hb][:, :]), reads=[B_aTst[hb]], pwrites=[B_scr["mixT"]], dma=True)
                deferred.setdefault(un["idx"] + 2, []).append(ep1)
                deferred.setdefault(un["idx"] + 4, []).append(ep2)

        head_loads(0)
        for stp in sel_steps(0):
            stp()
        pending = []
        per_head = len(U) // HLIM
        cast_stride = max(1, (len(U) * 3 // 4) // max(1, len(cast_jobs)))
        LOOK = int(_os.environ.get("LOOK", "2"))
        pos_in_head = 0
        stride = 1

        def after_pv(j):
            nonlocal pending, pos_in_head, stride
            for fn in deferred.pop(j, []):
                fn()

        def enter_head(hcur):
            nonlocal pending, pos_in_head, stride
            if hcur + 1 < HLIM:
                head_loads(hcur + 1)
                pending = sel_steps(hcur + 1)
                stride = max(1, (per_head - 8) // max(1, len(pending)))
                pos_in_head = 0

        for i, un in enumerate(U):
            if cast_jobs and i % cast_stride == 0:
                cast_jobs.pop(0)()
            do_ST(un)
            j = i - LOOK
            if j >= 0:
                do_PV(U[j])
                after_pv(j)
            if i == 0:
                enter_head(0)
            elif j >= 0 and U[j]["last"] and U[j]["p"] == NP - 1:
                enter_head(U[j]["h"] + 1)
            if pending:
                pos_in_head += 1
                nxt_new_head = (i + 1 < len(U)) and (U[i + 1]["h"] != un["h"])
                flush = nxt_new_head or (i + 1 == len(U))
                if pos_in_head % stride == 0 or flush:
                    nstep = len(pending) if flush else 1
                    for _ in range(nstep):
                        pending.pop(0)()
        for j in range(max(0, len(U) - LOOK), len(U)):
            do_PV(U[j])
            after_pv(j)
        for k_ in sorted(deferred):
            for fn in deferred[k_]:
                fn()
        while cast_jobs:
            cast_jobs.pop(0)()
        st2a = S.emit()
    print("phase2a", st2a, flush=True)
    import os as _os
    if _os.environ.get("STOP_AFTER") == "2":
        return

    with contextlib.ExitStack() as st:
        sb = lambda n, s, d=F32: st.enter_context(nc.sbuf_tensor(n, list(s), d))
        psb = lambda n, s, d=F32: st.enter_context(nc.psum_tensor(n, list(s), d))
        rq4 = [sb("rq4_%d" % i, [128, H, 512], BF16) for i in range(2)]; B_rq4 = [Buf() for _ in range(2)]
        rk4 = [sb("rk4_%d" % i, [128, H, 512], BF16) for i in range(2)]; B_rk4 = [Buf() for _ in range(2)]
        NKB = 3
        kd = [sb("kd%d" % i, [128, H * HD], BF16) for i in range(NKB)]; B_kd = [Buf() for _ in range(NKB)]
        rvt = [sb("rvt%d" % i, [128, H * HD], BF16) for i in range(NKB)]; B_rvt = [Buf() for _ in range(NKB)]
        NRG = 4
        rgt = [sb("rgt%d" % i, [128, H * HD], BF16) for i in range(NRG)]; B_rgt = [Buf() for _ in range(NRG)]
        decT = sb("decT", [128, H * 128]); grt = sb("grt", [128, H * HD]); B_c3 = Buf()
        Sb = sb("Sb", [128, H * HD], BF16); B_Sb = Buf()
        sTm = [sb("sTm%d" % i, [128, H, 128], BF16) for i in range(2)]; B_sTm = [Buf() for _ in range(2)]
        NOS = 3
        osb_ = [sb("osb%d" % i, [128, H * HD]) for i in range(NOS)]; B_osb_ = [Buf() for _ in range(NOS)]
        sq_ = [sb("sq2_%d" % i, [128, H * HD]) for i in range(2)]; B_sq_ = [Buf() for _ in range(2)]
        ssr_ = [sb("ssr%d" % i, [128, 16]) for i in range(2)]; B_ssr_ = [Buf() for _ in range(2)]
        rob = [sb("rob%d" % i, [128, H * HD], BF16) for i in range(NOS)]; B_rob = [Buf() for _ in range(NOS)]
        rTst = [sb("rTst%d" % i, [128, H, 512], BF16) for i in range(2)]; B_rTst = [Buf() for _ in range(2)]
        psT = [psb("psT%d" % i, [128, 512]) for i in range(2)]; B_psT = [Buf() for _ in range(2)]
        pO = [psb("pO%d" % i, [128, 512]) for i in range(2)]; B_pO = [Buf() for _ in range(2)]
        pU = [psb("pU2_%d" % i, [128, 512]) for i in range(2)]; B_pU = [Buf() for _ in range(2)]
        ptr = psb("ptr2", [128, H, 128], BF16); B_ptr = Buf()
        S.op("sp", lambda e: e.dma_start(out=decT[:, :], in_=L["c_decT"][:, :]), pwrites=[B_c3], dma=True)
        S.op("sp", lambda e: e.dma_start(out=grt[:, :], in_=L["g_ret"].ap().partition_broadcast(128)), pwrites=[B_c3], dma=True)

        def LD(t):
            g, tt = t // 4, t % 4
            gb = g % 2
            if tt == 0:
                S.op("sp", lambda e: e.dma_start(out=rq4[gb][:, :, :], in_=L["rqT_s"][:, :, g * 512:(g + 1) * 512].rearrange("h d t -> d h t")),
                     reads=[B_scr["rqT"]], writes=[B_rq4[gb]], dma=True)
                S.op("sp", lambda e: e.dma_start(out=rk4[gb][:, :, :], in_=L["rkT_s"][:, :, g * 512:(g + 1) * 512].rearrange("h d t -> d h t")),
                     reads=[B_scr["rkT"]], writes=[B_rk4[gb]], dma=True)
            for dst, Bd, src, nm, nb in ((kd, B_kd, "rkd_s", "rkd", NKB), (rvt, B_rvt, "rv_s", "rv", NKB), (rgt, B_rgt, "rgs_s", "rgs", NRG)):
                S.op("sp", lambda e, dst=dst, src=src, nb=nb: e.dma_start(out=dst[t % nb][:, :], in_=L[src][t * 128:(t + 1) * 128, :]),
                     reads=[B_scr[nm]], writes=[Bd[t % nb]], dma=True)

        def M1(t):
            g, tt = t // 4, t % 4
            gb = g % 2; tb = t % 2
            ts_ = slice(tt * 128, (tt + 1) * 128)
            for hh in range(H):
                bk, col = hh // 4, (hh % 4) * 128
                S.op("pe", lambda e, hh=hh, bk=bk, col=col: e.matmul(psT[bk][:, col:col + 128], lhsT=rk4[gb][:, hh, ts_], rhs=rq4[gb][:, hh, ts_], start=True, stop=True),
                     reads=[B_rk4[gb], B_rq4[gb]], pwrites=[B_psT[bk]])
            for bk in range(2):
                S.op("dve", lambda e, bk=bk: e.tensor_tensor(out=sTm[tb][:, bk * 4:(bk + 1) * 4, :], in0=psT[bk][:, :].rearrange("p (h t) -> p h t", h=4),
                                                        in1=decT[:, bk * 512:(bk + 1) * 512].rearrange("p (h t) -> p h t", h=4), op=ALU.mult),
                     reads=[B_psT[bk], B_c3], pwrites=[B_sTm[tb]])

        def M2(t):
            g, tt = t // 4, t % 4
            gb = g % 2; tb = t % 2; kb = t % NKB; ob_ = t % NOS
            ts_ = slice(tt * 128, (tt + 1) * 128)
            S.op("pool", lambda e: e.tensor_copy(out=Sb[:, :], in_=Sst[:, :]), reads=[B_S], writes=[B_Sb])
            for hh in range(H):
                bk, col = hh // 4, (hh % 4) * 128
                S.op("pe", lambda e, hh=hh, bk=bk, col=col: e.matmul(pO[bk][:, col:col + 128], lhsT=sTm[tb][:, hh, :], rhs=rvt[kb][:, hh * 128:(hh + 1) * 128], start=True, stop=False),
                     reads=[B_sTm[tb], B_rvt[kb]], pwrites=[B_pO[bk]])
                S.op("pe", lambda e, hh=hh, bk=bk, col=col: e.matmul(pO[bk][:, col:col + 128], lhsT=rq4[gb][:, hh, ts_], rhs=Sb[:, hh * 128:(hh + 1) * 128], start=False, stop=True),
                     reads=[B_rq4[gb], B_Sb], pwrites=[B_pO[bk]])
            for bk in range(2):
                S.op("dve", lambda e, bk=bk: e.tensor_tensor(out=osb_[ob_][:, bk * 512:(bk + 1) * 512].rearrange("p (h e) -> p h e", h=4), in0=pO[bk][:, :].rearrange("p (h e) -> p h e", h=4),
                                                        in1=bc_in(qdec_t, H, bk * 4, 4, 128), op=ALU.mult),
                     reads=[B_pO[bk], B_const], pwrites=[B_osb_[ob_]])

        def UU(t):
            kb = t % NKB
            for hh in range(H):
                bk, col = hh // 4, (hh % 4) * 128
                S.op("pe", lambda e, hh=hh, bk=bk, col=col: e.matmul(pU[bk][:, col:col + 128], lhsT=kd[kb][:, hh * 128:(hh + 1) * 128], rhs=rvt[kb][:, hh * 128:(hh + 1) * 128], start=True, stop=True),
                     reads=[B_kd[kb], B_rvt[kb]], pwrites=[B_pU[bk]])
            s3 = Sst[:, :].rearrange("p (h e) -> p h e", h=H)
            S.op("pool", lambda e: e.tensor_tensor(out=s3, in0=s3, in1=bc_in(gch_t, H, 0, 8, 128), op=ALU.mult), reads=[B_S, B_const], writes=[B_S])
            for bk in range(2):
                S.op("dve", lambda e, bk=bk: e.tensor_tensor(out=Sst[:, bk * 512:(bk + 1) * 512], in0=Sst[:, bk * 512:(bk + 1) * 512], in1=pU[bk][:, :], op=ALU.add),
                     reads=[B_S, B_pU[bk]], writes=[B_S])

        def NN(t):
            ob_ = t % NOS; sb_i = t % 2; rg_i = t % NRG
            osb, sq, ssr = osb_[ob_], sq_[sb_i], ssr_[sb_i]
            B_osb, B_sq, B_ssr = B_osb_[ob_], B_sq_[sb_i], B_ssr_[sb_i]
            S.op("pool", lambda e: e.tensor_tensor(out=sq[:, :], in0=osb[:, :], in1=osb[:, :], op=ALU.mult), reads=[B_osb], writes=[B_sq])
            S.op("dve", lambda e: e.tensor_reduce(out=ssr[:, 0:8], in_=sq[:, :].rearrange("p (h e) -> p h e", h=H), axis=AX.X, op=ALU.add), reads=[B_sq], writes=[B_ssr])
            S.op("pool", lambda e: e.tensor_scalar(out=ssr[:, 0:8], in0=ssr[:, 0:8], scalar1=1.0 / HD, scalar2=EPS, op0=ALU.mult, op1=ALU.add), reads=[B_ssr], writes=[B_ssr])
            S.op("pool", lambda e: e.tensor_tensor(out=ssr[:, 8:16], in0=ssr[:, 0:8], in1=mhalf[:, 0:8], op=ALU.pow), reads=[B_ssr, B_const], writes=[B_ssr])
            o3 = osb[:, :].rearrange("p (h e) -> p h e", h=H)
            S.op("dve", lambda e: e.tensor_tensor(out=o3, in0=o3, in1=bc_in(ssr, 16, 8, 8, 128), op=ALU.mult), reads=[B_osb, B_ssr], writes=[B_osb])
            S.op("pool", lambda e: e.tensor_tensor(out=osb[:, :], in0=osb[:, :], in1=grt[:, :], op=ALU.mult), reads=[B_osb, B_c3], writes=[B_osb])
            S.op("dve", lambda e: e.tensor_tensor(out=rob[ob_][:, :], in0=osb[:, :], in1=rgt[rg_i][:, :], op=ALU.mult), reads=[B_osb, B_rgt[rg_i]], writes=[B_rob[ob_]])

        def TT(t):
            g, tt = t // 4, t % 4
            gb = g % 2; ob_ = t % NOS
            ts_ = slice(tt * 128, (tt + 1) * 128)
            for hh in range(H):
                S.op("pe", lambda e, hh=hh: e.transpose(out=ptr[:, hh, :], in_=rob[ob_][:, hh * 128:(hh + 1) * 128], identity=identb[:, :]),
                     reads=[B_rob[ob_], B_const], pwrites=[B_ptr])
            S.op("act", lambda e: e.activation(out=rTst[gb][:, :, ts_], in_=ptr[:, :, :], func=AF.Copy), reads=[B_ptr], pwrites=[B_rTst[gb]])
            if tt == 3:
                S.op("sp", lambda e: e.dma_start(out=L["mixT_s"][H * HD:2 * H * HD, g * 512:(g + 1) * 512].rearrange("(h e) t -> e h t", h=H), in_=rTst[gb][:, :, :]),
                     reads=[B_rTst[gb]], pwrites=[B_scr["mixT"]], dma=True)

        NTL = NT
        LD(0)
        M1(0)
        for i in range(NTL + 2):
            if i + 1 < NTL:
                LD(i + 1)
                M1(i + 1)
            if i < NTL:
                M2(i)
                UU(i)
            if 0 <= i - 1 < NTL:
                NN(i - 1)
            if 0 <= i - 2 < NTL:
                TT(i - 2)
        if debug:
            S.op("sp", lambda e: e.dma_start(out=dbg["d_mixT"][:, :], in_=L["mixT_s"][:, :]), reads=[B_scr["mixT"]], dma=True)
        st2b = S.emit()
    print("phase2b", st2b, flush=True)

    with contextlib.ExitStack() as st:
        sb = lambda n, s, d=F32: st.enter_context(nc.sbuf_tensor(n, list(s), d))
        psb = lambda n, s, d=F32: st.enter_context(nc.psum_tensor(n, list(s), d))
        xres = sb("xres", [128, 4, D]); B_xres = [Buf() for _ in range(4)]
        mT = sb("mT", [128, KD, 512], BF16); B_mT = Buf()
        h2T = sb("h2T", [128, KD, 512], BF16); B_h2T = Buf()
        HK = KF // 2
        actT = sb("actT", [128, HK, 512], BF16); B_actT = Buf()
        ws = [sb("ws%d" % i, [128, KD, 256], BF16) for i in range(4)]; B_ws = [Buf() for _ in range(4)]
        wd = [sb("wd%d" % i, [128, 11, 512], BF16) for i in range(2)]; B_wd = [Buf() for _ in range(2)]
        wpp = sb("wpp", [128, 2, D], BF16); bpg_t = sb("bpg_t", [128, D]); B_c4 = Buf()
        pT = sb("pT", [128, 2, 512], BF16); B_pT = Buf()
        pin = [sb("pin%d" % i, [128, PLE]) for i in range(2)]; B_pin = [Buf() for _ in range(2)]
        pinb = [sb("pinb%d" % i, [128, PLE], BF16) for i in range(2)]; B_pinb = [Buf() for _ in range(2)]
        stmp = [sb("stmp%d" % i, [128, 512]) for i in range(2)]; B_stmp = [Buf() for _ in range(2)]
        gtmp = [sb("gtmp%d" % i, [128, 256]) for i in range(2)]; B_gtmp = [Buf() for _ in range(2)]
        xn = sb("xn3", [128, D], BF16); B_xn = Buf()
        st8 = [sb("st8b_%d" % i, [128, 8]) for i in range(2)]; B_st8 = [Buf() for _ in range(2)]
        pb = [psb("pb%d" % i, [128, 512]) for i in range(8)]; B_pb = [Buf() for _ in range(8)]
        pbb = [pb[i][:, :].bitcast(BF16) for i in range(8)]
        S.op("sp", lambda e: e.dma_start(out=wpp[:, :, :], in_=L["wb_pp"].ap().rearrange("(k p) n -> p k n", p=128)), reads=[B_w["pp"]], pwrites=[B_c4], dma=True)
        S.op("sp", lambda e: e.dma_start(out=bpg_t[:, :], in_=L["b_pg"].ap().partition_broadcast(128)), pwrites=[B_c4], dma=True)
        c3 = {"ws": 0, "wd": 0, "a": 0, "t": 0, "n": 0}

        def norm_T(gtab, dstT, B_dst):
            for tt in range(4):
                si = c3["n"] % 2; c3["n"] += 1
                S.op("act", lambda e, tt=tt, si=si: e.activation(out=xn[:, :], in_=xres[:, tt, :], func=AF.Square, accum_out=st8[si][:, 0:1]),
                     reads=[B_xres[tt]], writes=[B_xn, B_st8[si]])
                S.op("pool", lambda e, si=si: e.tensor_scalar(out=st8[si][:, 1:2], in0=st8[si][:, 0:1], scalar1=1.0 / D, scalar2=EPS, op0=ALU.mult, op1=ALU.add),
                     reads=[B_st8[si]], writes=[B_st8[si]])
                S.op("pool", lambda e, si=si: e.tensor_tensor(out=st8[si][:, 2:3], in0=st8[si][:, 1:2], in1=mhalf[:, 0:1], op=ALU.pow), reads=[B_st8[si], B_const], writes=[B_st8[si]])
                S.op("act", lambda e, tt=tt, si=si: e.activation(out=xn[:, :], in_=xres[:, tt, :], func=AF.Copy, scale=st8[si][:, 2:3]), reads=[B_xres[tt], B_st8[si]], writes=[B_xn])
                for k4 in range(4):
                    bi = 4 + (c3["t"] % 2); c3["t"] += 1
                    for kk in range(4):
                        k = k4 * 4 + kk
                        S.op("pe", lambda e, k=k, kk=kk, bi=bi: e.transpose(out=pbb[bi][:, kk * 128:(kk + 1) * 128], in_=xn[:, k * 128:(k + 1) * 128], identity=identb[:, :]),
                             reads=[B_xn, B_const], pwrites=[B_pb[bi]])
                    for kk in range(4):
                        k = k4 * 4 + kk
                        S.op("dve", lambda e, k=k, kk=kk, bi=bi, tt=tt: e.tensor_scalar(out=dstT[:, k, tt * 128:(tt + 1) * 128], in0=pbb[bi][:, kk * 128:(kk + 1) * 128],
                                                                                     scalar1=gtab[:, k:k + 1], scalar2=None, op0=ALU.mult),
                             reads=[B_pb[bi], B_const], pwrites=[B_dst])

        for g in range(NG):
            for tt in range(4):
                S.op("sp", lambda e, g=g, tt=tt: e.dma_start(out=xres[:, tt, :], in_=L["x_own"][(g * 4 + tt) * 128:(g * 4 + tt + 1) * 128, :]), writes=[B_xres[tt]], dma=True)
            S.op("sp", lambda e, g=g: e.dma_start(out=mT[:, :, :], in_=L["mixT_s"][:, g * 512:(g + 1) * 512].rearrange("(k p) t -> p k t", p=128)),
                 reads=[B_scr["mixT"]], writes=[B_mT], dma=True)
            for c in range(8):
                wi = c3["ws"] % 4; c3["ws"] += 1
                S.op("sp", lambda e, c=c, wi=wi: e.dma_start(out=ws[wi][:, :, :], in_=L["wb_o"].ap()[:, c * 256:(c + 1) * 256].rearrange("(k p) n -> p k n", p=128)),
                     reads=[B_w["o"]], writes=[B_ws[wi]], dma=True)
                for tt in range(4):
                    ai = c3["a"] % 4; c3["a"] += 1
                    for k in range(KD):
                        S.op("pe", lambda e, k=k, tt=tt, wi=wi, ai=ai: e.matmul(pb[ai][:, 0:256], lhsT=mT[:, k, tt * 128:(tt + 1) * 128], rhs=ws[wi][:, k, :], start=(k == 0), stop=(k == KD - 1)),
                             reads=[B_mT, B_ws[wi]], pwrites=[B_pb[ai]])
                    S.op("dve", lambda e, tt=tt, c=c, ai=ai: e.tensor_tensor(out=xres[:, tt, c * 256:(c + 1) * 256], in0=xres[:, tt, c * 256:(c + 1) * 256], in1=pb[ai][:, 0:256], op=ALU.add),
                         reads=[B_pb[ai], B_xres[tt]], writes=[B_xres[tt]])
            norm_T(gffn_t, h2T, B_h2T)
            for half in range(2):
                for js in range(HK // 2):
                    col0 = (half * HK + js * 2) * 128
                    wg_i = c3["ws"] % 4; c3["ws"] += 1
                    wu_i = c3["ws"] % 4; c3["ws"] += 1
                    S.op("sp", lambda e, col0=col0, wg_i=wg_i: e.dma_start(out=ws[wg_i][:, :, :], in_=L["wb_g"].ap()[:, col0:col0 + 256].rearrange("(k p) n -> p k n", p=128)),
                         reads=[B_w["g"]], writes=[B_ws[wg_i]], dma=True)
                    S.op("sp", lambda e, col0=col0, wu_i=wu_i: e.dma_start(out=ws[wu_i][:, :, :], in_=L["wb_u"].ap()[:, col0:col0 + 256].rearrange("(k p) n -> p k n", p=128)),
                         reads=[B_w["u"]], writes=[B_ws[wu_i]], dma=True)
                    for jj in range(2):
                        jl = js * 2 + jj
                        gi = (jl % 2) * 2; ui = gi + 1
                        for k in range(KD):
                            S.op("pe", lambda e, k=k, jj=jj, wg_i=wg_i, gi=gi: e.matmul(pb[gi][:, :], lhsT=ws[wg_i][:, k, jj * 128:(jj + 1) * 128], rhs=h2T[:, k, :], start=(k == 0), stop=(k == KD - 1)),
                                 reads=[B_ws[wg_i], B_h2T], pwrites=[B_pb[gi]])
                        for k in range(KD):
                            S.op("pe", lambda e, k=k, jj=jj, wu_i=wu_i, ui=ui: e.matmul(pb[ui][:, :], lhsT=ws[wu_i][:, k, jj * 128:(jj + 1) * 128], rhs=h2T[:, k, :], start=(k == 0), stop=(k == KD - 1)),
                                 reads=[B_ws[wu_i], B_h2T], pwrites=[B_pb[ui]])
                        sj = jl % 2
                        S.op("act", lambda e, gi=gi, sj=sj: e.activation(out=stmp[sj][:, :], in_=pb[gi][:, :], func=AF.Silu), reads=[B_pb[gi]], writes=[B_stmp[sj]])
                        S.op("dve", lambda e, ui=ui, sj=sj, jl=jl: e.tensor_tensor(out=actT[:, jl, :], in0=stmp[sj][:, :], in1=pb[ui][:, :], op=ALU.mult),
                             reads=[B_stmp[sj], B_pb[ui]], pwrites=[B_actT])
                for c in range(4):
                    for sub in range(2):
                        di = c3["wd"] % 2; c3["wd"] += 1
                        r0 = (half * HK + sub * 11) * 128
                        S.op("sp", lambda e, r0=r0, c=c, di=di: e.dma_start(out=wd[di][:, :, :], in_=L["wb_d"].ap()[r0:r0 + 11 * 128, c * 512:(c + 1) * 512].rearrange("(k p) n -> p k n", p=128)),
                             reads=[B_w["d"]], writes=[B_wd[di]], dma=True)
                        for tt in range(4):
                            for k in range(11):
                                S.op("pe", lambda e, k=k, tt=tt, sub=sub, di=di: e.matmul(pb[4 + tt][:, :], lhsT=actT[:, sub * 11 + k, tt * 128:(tt + 1) * 128], rhs=wd[di][:, k, :],
                                                                                         start=(sub == 0 and k == 0), stop=(sub == 1 and k == 10)),
                                     reads=[B_actT, B_wd[di]], pwrites=[B_pb[4 + tt]])
                    for tt in range(4):
                        S.op("dve", lambda e, tt=tt, c=c: e.tensor_tensor(out=xres[:, tt, c * 512:(c + 1) * 512], in0=xres[:, tt, c * 512:(c + 1) * 512], in1=pb[4 + tt][:, :], op=ALU.add),
                             reads=[B_pb[4 + tt], B_xres[tt]], writes=[B_xres[tt]])
            norm_T(gple_t, mT, B_mT)
            for tt in range(4):
                pi = tt % 2
                S.op("sp", lambda e, g=g, tt=tt, pi=pi: e.dma_start(out=pin[pi][:, :], in_=L["p_own"][(g * 4 + tt) * 128:(g * 4 + tt + 1) * 128, :]), writes=[B_pin[pi]], dma=True)
                S.op("pool", lambda e, pi=pi: e.tensor_copy(out=pinb[pi][:, :], in_=pin[pi][:, :]), reads=[B_pin[pi]], writes=[B_pinb[pi]])
                for kp in range(2):
                    S.op("pe", lambda e, kp=kp, pi=pi: e.transpose(out=pbb[4][:, kp * 128:(kp + 1) * 128], in_=pinb[pi][:, kp * 128:(kp + 1) * 128], identity=identb[:, :]),
                         reads=[B_pinb[pi], B_const], pwrites=[B_pb[4]])
                S.op("dve", lambda e, tt=tt: e.tensor_copy(out=pT[:, :, tt * 128:(tt + 1) * 128], in_=pbb[4][:, 0:256].rearrange("p (k t) -> p k t", k=2)), reads=[B_pb[4]], pwrites=[B_pT])
            for c in range(8):
                wi = c3["ws"] % 4; c3["ws"] += 1
                S.op("sp", lambda e, c=c, wi=wi: e.dma_start(out=ws[wi][:, :, :], in_=L["wb_pg"].ap()[:, c * 256:(c + 1) * 256].rearrange("(k p) n -> p k n", p=128)),
                     reads=[B_w["pg"]], writes=[B_ws[wi]], dma=True)
                for tt in range(4):
                    ai = (c3["a"] % 2) * 2; c3["a"] += 1
                    bi_ = ai + 1
                    for k in range(KD):
                        S.op("pe", lambda e, k=k, tt=tt, wi=wi, ai=ai: e.matmul(pb[ai][:, 0:256], lhsT=mT[:, k, tt * 128:(tt + 1) * 128], rhs=ws[wi][:, k, :], start=(k == 0), stop=(k == KD - 1)),
                             reads=[B_mT, B_ws[wi]], pwrites=[B_pb[ai]])
                    for kp in range(2):
                        S.op("pe", lambda e, kp=kp, tt=tt, c=c, bi_=bi_: e.matmul(pb[bi_][:, 0:256], lhsT=pT[:, kp, tt * 128:(tt + 1) * 128], rhs=wpp[:, kp, c * 256:(c + 1) * 256], start=(kp == 0), stop=(kp == 1)),
                             reads=[B_pT, B_c4], pwrites=[B_pb[bi_]])
                    gj = tt % 2
                    S.op("dve", lambda e, ai=ai, c=c, gj=gj: e.tensor_tensor(out=gtmp[gj][:, :], in0=pb[ai][:, 0:256], in1=bpg_t[:, c * 256:(c + 1) * 256], op=ALU.add),
                         reads=[B_pb[ai], B_c4], writes=[B_gtmp[gj]])
                    S.op("act", lambda e, gj=gj: e.activation(out=gtmp[gj][:, :], in_=gtmp[gj][:, :], func=AF.Sigmoid), reads=[B_gtmp[gj]], writes=[B_gtmp[gj]])
                    S.op("dve", lambda e, bi_=bi_, gj=gj: e.tensor_tensor(out=gtmp[gj][:, :], in0=pb[bi_][:, 0:256], in1=gtmp[gj][:, :], op=ALU.mult),
                         reads=[B_pb[bi_], B_gtmp[gj]], writes=[B_gtmp[gj]])
                    S.op("pool", lambda e, tt=tt, c=c, gj=gj: e.tensor_tensor(out=xres[:, tt, c * 256:(c + 1) * 256], in0=xres[:, tt, c * 256:(c + 1) * 256], in1=gtmp[gj][:, :], op=ALU.add),
                         reads=[B_gtmp[gj], B_xres[tt]], writes=[B_xres[tt]])
            for tt in range(4):
                S.op("sp", lambda e, g=g, tt=tt: e.dma_start(out=L["out"][(g * 4 + tt) * 128:(g * 4 + tt + 1) * 128, :], in_=xres[:, tt, :]), reads=[B_xres[tt]], dma=True)
        st3 = S.emit()
    print("phase3", st3, flush=True)


def _tables(NBH, s):
    TH = NBH * 256
    NBK = 2 * NBH
    f32 = np.float32
    scale = f32(1.0 / np.sqrt(HD))
    inv_a = np.power(f32(10000.0), -(np.arange(0, HD, 2, dtype=f32) / f32(HD))).astype(f32)
    inv_r = np.power(f32(10000.0), -np.linspace(0.0, 1.0, HD // 2, dtype=f32)).astype(f32)

    def rope_tab(pos, inv):
        ang = (pos.astype(f32)[:, None] * inv[None, :]).astype(f32)
        return np.concatenate([np.cos(ang), np.sin(ang)], axis=1).astype(f32)

    pos_pre = np.arange(TH)
    pos_own = s * TH + np.arange(TH)
    t = {}
    t["rope_a_pre"] = rope_tab(pos_pre, inv_a); t["rope_a_own"] = rope_tab(pos_own, inv_a)
    t["rope_r_pre"] = rope_tab(pos_pre, inv_r); t["rope_r_own"] = rope_tab(pos_own, inv_r)
    gam = (1.0 - np.power(2.0, -5.0 - np.arange(H, dtype=np.float64)))
    lg = np.log(gam)
    j = np.arange(128, dtype=np.float64)
    t["c_kdec"] = (float(scale) * np.exp(lg[None, :] * (127.0 - j)[:, None])).astype(f32)
    t["c_qdec"] = np.exp(lg[None, :] * (j + 1.0)[:, None]).astype(f32)
    t["c_gch"] = np.broadcast_to(np.exp(lg * 128.0)[None, :], (128, H)).astype(f32).copy()
    m = j[:, None, None]; tq = j[None, None, :]
    dec = float(scale) * np.exp(-lg[None, :, None] * (m + 1.0)) * (m <= tq)
    t["c_decT"] = dec.astype(f32).reshape(128, H * 128)
    past = np.full((NBH, NBK), NEGP, dtype=f32)
    diag = np.full((NBH, NBK), -3.0e38, dtype=f32)
    for i in range(NBH):
        if s == 1:
            past[i, :NBH] = 0.0
        past[i, NBH:NBH + i] = 0.0
        diag[i, NBH + i] = 0.0
    t["c_past"] = np.broadcast_to(past.reshape(1, -1), (128, NBH * NBK)).copy()
    t["c_diag"] = np.broadcast_to(diag.reshape(1, -1), (128, NBH * NBK)).copy()
    tk = np.arange(128)[:, None, None]; pp = np.arange(4)[None, :, None]; cc = np.arange(512)[None, None, :]
    t["c_mask"] = ((pp * 128 + tk) <= cc).astype(f32).reshape(128, 4 * 512).astype(ml_dtypes.bfloat16)
    oh = np.zeros((128, NBK, 128), dtype=f32)
    for n in range(NBK):
        oh[n, n, :] = 1.0
    t["c_onehot"] = oh.reshape(128, NBK * 128).astype(ml_dtypes.bfloat16)
    t["c_identb"] = np.eye(128, dtype=f32).astype(ml_dtypes.bfloat16)
    t["c_identf"] = np.eye(128, dtype=f32)
    return t


def make_in_maps(inputs, NBH, n_batch):
    TH = NBH * 256
    x = np.asarray(inputs["x"], dtype=np.float32)
    p = np.asarray(inputs["p"], dtype=np.float32)
    shared = {
        "w_in": inputs["w_in"][0], "w_o": inputs["w_o"][0], "w_gate": inputs["w_gate"][0], "w_up": inputs["w_up"][0],
        "w_down": inputs["w_down"][0], "w_pg": inputs["w_ple_gate"][0], "w_pp": inputs["w_ple_proj"][0],
        "g_mix": inputs["g_mix"][0], "g_ffn": inputs["g_ffn"][0], "g_ple": inputs["g_ple"][0],
        "q_norm": inputs["q_norm"][0], "k_norm": inputs["k_norm"][0], "g_ret": inputs["g_ret"][0],
        "b_pg": inputs["b_ple_gate"][0],
    }
    shared = {k: np.ascontiguousarray(np.asarray(v, dtype=np.float32)) for k, v in shared.items()}
    tabs = [_tables(NBH, 0), _tables(NBH, 1)]
    maps = []
    for c in range(2 * n_batch):
        b, s = c // 2, c % 2
        m = dict(shared)
        m.update(tabs[s])
        m["x_own"] = np.ascontiguousarray(x[b, s * TH:(s + 1) * TH])
        m["x_pre"] = np.ascontiguousarray(x[b, 0:TH]) if s == 1 else np.zeros((TH, D), np.float32)
        m["p_own"] = np.ascontiguousarray(p[0, b, s * TH:(s + 1) * TH])
        maps.append(m)
    return maps


_NC_CACHE = {}


def kernel(**inputs):
    x = np.asarray(inputs["x"])
    B, T, _ = x.shape
    NBH = T // 512
    key = (NBH,)
    if key not in _NC_CACHE:
        _NC_CACHE[key] = build(NBH)
    nc = _NC_CACHE[key]
    maps = make_in_maps(inputs, NBH, B)
    res = run_bass_kernel_spmd(nc, maps, core_ids=list(range(2 * B)))
    TH = NBH * 256
    out = np.empty((B, T, D), np.float32)
    for c in range(2 * B):
        b, s = c // 2, c % 2
        out[b, s * TH:(s + 1) * TH] = np.asarray(res.results[c]["out"], dtype=np.float32)
    return out
```

```python
import numpy as np
import contextlib
import ml_dtypes
import concourse.bass as bass
import concourse.mybir as mybir
from concourse.bass_utils import run_bass_kernel_spmd
from concourse.alu_op_type import AluOpType as ALU

F32 = mybir.dt.float32
BF16 = mybir.dt.bfloat16
AF = mybir.ActivationFunctionType
AX = mybir.AxisListType

D = 2048
KD = 16
HD = 128
H = 8
INC = 7168
DFF = 5632
KF = 44
PLE = 256
EPS = 1e-6
BIG = 30000.0
NEGP = -1.0e9

N_DMA_SEMS = 32
N_SW_SEMS = 8
ENGS = ("pe", "act", "dve", "pool", "sp")


class Buf:
    __slots__ = ("w", "r", "pr", "name")

    def __init__(self, name=""):
        self.w = {}
        self.r = {}
        self.pr = set()
        self.name = name


class Op:
    __slots__ = ("eng", "fn", "deps", "idx", "pos", "inc", "tick", "is_dma", "dsem", "dval")


class Sched:
    def __init__(self, nc, stack):
        self.nc = nc
        self.sem = {e: stack.enter_context(nc.semaphore("s_" + e)) for e in ENGS if e != "sp"}
        self.dsems = [stack.enter_context(nc.semaphore("d%d" % i)) for i in range(N_DMA_SEMS)]
        self.drr_sw = 0
        self.tickbase = {e: 0 for e in self.sem}
        self.dcount = [0] * N_DMA_SEMS
        self.drr = 0
        self.waited = {e: {} for e in ENGS}
        self.ops = []
        self.pos = {e: 0 for e in ENGS}
        self.phase_dma = {}
        self.touched = set()
        self.lastop = {}

    def op(self, eng, fn, reads=(), writes=(), pwrites=(), dma=False):
        o = Op()
        o.eng = eng
        o.fn = fn
        o.idx = len(self.ops)
        o.is_dma = dma
        o.inc = False
        o.tick = 0
        o.pos = self.pos[eng]
        self.pos[eng] += 1
        deps = set()
        for b in reads:
            deps.update(b.w.values())
        for b in writes:
            deps.update(b.r.values())
            deps.update(b.w.values())
        for b in pwrites:
            deps.update(b.r.values())
            if b.r:
                deps.update(b.w.values())
            else:
                deps.update(b.pr)
        deps.discard(o.idx)
        o.deps = deps
        if dma:
            if eng == "pool":
                s = N_DMA_SEMS - N_SW_SEMS + self.drr_sw
                self.drr_sw = (self.drr_sw + 1) % N_SW_SEMS
            else:
                s = self.drr
                self.drr = (self.drr + 1) % (N_DMA_SEMS - N_SW_SEMS)
            if s in self.phase_dma:
                deps.add(self.phase_dma[s])
            self.dcount[s] += 1
            o.dsem = s
            o.dval = 16 * self.dcount[s]
            self.phase_dma[s] = o.idx
        key = ("dma", o.idx) if dma else eng
        if not dma:
            self.lastop[eng] = o.idx
        for b in reads:
            self.touched.add(b)
            b.r[key] = o.idx
        for b in writes:
            self.touched.add(b)
            b.pr = set(b.r.values()) | set(b.w.values())
            b.w = {key: o.idx}
            b.r = {}
        for b in pwrites:
            self.touched.add(b)
            if b.r:
                b.pr = set(b.r.values()) | set(b.w.values())
                b.w = {key: o.idx}
                b.r = {}
            else:
                b.w[key] = o.idx
        self.ops.append(o)
        return o

    def _needs(self, c, p):
        if p.is_dma:
            return True
        if p.eng == c.eng:
            if c.eng == "pe":
                return False
            if c.is_dma:
                return True
            if c.eng == "pool":
                return True
            return p.pos >= c.pos - 3
        return True

    def emit(self):
        nc = self.nc
        ops = self.ops
        f = Op()
        f.eng = "sp"; f.fn = None; f.idx = len(ops); f.is_dma = False
        f.inc = False; f.tick = 0; f.pos = self.pos["sp"]
        f.deps = set(self.phase_dma.values())
        for e, i in self.lastop.items():
            if e != "sp":
                f.deps.add(i)
        ops.append(f)
        for o in ops:
            for d in o.deps:
                p = ops[d]
                if (not p.is_dma) and self._needs(o, p):
                    p.inc = True
        cnt = dict(self.tickbase)
        for o in ops:
            if (not o.is_dma) and o.inc:
                cnt[o.eng] += 1
                o.tick = cnt[o.eng]
        per = {e: [] for e in ENGS}
        for o in ops:
            per[o.eng].append(o)
        waited = self.waited
        sem = self.sem
        dsems = self.dsems

        def run(e, eng):
            wd = waited[e]
            for o in per[e]:
                for d in sorted(o.deps):
                    p = ops[d]
                    if not self._needs(o, p):
                        continue
                    if p.is_dma:
                        k = ("d", p.dsem)
                        if wd.get(k, 0) < p.dval:
                            eng.wait_ge(dsems[p.dsem], p.dval)
                            wd[k] = p.dval
                    else:
                        k = p.eng
                        if wd.get(k, 0) < p.tick:
                            eng.wait_ge(sem[p.eng], p.tick)
                            wd[k] = p.tick
                if o.fn is None:
                    continue
                inst = o.fn(eng)
                if o.is_dma:
                    inst.then_inc(dsems[o.dsem], 16)
                elif o.inc:
                    inst.then_inc(sem[o.eng], 1)

        with nc.Block() as block:
            @block.tensor
            def _(eng):
                run("pe", eng)

            @block.scalar
            def _(eng):
                run("act", eng)

            @block.vector
            def _(eng):
                run("dve", eng)

            @block.gpsimd
            def _(eng):
                run("pool", eng)

            @block.sync
            def _(eng):
                run("sp", eng)

        self.tickbase = cnt
        self.ops = []
        self.pos = {e: 0 for e in ENGS}
        self.phase_dma = {}
        for b in self.touched:
            b.w = {}
            b.r = {}
            b.pr = set()
        self.touched = set()
        self.lastop = {}
        return {e: len(per[e]) for e in ENGS}


def bc_mid(t, row, off, n_mid, n_in):
    return bass.AP(t, off, [[row, 128], [0, n_mid], [1, n_in]])


def bc_in(t, row, off, n_mid, n_in):
    return bass.AP(t, off, [[row, 128], [1, n_mid], [0, n_in]])


def build(NBH, debug=False):
    TH = NBH * 256
    NT = TH // 128
    NG = TH // 512
    NBK = 2 * NBH
    TT = 2 * TH
    scale = 1.0 / np.sqrt(HD)

    nc = bass.Bass("TRN2", target_bir_lowering=False)
    din = lambda n, s, d=F32: nc.dram_tensor(n, list(s), d, kind="ExternalInput")
    x_pre = din("x_pre", [TH, D]); x_own = din("x_own", [TH, D]); p_own = din("p_own", [TH, PLE])
    w_in = din("w_in", [D, INC]); w_o = din("w_o", [D, D]); w_gate = din("w_gate", [D, DFF])
    w_up = din("w_up", [D, DFF]); w_down = din("w_down", [DFF, D]); w_pg = din("w_pg", [D, D])
    w_pp = din("w_pp", [PLE, D])
    g_mix = din("g_mix", [D]); g_ffn = din("g_ffn", [D]); g_ple = din("g_ple", [D])
    q_norm = din("q_norm", [HD]); k_norm = din("k_norm", [HD]); g_ret = din("g_ret", [H * HD])
    b_pg = din("b_pg", [D])
    rope_a_pre = din("rope_a_pre", [TH, 128]); rope_a_own = din("rope_a_own", [TH, 128])
    rope_r_pre = din("rope_r_pre", [TH, 128]); rope_r_own = din("rope_r_own", [TH, 128])
    c_kdec = din("c_kdec", [128, H]); c_qdec = din("c_qdec", [128, H]); c_gch = din("c_gch", [128, H])
    c_decT = din("c_decT", [128, H * 128])
    c_past = din("c_past", [128, NBH * NBK]); c_diag = din("c_diag", [128, NBH * NBK])
    c_mask = din("c_mask", [128, 4 * 512], BF16)
    c_onehot = din("c_onehot", [128, NBK * 128], BF16)
    c_identb = din("c_identb", [128, 128], BF16); c_identf = din("c_identf", [128, 128])
    out = nc.dram_tensor("out", [TH, D], F32, kind="ExternalOutput")

    dscr = lambda n, s, d=BF16: nc.dram_tensor(n, list(s), d)
    wb_in = dscr("wb_in", [D, INC]); wb_o = dscr("wb_o", [D, D]); wb_g = dscr("wb_g", [D, DFF])
    wb_u = dscr("wb_u", [D, DFF]); wb_d = dscr("wb_d", [DFF, D]); wb_pg = dscr("wb_pg", [D, D])
    wb_pp = dscr("wb_pp", [PLE, D])
    kT_s = dscr("kT_s", [H, 128, TT]); v_s = dscr("v_s", [TT, H * HD]); qT_s = dscr("qT_s", [H, 128, TH])
    rqT_s = dscr("rqT_s", [H, 128, TH]); rkT_s = dscr("rkT_s", [H, 128, TH])
    rkd_s = dscr("rkd_s", [TH, H * HD]); rv_s = dscr("rv_s", [TH, H * HD]); rgs_s = dscr("rgs_s", [TH, H * HD])
    mixT_s = dscr("mixT_s", [D, TH])
    dbg = {}
    if debug:
        for n, s in (("d_qT", [H, 128, TH]), ("d_kT", [H, 128, TT]), ("d_v", [TT, H * HD]), ("d_mixT", [D, TH]),
                     ("d_rqT", [H, 128, TH]), ("d_rkd", [TH, H * HD])):
            dbg[n] = nc.dram_tensor(n, s, BF16, kind="ExternalOutput")

    with contextlib.ExitStack() as gst:
        S = Sched(nc, gst)
        gsb = lambda n, s, d=F32: gst.enter_context(nc.sbuf_tensor(n, list(s), d))
        identb = gsb("identb", [128, 128], BF16); identf = gsb("identf", [128, 128])
        gmix_t = gsb("gmix_t", [128, KD]); gffn_t = gsb("gffn_t", [128, KD]); gple_t = gsb("gple_t", [128, KD])
        qn_t = gsb("qn_t", [128, HD]); kn_t = gsb("kn_t", [128, HD])
        kdec_t = gsb("kdec_t", [128, H]); qdec_t = gsb("qdec_t", [128, H]); gch_t = gsb("gch_t", [128, H])
        mhalf = gsb("mhalf", [128, 8])
        kmT = gsb("kmT", [128, H * NBK])
        Sst = gsb("Sst", [128, H * 128])
        B_const = Buf("const"); B_kmT = Buf("kmT"); B_S = Buf("S")
        B_w = {n: Buf(n) for n in ("in", "o", "g", "u", "d", "pg", "pp")}
        B_scr = {n: Buf(n) for n in ("kT", "v", "qT", "rqT", "rkT", "rkd", "rv", "rgs", "mixT")}

        def cast_w(src, dst, R, C, buf):
            for r0 in range(0, R, 1024):
                rr = min(1024, R - r0)
                for c0 in range(0, C, 2048):
                    cc = min(2048, C - c0)
                    S.op("pool", lambda e, r0=r0, rr=rr, c0=c0, cc=cc: e.dma_start(
                        out=dst[r0:r0 + rr, c0:c0 + cc], in_=src[r0:r0 + rr, c0:c0 + cc]),
                        pwrites=[buf], dma=True)

        cast_w(w_in, wb_in, D, INC, B_w["in"])
        ld = lambda o_, i_, **kw: S.op("sp", lambda e: e.dma_start(out=o_, in_=i_, **kw), pwrites=[B_const], dma=True)
        ld(identb[:, :], c_identb[:, :]); ld(identf[:, :], c_identf[:, :])
        for t_, g_ in ((gmix_t, g_mix), (gffn_t, g_ffn), (gple_t, g_ple)):
            ld(t_[:, :], g_.ap().rearrange("(k p) -> p k", p=128), allow_slow_non_contiguous=True)
        ld(qn_t[:, :], q_norm.ap().partition_broadcast(128)); ld(kn_t[:, :], k_norm.ap().partition_broadcast(128))
        ld(kdec_t[:, :], c_kdec[:, :]); ld(qdec_t[:, :], c_qdec[:, :]); ld(gch_t[:, :], c_gch[:, :])
        S.op("pool", lambda e: e.memset(mhalf[:, :], -0.5), pwrites=[B_const])
        S.op("pool", lambda e: e.memset(Sst[:, :], 0.0), writes=[B_S])
        S.op("pool", lambda e: e.memset(kmT[:, :], 0.0), writes=[B_kmT])
        st0 = S.emit()

        with contextlib.ExitStack() as st:
            sb = lambda n, s, d=F32: st.enter_context(nc.sbuf_tensor(n, list(s), d))
            psb = lambda n, s, d=F32: st.enter_context(nc.psum_tensor(n, list(s), d))
            NSL = 3
            wsl = [sb("wsl%d" % i, [128, KD, 512], BF16) for i in range(NSL)]; B_wsl = [Buf() for _ in range(NSL)]
            xt = [sb("xt%d" % i, [128, D]) for i in range(2)]; B_xt = [Buf() for _ in range(2)]
            xn = [sb("xn%d" % i, [128, D], BF16) for i in range(2)]; B_xn = [Buf() for _ in range(2)]
            st8 = [sb("st8_%d" % i, [128, 8]) for i in range(2)]; B_st8 = [Buf() for _ in range(2)]
            hT = [sb("hT%d" % i, [128, KD, 512], BF16) for i in range(2)]; B_hT = [Buf() for _ in range(2)]
            NTMP = 6
            tmp = [[sb("tmp%d_%d" % (i, j), [128, 512]) for j in range(3)] for i in range(NTMP)]
            B_tmp = [[Buf() for j in range(3)] for i in range(NTMP)]
            sm = [sb("sm%d" % i, [128, 8]) for i in range(NTMP)]; B_sm = [Buf() for _ in range(NTMP)]
            NOB = 8
            ob = [sb("ob%d" % i, [128, 512], BF16) for i in range(NOB)]; B_ob = [Buf() for _ in range(NOB)]
            NSTG = 3
            stage = [sb("stage%d" % i, [128, 4, 512], BF16) for i in range(NSTG)]; B_stage = [Buf() for _ in range(NSTG)]
            kdh = sb("kdh", [128, 4, H * HD], BF16); B_kdh = [Buf() for _ in range(4)]
            NVO = 4
            vob = [sb("vob%d" % i, [128, 512], BF16) for i in range(NVO)]; B_vob = [Buf() for _ in range(NVO)]
            NKO = 8
            kdo = [sb("kdo%d" % i, [128, 512], BF16) for i in range(NKO)]; B_kdo = [Buf() for _ in range(NKO)]
            ropa = [sb("ropa%d" % i, [128, 4, 128]) for i in range(2)]; B_ropa = [Buf() for _ in range(2)]
            ropr = [sb("ropr%d" % i, [128, 4, 128]) for i in range(2)]; B_ropr = [Buf() for _ in range(2)]
            NACC = 4
            pacc = [psb("pacc%d" % i, [128, 512]) for i in range(NACC)]; B_pacc = [Buf() for _ in range(NACC)]
            ptx = psb("ptx", [128, 8, 128], BF16); B_ptx = Buf()
            ptq = psb("ptq", [128, 8, 128], BF16); B_ptq = Buf()
            pU = [psb("pU%d" % i, [128, 512]) for i in range(2)]; B_pU = [Buf() for _ in range(2)]

            cast_w(w_o, wb_o, D, D, B_w["o"]); cast_w(w_pg, wb_pg, D, D, B_w["pg"]); cast_w(w_pp, wb_pp, PLE, D, B_w["pp"])

            ctr = {"tile": 0, "slab": 0, "pp": 0, "acc": 0, "vo": 0, "stg": 0, "ko": 0, "ob": 0}

            def rope(T_, BT_, tab, Btab, outb, Bout, tt, eng):
                zc = T_[0]; Bz = BT_[0]
                z1 = bass.AP(zc, 0, [[512, 128], [128, 4], [1, 64]])
                z2 = bass.AP(zc, 64, [[512, 128], [128, 4], [1, 64]])
                cs = bc_mid(tab, 512, tt * 128, 4, 64); sn = bc_mid(tab, 512, tt * 128 + 64, 4, 64)
                v4 = lambda t_, off: bass.AP(t_, off, [[512, 128], [64, 4], [1, 64]])
                a, b_, c_, d_ = v4(T_[1], 0), v4(T_[1], 256), v4(T_[2], 0), v4(T_[2], 256)
                o1 = bass.AP(outb, 0, [[512, 128], [128, 4], [1, 64]])
                o2 = bass.AP(outb, 64, [[512, 128], [128, 4], [1, 64]])
                S.op(eng, lambda e: e.tensor_tensor(out=a, in0=z1, in1=cs, op=ALU.mult), reads=[Bz, Btab], writes=[BT_[1]])
                S.op(eng, lambda e: e.tensor_tensor(out=b_, in0=z2, in1=sn, op=ALU.mult), reads=[Bz, Btab], pwrites=[BT_[1]])
                S.op(eng, lambda e: e.tensor_tensor(out=c_, in0=z2, in1=cs, op=ALU.mult), reads=[Bz, Btab], writes=[BT_[2]])
                S.op(eng, lambda e: e.tensor_tensor(out=d_, in0=z1, in1=sn, op=ALU.mult), reads=[Bz, Btab], pwrites=[BT_[2]])
                S.op(eng, lambda e: e.tensor_tensor(out=o1, in0=a, in1=b_, op=ALU.subtract), reads=[BT_[1]], pwrites=[Bout])
                S.op(eng, lambda e: e.tensor_tensor(out=o2, in0=c_, in1=d_, op=ALU.add), reads=[BT_[2]], pwrites=[Bout])

            seq = [(g, False) for g in range(NG)] + [(g, True) for g in range(NG)]

            def prologue_steps(q):
                g, own = seq[q]
                hb = q % 2
                xsrc = x_own if own else x_pre
                ra_src = rope_a_own if own else rope_a_pre
                rr_src = rope_r_own if own else rope_r_pre
                steps = []
                for tt in range(4):
                    t = g * 4 + tt
                    ti = ctr["tile"] % 2; ctr["tile"] += 1

                    def sa(tt=tt, t=t, ti=ti):
                        if tt == 0:
                            S.op("sp", lambda e: e.dma_start(out=ropa[hb][:, :, :], in_=ra_src[g * 512:(g + 1) * 512, :].rearrange("(t p) c -> p t c", p=128)), writes=[B_ropa[hb]], dma=True)
                            S.op("sp", lambda e: e.dma_start(out=ropr[hb][:, :, :], in_=rr_src[g * 512:(g + 1) * 512, :].rearrange("(t p) c -> p t c", p=128)), writes=[B_ropr[hb]], dma=True)
                        S.op("sp", lambda e: e.dma_start(out=xt[ti][:, :], in_=xsrc[t * 128:(t + 1) * 128, :]), writes=[B_xt[ti]], dma=True)
                        S.op("act", lambda e: e.activation(out=xn[ti][:, :], in_=xt[ti][:, :], func=AF.Square, accum_out=st8[ti][:, 0:1]),
                             reads=[B_xt[ti]], writes=[B_xn[ti], B_st8[ti]])
                        S.op("pool", lambda e: e.tensor_scalar(out=st8[ti][:, 1:2], in0=st8[ti][:, 0:1], scalar1=1.0 / D, scalar2=EPS, op0=ALU.mult, op1=ALU.add),
                             reads=[B_st8[ti]], writes=[B_st8[ti]])
                        S.op("pool", lambda e: e.tensor_tensor(out=st8[ti][:, 2:3], in0=st8[ti][:, 1:2], in1=mhalf[:, 0:1], op=ALU.pow),
                             reads=[B_st8[ti], B_const], writes=[B_st8[ti]])
                        S.op("act", lambda e: e.activation(out=xn[ti][:, :], in_=xt[ti][:, :], func=AF.Copy, scale=st8[ti][:, 2:3]),
                             reads=[B_xt[ti], B_st8[ti]], writes=[B_xn[ti]])

                    def sb_(tt=tt, ti=ti):
                        for k4 in range(4):
                            for kk in range(4):
                                k = k4 * 4 + kk
                                S.op("pe", lambda e, k=k, kk=kk: e.transpose(out=ptx[:, kk, :], in_=xn[ti][:, k * 128:(k + 1) * 128], identity=identb[:, :]),
                                     reads=[B_xn[ti], B_const], pwrites=[B_ptx])
                            for kk in range(4):
                                k = k4 * 4 + kk
                                S.op("dve", lambda e, k=k, kk=kk: e.tensor_scalar(out=hT[hb][:, k, tt * 128:(tt + 1) * 128], in0=ptx[:, kk, :],
                                                                               scalar1=gmix_t[:, k:k + 1], scalar2=None, op0=ALU.mult),
                                     reads=[B_ptx, B_const], pwrites=[B_hT[hb]])
                    steps.append(sa); steps.append(sb_)
                return steps

            units = []
            extra = {}
            chunk_list = []

            def slab_load(j):
                q, c = chunk_list[j]
                si = j % NSL
                S.op("sp", lambda e: e.dma_start(out=wsl[si][:, :, :], in_=wb_in.ap()[:, c * 512:(c + 1) * 512].rearrange("(k p) n -> p k n", p=128)),
                     reads=[B_w["in"]], writes=[B_wsl[si]], dma=True)

            for q, (g, own) in enumerate(seq):
                for c in (list(range(14)) if own else [2, 3, 4, 5, 8, 9, 10, 11]):
                    chunk_list.append((q, c))
            first_unit_of_group = {}
            for j, (q, c) in enumerate(chunk_list):
                g, own = seq[q]
                hb = q % 2
                tokoff = TH if own else 0
                typ = c // 2
                hc = c % 2
                si = j % NSL
                if q not in first_unit_of_group:
                    first_unit_of_group[q] = len(units)
                extra.setdefault(len(units), []).append(lambda j=j: slab_load(j + 2) if j + 2 < len(chunk_list) else None)
                need_stage = typ in (0, 1, 3) or (typ == 4 and own)
                if need_stage:
                    sidx = ctr["stg"] % NSTG; ctr["stg"] += 1
                for tt in range(4):
                    t = g * 4 + tt
                    tok0 = tokoff + t * 128
                    ai = ctr["acc"] % NACC; ctr["acc"] += 1
                    qk = typ in (0, 1, 3, 4)
                    attn = typ in (0, 1)
                    if qk:
                        pi = ctr["pp"] % NTMP; ctr["pp"] += 1
                        T_, BT_ = tmp[pi], B_tmp[pi]
                        oi = ctr["ob"] % NOB; ctr["ob"] += 1
                        ueng = "dve" if (ctr["ob"] % 2 == 0) else "pool"
                    need_vo = (not qk) or (typ == 4 and own)
                    if not qk:
                        vi = ctr["vo"] % NVO; ctr["vo"] += 1
                    elif need_vo:
                        vi = ctr["ko"] % NKO; ctr["ko"] += 1
                    A = B = C = Dd = None

                    def A(tt=tt, si=si, ai=ai, hb=hb, typ=typ, qk=qk, T_=(T_ if qk else None), BT_=(BT_ if qk else None), vi=(vi if need_vo else None)):
                        for k in range(KD):
                            S.op("pe", lambda e, k=k: e.matmul(pacc[ai][:, :], lhsT=hT[hb][:, k, tt * 128:(tt + 1) * 128], rhs=wsl[si][:, k, :], start=(k == 0), stop=(k == KD - 1)),
                                 reads=[B_hT[hb], B_wsl[si]], pwrites=[B_pacc[ai]])
                        if qk:
                            S.op("act", lambda e: e.activation(out=T_[0][:, :], in_=pacc[ai][:, :], func=AF.Copy), reads=[B_pacc[ai]], writes=[BT_[0]])
                        elif typ == 6:
                            S.op("act", lambda e: e.activation(out=vob[vi][:, :], in_=pacc[ai][:, :], func=AF.Silu), reads=[B_pacc[ai]], writes=[B_vob[vi]])
                        else:
                            S.op("act", lambda e: e.activation(out=vob[vi][:, :], in_=pacc[ai][:, :], func=AF.Copy), reads=[B_pacc[ai]], writes=[B_vob[vi]])

                    if not qk:
                        def B(tt=tt, t=t, tok0=tok0, hc=hc, typ=typ, own=own, vi=vi):
                            if typ == 2:
                                S.op("sp", lambda e: e.dma_start(out=v_s[tok0:tok0 + 128, hc * 512:(hc + 1) * 512], in_=vob[vi][:, :]), reads=[B_vob[vi]], pwrites=[B_scr["v"]], dma=True)
                            elif typ == 6:
                                S.op("sp", lambda e: e.dma_start(out=rgs_s[t * 128:(t + 1) * 128, hc * 512:(hc + 1) * 512], in_=vob[vi][:, :]), reads=[B_vob[vi]], pwrites=[B_scr["rgs"]], dma=True)
                            elif own:
                                S.op("sp", lambda e: e.dma_start(out=rv_s[t * 128:(t + 1) * 128, hc * 512:(hc + 1) * 512], in_=vob[vi][:, :]), reads=[B_vob[vi]], pwrites=[B_scr["rv"]], dma=True)
                            else:
                                for hh in range(4):
                                    h = hc * 4 + hh
                                    S.op("pe", lambda e, h=h, hh=hh: e.matmul(pU[hc][:, hh * 128:(hh + 1) * 128], lhsT=kdh[:, tt, h * 128:(h + 1) * 128],
                                                                          rhs=vob[vi][:, hh * 128:(hh + 1) * 128], start=True, stop=True),
                                         reads=[B_kdh[tt], B_vob[vi]], pwrites=[B_pU[hc]])
                                sv = bass.AP(Sst, hc * 512, [[H * 128, 128], [128, 4], [1, 128]])
                                S.op("pool", lambda e: e.tensor_tensor(out=sv, in0=sv, in1=bc_in(gch_t, H, hc * 4, 4, 128), op=ALU.mult), reads=[B_S, B_const], writes=[B_S])
                                S.op("dve", lambda e: e.tensor_tensor(out=Sst[:, hc * 512:(hc + 1) * 512], in0=Sst[:, hc * 512:(hc + 1) * 512], in1=pU[hc][:, :], op=ALU.add),
                                     reads=[B_S, B_pU[hc]], writes=[B_S])
                        units.append([A, B, None, None])
                        continue

                    tab, Btab = (ropa[hb], B_ropa[hb]) if attn else (ropr[hb], B_ropr[hb])
                    if attn:
                        def B(pi=pi, T_=T_, BT_=BT_, typ=typ, ueng=ueng):
                            gn = qn_t if typ == 0 else kn_t
                            zc, sq = T_[0], T_[1]
                            S.op("pool", lambda e: e.tensor_tensor(out=sq[:, :], in0=zc[:, :], in1=zc[:, :], op=ALU.mult), reads=[BT_[0]], writes=[BT_[1]])
                            S.op("dve", lambda e: e.tensor_reduce(out=sm[pi][:, 0:4], in_=sq[:, :].rearrange("p (h d) -> p h d", h=4), axis=AX.X, op=ALU.add),
                                 reads=[BT_[1]], writes=[B_sm[pi]])
                            S.op("pool", lambda e: e.tensor_scalar(out=sm[pi][:, 0:4], in0=sm[pi][:, 0:4], scalar1=1.0 / HD, scalar2=EPS, op0=ALU.mult, op1=ALU.add),
                                 reads=[B_sm[pi]], writes=[B_sm[pi]])
                            S.op("pool", lambda e: e.tensor_tensor(out=sm[pi][:, 4:8], in0=sm[pi][:, 0:4], in1=mhalf[:, 0:4], op=ALU.pow), reads=[B_sm[pi], B_const], writes=[B_sm[pi]])
                            z3 = zc[:, :].rearrange("p (h d) -> p h d", h=4)
                            S.op(ueng, lambda e: e.tensor_tensor(out=z3, in0=z3, in1=bc_in(sm[pi], 8, 4, 4, 128), op=ALU.mult), reads=[BT_[0], B_sm[pi]], writes=[BT_[0]])
                            S.op(ueng, lambda e: e.tensor_tensor(out=z3, in0=z3, in1=bc_mid(gn, HD, 0, 4, 128), op=ALU.mult), reads=[BT_[0], B_const], writes=[BT_[0]])

                    def C(pi=oi, T_=T_, BT_=BT_, tab=tab, Btab=Btab, tt=tt, typ=typ, own=own, hc=hc, vi=(vi if need_vo else None), ueng=ueng):
                        rope(T_, BT_, tab, Btab, ob[pi], B_ob[pi], tt, ueng)
                        if typ == 4:
                            o3 = ob[pi][:, :].rearrange("p (h d) -> p h d", h=4)
                            if own:
                                S.op(ueng, lambda e: e.tensor_tensor(out=kdo[vi][:, :].rearrange("p (h d) -> p h d", h=4), in0=o3, in1=bc_in(kdec_t, H, hc * 4, 4, 128), op=ALU.mult),
                                     reads=[B_ob[pi], B_const], writes=[B_kdo[vi]])
                            else:
                                S.op(ueng, lambda e: e.tensor_tensor(out=kdh[:, tt, hc * 512:(hc + 1) * 512].rearrange("p (h d) -> p h d", h=4), in0=o3,
                                                                     in1=bc_in(kdec_t, H, hc * 4, 4, 128), op=ALU.mult),
                                     reads=[B_ob[pi], B_const], pwrites=[B_kdh[tt]])

                    if need_stage:
                        def Dd(pi=oi, tt=tt, t=t, typ=typ, own=own, hc=hc, sidx=sidx, g=g, tokoff=tokoff, vi=(vi if need_vo else None)):
                            if typ == 4 and own:
                                S.op("sp", lambda e: e.dma_start(out=rkd_s[t * 128:(t + 1) * 128, hc * 512:(hc + 1) * 512], in_=kdo[vi][:, :]), reads=[B_kdo[vi]], pwrites=[B_scr["rkd"]], dma=True)
                            for hh in range(4):
                                S.op("pe", lambda e, hh=hh: e.transpose(out=ptq[:, hh, :], in_=ob[pi][:, hh * 128:(hh + 1) * 128], identity=identb[:, :]),
                                     reads=[B_ob[pi], B_const], pwrites=[B_ptq])
                            S.op("act", lambda e: e.activation(out=stage[sidx][:, :, tt * 128:(tt + 1) * 128], in_=ptq[:, 0:4, :], func=AF.Copy), reads=[B_ptq], pwrites=[B_stage[sidx]])
                            if tt == 3:
                                dst, bname, toff = {0: (qT_s, "qT", 0), 1: (kT_s, "kT", tokoff), 3: (rqT_s, "rqT", 0), 4: (rkT_s, "rkT", 0)}[typ]
                                c0 = toff + g * 512
                                S.op("sp", lambda e: e.dma_start(out=dst[hc * 4:(hc + 1) * 4, :, c0:c0 + 512].rearrange("h d t -> d h t"), in_=stage[sidx][:, :, :]),
                                     reads=[B_stage[sidx]], pwrites=[B_scr[bname]], dma=True)
                                if typ == 1:
                                    blk0 = (tokoff + g * 512) // 256
                                    kv = bass.AP(kmT, hc * 4 * NBK + blk0, [[H * NBK, 128], [NBK, 4], [1, 2]])
                                    S.op("dve", lambda e: e.tensor_reduce(out=kv, in_=stage[sidx][:, :, :].rearrange("p h (b t) -> p h b t", b=2), axis=AX.X, op=ALU.add),
                                         reads=[B_stage[sidx]], pwrites=[B_kmT])
                    units.append([A, (B if attn else None), C, (Dd if need_stage else None)])

            for q in range(len(seq)):
                u0 = first_unit_of_group[q]
                u1 = first_unit_of_group[q + 1] if q + 1 < len(seq) else len(units)
                if q + 1 < len(seq):
                    steps = prologue_steps(q + 1)
                    n = u1 - u0
                    for i_, stp in enumerate(steps):
                        extra.setdefault(u0 + 10 + (i_ * (n - 15)) // 8, []).append(stp)
            ctr["tile"] = 0
            pre0 = prologue_steps(0)
            slab_load(0)
            slab_load(1)
            for stp in pre0:
                stp()
            SKEW = (0, 1, 4, 9)
            NU = len(units)
            for i in range(NU + SKEW[3]):
                if i < NU:
                    for fn in extra.get(i, []):
                        fn()
                for s_ in range(4):
                    j = i - SKEW[s_]
                    if 0 <= j < NU and units[j][s_] is not None:
                        units[j][s_]()

            if debug:
                S.op("sp", lambda e: e.dma_start(out=dbg["d_qT"][:, :, :], in_=qT_s[:, :, :]), reads=[B_scr["qT"]], dma=True)
                S.op("sp", lambda e: e.dma_start(out=dbg["d_kT"][:, :, :], in_=kT_s[:, :, :]), reads=[B_scr["kT"]], dma=True)
                S.op("sp", lambda e: e.dma_start(out=dbg["d_v"][:, :], in_=v_s[:, :]), reads=[B_scr["v"]], dma=True)
                S.op("sp", lambda e: e.dma_start(out=dbg["d_rqT"][:, :, :], in_=rqT_s[:, :, :]), reads=[B_scr["rqT"]], dma=True)
                S.op("sp", lambda e: e.dma_start(out=dbg["d_rkd"][:, :], in_=rkd_s[:, :]), reads=[B_scr["rkd"]], dma=True)
            st1 = S.emit()
        print("phase0", st0, "phase1", st1, flush=True)
        import os as _os
        if _os.environ.get("STOP_AFTER") != "1":
            build_rest(nc, S, locals())
    return nc


def build_rest(nc, S, L):
    NBH, TH, NT, NG, NBK, TT, scale, debug, dbg = (L[k] for k in ("NBH", "TH", "NT", "NG", "NBK", "TT", "scale", "debug", "dbg"))
    identb, identf, gffn_t, gple_t, kdec_t, qdec_t, gch_t, mhalf, kmT, Sst = (L[k] for k in (
        "identb", "identf", "gffn_t", "gple_t", "kdec_t", "qdec_t", "gch_t", "mhalf", "kmT", "Sst"))
    B_const, B_kmT, B_S, B_w, B_scr, cast_w = (L[k] for k in ("B_const", "B_kmT", "B_S", "B_w", "B_scr", "cast_w"))
    scale = float(scale)

    with contextlib.ExitStack() as st:
        sb = lambda n, s, d=F32: st.enter_context(nc.sbuf_tensor(n, list(s), d))
        psb = lambda n, s, d=F32: st.enter_context(nc.psum_tensor(n, list(s), d))
        cast_jobs = []
        for src_, dst_, R_, C_, bn_ in ((L["w_gate"], L["wb_g"], D, DFF, "g"), (L["w_up"], L["wb_u"], D, DFF, "u"), (L["w_down"], L["wb_d"], DFF, D, "d")):
            for r0 in range(0, R_, 1024):
                rr = min(1024, R_ - r0)
                for c0 in range(0, C_, 2048):
                    cc = min(2048, C_ - c0)
                    cast_jobs.append(lambda src_=src_, dst_=dst_, r0=r0, rr=rr, c0=c0, cc=cc, bn_=bn_: S.op(
                        "pool", lambda e: e.dma_start(out=dst_[r0:r0 + rr, c0:c0 + cc], in_=src_[r0:r0 + rr, c0:c0 + cc]), pwrites=[B_w[bn_]], dma=True))
        kTh = [sb("kTh%d" % i, [128, TT], BF16) for i in range(2)]; B_kTh = [Buf() for _ in range(2)]
        V1 = [sb("V1_%d" % i, [128, 2 * NT, 129], BF16) for i in range(2)]; B_V1 = [Buf() for _ in range(2)]
        qTh = [sb("qTh%d" % i, [128, TH], BF16) for i in range(2)]; B_qTh = [Buf() for _ in range(2)]
        aTst = [sb("aTst%d" % i, [128, TH], BF16) for i in range(2)]; B_aTst = [Buf() for _ in range(2)]
        kmb = sb("kmb", [128, H * NBK], BF16); B_kmb = Buf()
        pastt = sb("pastt", [128, NBH * NBK]); diagt = sb("diagt", [128, NBH * NBK])
        maskc = sb("maskc", [128, 4 * 512], BF16); oneh = sb("oneh", [128, NBK * 128], BF16)
        B_c2 = Buf()
        NPT = 5
        PT2 = [sb("PT2_%d" % i, [128, 1024], BF16) for i in range(NPT)]; B_PT = [Buf() for _ in range(NPT)]
        gs = sb("gs", [128, 4 * NBK]); B_gs = Buf()
        m8 = sb("m8", [128, 4 * 8]); B_m8 = Buf()
        selb = sb("selb", [128, 4 * NBK]); B_selb = Buf()
        NP = NBH // 2
        selbTa = [sb("selbTa%d" % i, [128, NP * 512], BF16) for i in range(2)]; B_selbTa = [[Buf() for _ in range(NP)] for _ in range(2)]
        PaD = [sb("PaD%d" % i, [128, 1024]) for i in range(2)]; B_PaD = [Buf() for _ in range(2)]
        PaP = [sb("PaP%d" % i, [128, 1024]) for i in range(2)]; B_PaP = [Buf() for _ in range(2)]
        psm = [sb("psm%d" % i, [128, 512]) for i in range(3)]; B_psm = [Buf() for _ in range(3)]
        rcr = sb("rcr", [1, 512]); B_rcr = Buf()
        bcs = sb("bcs", [128, 512]); B_bcs = Buf()
        onesc = sb("onesc", [128, 1], BF16); onesr = sb("onesr", [1, 128], BF16)
        phl = [sb("phl%d" % i, [128, 512], BF16) for i in range(2)]; B_phl = Buf()
        rhl = [sb("rhl%d" % i, [1, 512], BF16) for i in range(2)]; B_rhl = Buf()
        pSTw = [psb("pSTw%d" % i, [128, 1024]) for i in range(2)]; B_pST = [Buf() for _ in range(2)]
        pacc_o = [psb("pacco%d" % i, [128, 512]) for i in range(2)]; B_pacc_o = [Buf() for _ in range(2)]
        pm1 = psb("pm1", [128, 512]); B_pm1 = Buf()
        pm2 = psb("pm2", [128, 512]); B_pm2 = Buf()
        pm2b = pm2[:, :].bitcast(BF16)

        ldc = lambda o_, i_: S.op("sp", lambda e: e.dma_start(out=o_, in_=i_), pwrites=[B_c2], dma=True)
        ldc(pastt[:, :], L["c_past"][:, :]); ldc(diagt[:, :], L["c_diag"][:, :]); ldc(maskc[:, :], L["c_mask"][:, :]); ldc(oneh[:, :], L["c_onehot"][:, :])
        S.op("dve", lambda e: e.tensor_copy(out=kmb[:, :], in_=kmT[:, :]), reads=[B_kmT], writes=[B_kmb])
        for i in range(2):
            for p_ in range(NP):
                S.op("pool", lambda e, i=i, p_=p_: e.memset(selbTa[i][:, p_ * 512:(p_ + 1) * 512], 0.0), writes=[B_selbTa[i][p_]])
        S.op("pool", lambda e: e.memset(onesc[:, :], 1.0), pwrites=[B_c2])
        S.op("pool", lambda e: e.memset(onesr[:, :], 1.0), pwrites=[B_c2])

        def head_loads(h):
            hb = h % 2
            S.op("sp", lambda e: e.dma_start(out=qTh[hb][:, :], in_=L["qT_s"][h, :, :]), reads=[B_scr["qT"]], writes=[B_qTh[hb]], dma=True)
            S.op("sp", lambda e: e.dma_start(out=kTh[hb][:, :], in_=L["kT_s"][h, :, :]), reads=[B_scr["kT"]], writes=[B_kTh[hb]], dma=True)
            nvs = max(1, (2 * NT) // 16)
            for vs in range(nvs):
                t0_, t1_ = vs * (2 * NT // nvs), (vs + 1) * (2 * NT // nvs)
                S.op("sp", lambda e, t0_=t0_, t1_=t1_: e.dma_start(out=V1[hb][:, t0_:t1_, 0:128],
                                                                 in_=L["v_s"][t0_ * 128:t1_ * 128, h * 128:(h + 1) * 128].rearrange("(t p) d -> p t d", p=128)),
                     reads=[B_scr["v"]], writes=[B_V1[hb]] if vs == 0 else [], pwrites=[] if vs == 0 else [B_V1[hb]], dma=True)

        def sel_steps(h):
            hb = h % 2
            steps = []
            row = NBH * NBK
            v4 = lambda t_: bass.AP(t_, 0, [[4 * NBK, 128], [2 * NBK, 2], [NBK, 2], [1, NBK]])
            g3 = lambda t_: bass.AP(t_, 0, [[4 * NBK, 128], [NBK, 4], [1, NBK]])
            pm1v = bass.AP(pm1, 0, [[512, 128], [2 * NBK, 2], [NBK, 2], [1, NBK]])
            thr = bass.AP(m8, 2, [[32, 128], [8, 4], [0, NBK]])
            for p in range(NP):
                pastv = bass.AP(pastt, 2 * p * NBK, [[row, 128], [NBK, 2], [0, 2], [1, NBK]])
                diagv = bass.AP(diagt, 2 * p * NBK, [[row, 128], [NBK, 2], [0, 2], [1, NBK]])

                def s1(p=p):
                    for qt in range(4):
                        tq0 = p * 512 + qt * 128
                        S.op("pe", lambda e, qt=qt, tq0=tq0: e.matmul(pm1[:, qt * NBK:(qt + 1) * NBK], lhsT=qTh[hb][:, tq0:tq0 + 128], rhs=kmb[:, h * NBK:(h + 1) * NBK],
                                                                  start=True, stop=True), reads=[B_qTh[hb], B_kmb], pwrites=[B_pm1])

                def s2(pastv=pastv):
                    S.op("dve", lambda e: e.tensor_tensor(out=v4(gs), in0=pm1v, in1=pastv, op=ALU.add), reads=[B_pm1, B_c2], writes=[B_gs])
                    for qt in range(4):
                        S.op("dve", lambda e, qt=qt: e.max(out=m8[:, qt * 8:(qt + 1) * 8], in_=gs[:, qt * NBK:(qt + 1) * NBK]), reads=[B_gs], pwrites=[B_m8])

                def s3(pastv=pastv, diagv=diagv):
                    S.op("dve", lambda e: e.tensor_tensor(out=g3(selb), in0=g3(gs), in1=thr, op=ALU.is_ge), reads=[B_gs, B_m8], writes=[B_selb])
                    S.op("dve", lambda e: e.tensor_scalar(out=selb[:, :], in0=selb[:, :], scalar1=BIG, scalar2=-BIG, op0=ALU.mult, op1=ALU.add), reads=[B_selb], writes=[B_selb])
                    S.op("dve", lambda e: e.tensor_tensor(out=v4(selb), in0=v4(selb), in1=pastv, op=ALU.min), reads=[B_selb, B_c2], writes=[B_selb])
                    S.op("dve", lambda e: e.tensor_tensor(out=v4(selb), in0=v4(selb), in1=diagv, op=ALU.max), reads=[B_selb, B_c2], writes=[B_selb])

                def s4():
                    for qt in range(4):
                        S.op("pe", lambda e, qt=qt: e.transpose(out=pm2[0:NBK, qt * 128:(qt + 1) * 128], in_=selb[:, qt * NBK:(qt + 1) * NBK], identity=identf[:, :]),
                             reads=[B_selb, B_const], pwrites=[B_pm2])

                def s5(p=p):
                    S.op("dve", lambda e: e.tensor_copy(out=selbTa[hb][0:NBK, p * 512:(p + 1) * 512], in_=pm2[0:NBK, :]), reads=[B_pm2], writes=[B_selbTa[hb][p]])

                def s45(s4=s4, s5=s5):
                    s4(); s5()
                steps += [s1, s2, s3, s45]
            return steps

        U = []
        import os as _os
        HLIM = int(_os.environ.get("A_HEADS", H))
        for h in range(HLIM):
            hb = h % 2
            for p in range(NP):
                keys = [(kt, kt // 2, None) for kt in range(NT)]
                for ko in range(4 * p + 4):
                    keys.append((NT + ko, NBH + ko // 2, (ko - 4 * p) if ko >= 4 * p else None))
                nk2 = len(keys) // 2
                for u in range(nk2):
                    U.append(dict(h=h, hb=hb, p=p, keys=keys[2 * u:2 * u + 2], first=(u == 0), last=(u == nk2 - 1), idx=len(U), uinpair=u, pairidx=h * NP + p))

        def do_ST(un):
            h, hb, p = un["h"], un["hb"], un["p"]
            si = un["idx"] % 2; pj = un["idx"] % NPT
            for half, (ktile, n, pidx) in enumerate(un["keys"]):
                osl = slice(half * 512, (half + 1) * 512)
                S.op("pe", lambda e, ktile=ktile, osl=osl: e.matmul(pSTw[si][:, osl], lhsT=kTh[hb][:, ktile * 128:(ktile + 1) * 128], rhs=qTh[hb][:, p * 512:(p + 1) * 512],
                                                                start=True, stop=False), reads=[B_kTh[hb], B_qTh[hb]], pwrites=[B_pST[si]])
                S.op("pe", lambda e, n=n, osl=osl: e.matmul(pSTw[si][:, osl], lhsT=oneh[:, n * 128:(n + 1) * 128], rhs=selbTa[hb][:, p * 512:(p + 1) * 512], start=False, stop=True),
                     reads=[B_c2, B_selbTa[hb][p]], pwrites=[B_pST[si]])
            S.op("act", lambda e: e.activation(out=PT2[pj][:, :], in_=pSTw[si][:, :], func=AF.Exp, scale=scale), reads=[B_pST[si]], writes=[B_PT[pj]])
            for half, (ktile, n, pidx) in enumerate(un["keys"]):
                if pidx is not None:
                    osl = slice(half * 512, (half + 1) * 512)
                    S.op("pool", lambda e, osl=osl, pidx=pidx: e.tensor_tensor(out=PT2[pj][:, osl], in0=PT2[pj][:, osl], in1=maskc[:, pidx * 512:(pidx + 1) * 512], op=ALU.mult),
                         reads=[B_PT[pj], B_c2], writes=[B_PT[pj]])

        deferred = {}

        def do_PV(un):
            h, hb, p = un["h"], un["hb"], un["p"]
            pj = un["idx"] % NPT
            pb = un["pairidx"] % 2
            for half, (ktile, n, pidx) in enumerate(un["keys"]):
                S.op("pe", lambda e, half=half, ktile=ktile: e.matmul(pacc_o[pb][:, :], lhsT=V1[hb][:, ktile, 0:128], rhs=PT2[pj][:, half * 512:(half + 1) * 512],
                                                                start=(un["first"] and half == 0), stop=(un["last"] and half == 1)),
                     reads=[B_PT[pj], B_V1[hb]], pwrites=[B_pacc_o[pb]])
            u = un["uinpair"]
            eng, Pa, BPa = ("dve", PaD[pb], B_PaD[pb]) if u % 2 == 0 else ("pool", PaP[pb], B_PaP[pb])
            if u < 2:
                S.op(eng, lambda e: e.tensor_copy(out=Pa[:, :], in_=PT2[pj][:, :]), reads=[B_PT[pj]], writes=[BPa])
            else:
                S.op(eng, lambda e: e.tensor_tensor(out=Pa[:, :], in0=Pa[:, :], in1=PT2[pj][:, :], op=ALU.add), reads=[B_PT[pj], BPa], writes=[BPa])
            if un["last"]:
                def ep1(pb=pb):
                    S.op("dve", lambda e: e.tensor_tensor(out=psm[0][:, :], in0=PaD[pb][:, 0:512], in1=PaD[pb][:, 512:1024], op=ALU.add), reads=[B_PaD[pb]], writes=[B_psm[0]])
                    S.op("pool", lambda e: e.tensor_tensor(out=psm[1][:, :], in0=PaP[pb][:, 0:512], in1=PaP[pb][:, 512:1024], op=ALU.add), reads=[B_PaP[pb]], writes=[B_psm[1]])
                    S.op("dve", lambda e: e.tensor_tensor(out=psm[2][:, :], in0=psm[0][:, :], in1=psm[1][:, :], op=ALU.add), reads=[B_psm[0], B_psm[1]], writes=[B_psm[2]])
                    S.op("dve", lambda e: e.tensor_copy(out=phl[0][:, :], in_=psm[2][:, :]), reads=[B_psm[2]], writes=[B_phl])
                    S.op("dve", lambda e: e.tensor_tensor(out=phl[1][:, :], in0=psm[2][:, :], in1=phl[0][:, :], op=ALU.subtract), reads=[B_psm[2], B_phl], pwrites=[B_phl])

                def ep2(pb=pb, hb=hb, p=p, h=h):
                    S.op("pe", lambda e: e.matmul(pm2[0:1, 0:512], lhsT=onesc[:, 0:1], rhs=phl[0][:, :], start=True, stop=False), reads=[B_phl, B_c2], pwrites=[B_pm2])
                    S.op("pe", lambda e: e.matmul(pm2[0:1, 0:512], lhsT=onesc[:, 0:1], rhs=phl[1][:, :], start=False, stop=True), reads=[B_phl, B_c2], pwrites=[B_pm2])
                    S.op("dve", lambda e: e.reciprocal(out=rcr[:, :], in_=pm2[0:1, 0:512]), reads=[B_pm2], writes=[B_rcr])
                    S.op("dve", lambda e: e.tensor_copy(out=rhl[0][:, :], in_=rcr[:, :]), reads=[B_rcr], writes=[B_rhl])
                    S.op("dve", lambda e: e.tensor_tensor(out=rhl[1][:, :], in0=rcr[:, :], in1=rhl[0][:, :], op=ALU.subtract), reads=[B_rcr, B_rhl], pwrites=[B_rhl])
                    S.op("pe", lambda e: e.matmul(pm2[:, 0:512], lhsT=onesr[0:1, :], rhs=rhl[0][0:1, :], start=True, stop=False), reads=[B_rhl, B_c2], pwrites=[B_pm2])
                    S.op("pe", lambda e: e.matmul(pm2[:, 0:512], lhsT=onesr[0:1, :], rhs=rhl[1][0:1, :], start=False, stop=True), reads=[B_rhl, B_c2], pwrites=[B_pm2])
                    S.op("act", lambda e: e.activation(out=bcs[:, :], in_=pm2[:, 0:512], func=AF.Copy), reads=[B_pm2], writes=[B_bcs])
                    S.op("dve", lambda e: e.tensor_tensor(out=aTst[hb][:, p * 512:(p + 1) * 512], in0=pacc_o[pb][:, :], in1=bcs[:, :], op=ALU.mult),
                         reads=[B_pacc_o[pb], B_bcs], pwrites=[B_aTst[hb]])
                    if p == NP - 1:
                        S.op("sp", lambda e: e.dma_start(out=L["mixT_s"][h * 128:(h + 1) * 128, :], in_=aTst[hb][:, :]), reads=[B_aTst[hb]], pwrites=[B_scr["mixT"]], dma=True)
                deferred.setdefault(un["idx"] + 2, []).append(ep1)
                deferred.setdefault(un["idx"] + 4, []).append(ep2)

        head_loads(0)
        for stp in sel_steps(0):
            stp()
        pending = []
        per_head = len(U) // HLIM
        cast_stride = max(1, (len(U) * 3 // 4) // max(1, len(cast_jobs)))
        LOOK = int(_os.environ.get("LOOK", "2"))
        pos_in_head = 0
        stride = 1

        def after_pv(j):
            nonlocal pending, pos_in_head, stride
            for fn in deferred.pop(j, []):
                fn()

        def enter_head(hcur):
            nonlocal pending, pos_in_head, stride
            if hcur + 1 < HLIM:
                head_loads(hcur + 1)
                pending = sel_steps(hcur + 1)
                stride = max(1, (per_head - 8) // max(1, len(pending)))
                pos_in_head = 0

        for i, un in enumerate(U):
            if cast_jobs and i % cast_stride == 0:
                cast_jobs.pop(0)()
            do_ST(un)
            j = i - LOOK
            if j >= 0:
                do_PV(U[j])
                after_pv(j)
            if i == 0:
                enter_head(0)
            elif j >= 0 and U[j]["last"] and U[j]["p"] == NP - 1:
                enter_head(U[j]["h"] + 1)
            if pending:
                pos_in_head += 1
                nxt_new_head = (i + 1 < len(U)) and (U[i + 1]["h"] != un["h"])
                flush = nxt_new_head or (i + 1 == len(U))
                if pos_in_head % stride == 0 or flush:
                    nstep = len(pending) if flush else 1
                    for _ in range(nstep):
                        pending.pop(0)()
        for j in range(max(0, len(U) - LOOK), len(U)):
            do_PV(U[j])
            after_pv(j)
        for k_ in sorted(deferred):
            for fn in deferred[k_]:
                fn()
        while cast_jobs:
            cast_jobs.pop(0)()
        st2a = S.emit()
    print("phase2a", st2a, flush=True)
    import os as _os
    if _os.environ.get("STOP_AFTER") == "2":
        return

    with contextlib.ExitStack() as st:
        sb = lambda n, s, d=F32: st.enter_context(nc.sbuf_tensor(n, list(s), d))
        psb = lambda n, s, d=F32: st.enter_context(nc.psum_tensor(n, list(s), d))
        rq4 = [sb("rq4_%d" % i, [128, H, 512], BF16) for i in range(2)]; B_rq4 = [Buf() for _ in range(2)]
        rk4 = [sb("rk4_%d" % i, [128, H, 512], BF16) for i in range(2)]; B_rk4 = [Buf() for _ in range(2)]
        NKB = 3
        kd = [sb("kd%d" % i, [128, H * HD], BF16) for i in range(NKB)]; B_kd = [Buf() for _ in range(NKB)]
        rvt = [sb("rvt%d" % i, [128, H * HD], BF16) for i in range(NKB)]; B_rvt = [Buf() for _ in range(NKB)]
        NRG = 4
        rgt = [sb("rgt%d" % i, [128, H * HD], BF16) for i in range(NRG)]; B_rgt = [Buf() for _ in range(NRG)]
        decT = sb("decT", [128, H * 128]); grt = sb("grt", [128, H * HD]); B_c3 = Buf()
        Sb = sb("Sb", [128, H * HD], BF16); B_Sb = Buf()
        sTm = [sb("sTm%d" % i, [128, H, 128], BF16) for i in range(2)]; B_sTm = [Buf() for _ in range(2)]
        NOS = 3
        osb_ = [sb("osb%d" % i, [128, H * HD]) for i in range(NOS)]; B_osb_ = [Buf() for _ in range(NOS)]
        sq_ = [sb("sq2_%d" % i, [128, H * HD]) for i in range(2)]; B_sq_ = [Buf() for _ in range(2)]
        ssr_ = [sb("ssr%d" % i, [128, 16]) for i in range(2)]; B_ssr_ = [Buf() for _ in range(2)]
        rob = [sb("rob%d" % i, [128, H * HD], BF16) for i in range(NOS)]; B_rob = [Buf() for _ in range(NOS)]
        rTst = [sb("rTst%d" % i, [128, H, 512], BF16) for i in range(2)]; B_rTst = [Buf() for _ in range(2)]
        psT = [psb("psT%d" % i, [128, 512]) for i in range(2)]; B_psT = [Buf() for _ in range(2)]
        pO = [psb("pO%d" % i, [128, 512]) for i in range(2)]; B_pO = [Buf() for _ in range(2)]
        pU = [psb("pU2_%d" % i, [128, 512]) for i in range(2)]; B_pU = [Buf() for _ in range(2)]
        ptr = psb("ptr2", [128, H, 128], BF16); B_ptr = Buf()
        S.op("sp", lambda e: e.dma_start(out=decT[:, :], in_=L["c_decT"][:, :]), pwrites=[B_c3], dma=True)
        S.op("sp", lambda e: e.dma_start(out=grt[:, :], in_=L["g_ret"].ap().partition_broadcast(128)), pwrites=[B_c3], dma=True)

        def LD(t):
            g, tt = t // 4, t % 4
            gb = g % 2
            if tt == 0:
                S.op("sp", lambda e: e.dma_start(out=rq4[gb][:, :, :], in_=L["rqT_s"][:, :, g * 512:(g + 1) * 512].rearrange("h d t -> d h t")),
                     reads=[B_scr["rqT"]], writes=[B_rq4[gb]], dma=True)
                S.op("sp", lambda e: e.dma_start(out=rk4[gb][:, :, :], in_=L["rkT_s"][:, :, g * 512:(g + 1) * 512].rearrange("h d t -> d h t")),
                     reads=[B_scr["rkT"]], writes=[B_rk4[gb]], dma=True)
            for dst, Bd, src, nm, nb in ((kd, B_kd, "rkd_s", "rkd", NKB), (rvt, B_rvt, "rv_s", "rv", NKB), (rgt, B_rgt, "rgs_s", "rgs", NRG)):
                S.op("sp", lambda e, dst=dst, src=src, nb=nb: e.dma_start(out=dst[t % nb][:, :], in_=L[src][t * 128:(t + 1) * 128, :]),
                     reads=[B_scr[nm]], writes=[Bd[t % nb]], dma=True)

        def M1(t):
            g, tt = t // 4, t % 4
            gb = g % 2; tb = t % 2
            ts_ = slice(tt * 128, (tt + 1) * 128)
            for hh in range(H):
                bk, col = hh // 4, (hh % 4) * 128
                S.op("pe", lambda e, hh=hh, bk=bk, col=col: e.matmul(psT[bk][:, col:col + 128], lhsT=rk4[gb][:, hh, ts_], rhs=rq4[gb][:, hh, ts_], start=True, stop=True),
                     reads=[B_rk4[gb], B_rq4[gb]], pwrites=[B_psT[bk]])
            for bk in range(2):
                S.op("dve", lambda e, bk=bk: e.tensor_tensor(out=sTm[tb][:, bk * 4:(bk + 1) * 4, :], in0=psT[bk][:, :].rearrange("p (h t) -> p h t", h=4),
                                                        in1=decT[:, bk * 512:(bk + 1) * 512].rearrange("p (h t) -> p h t", h=4), op=ALU.mult),
                     reads=[B_psT[bk], B_c3], pwrites=[B_sTm[tb]])

        def M2(t):
            g, tt = t // 4, t % 4
            gb = g % 2; tb = t % 2; kb = t % NKB; ob_ = t % NOS
            ts_ = slice(tt * 128, (tt + 1) * 128)
            S.op("pool", lambda e: e.tensor_copy(out=Sb[:, :], in_=Sst[:, :]), reads=[B_S], writes=[B_Sb])
            for hh in range(H):
                bk, col = hh // 4, (hh % 4) * 128
                S.op("pe", lambda e, hh=hh, bk=bk, col=col: e.matmul(pO[bk][:, col:col + 128], lhsT=sTm[tb][:, hh, :], rhs=rvt[kb][:, hh * 128:(hh + 1) * 128], start=True, stop=False),
                     reads=[B_sTm[tb], B_rvt[kb]], pwrites=[B_pO[bk]])
                S.op("pe", lambda e, hh=hh, bk=bk, col=col: e.matmul(pO[bk][:, col:col + 128], lhsT=rq4[gb][:, hh, ts_], rhs=Sb[:, hh * 128:(hh + 1) * 128], start=False, stop=True),
                     reads=[B_rq4[gb], B_Sb], pwrites=[B_pO[bk]])
            for bk in range(2):
                S.op("dve", lambda e, bk=bk: e.tensor_tensor(out=osb_[ob_][:, bk * 512:(bk + 1) * 512].rearrange("p (h e) -> p h e", h=4), in0=pO[bk][:, :].rearrange("p (h e) -> p h e", h=4),
                                                        in1=bc_in(qdec_t, H, bk * 4, 4, 128), op=ALU.mult),
                     reads=[B_pO[bk], B_const], pwrites=[B_osb_[ob_]])

        def UU(t):
            kb = t % NKB
            for hh in range(H):
                bk, col = hh // 4, (hh % 4) * 128
                S.op("pe", lambda e, hh=hh, bk=bk, col=col: e.matmul(pU[bk][:, col:col + 128], lhsT=kd[kb][:, hh * 128:(hh + 1) * 128], rhs=rvt[kb][:, hh * 128:(hh + 1) * 128], start=True, stop=True),
                     reads=[B_kd[kb], B_rvt[kb]], pwrites=[B_pU[bk]])
            s3 = Sst[:, :].rearrange("p (h e) -> p h e", h=H)
            S.op("pool", lambda e: e.tensor_tensor(out=s3, in0=s3, in1=bc_in(gch_t, H, 0, 8, 128), op=ALU.mult), reads=[B_S, B_const], writes=[B_S])
            for bk in range(2):
                S.op("dve", lambda e, bk=bk: e.tensor_tensor(out=Sst[:, bk * 512:(bk + 1) * 512], in0=Sst[:, bk * 512:(bk + 1) * 512], in1=pU[bk][:, :], op=ALU.add),
                     reads=[B_S, B_pU[bk]], writes=[B_S])

        def NN(t):
            ob_ = t % NOS; sb_i = t % 2; rg_i = t % NRG
            osb, sq, ssr = osb_[ob_], sq_[sb_i], ssr_[sb_i]
            B_osb, B_sq, B_ssr = B_osb_[ob_], B_sq_[sb_i], B_ssr_[sb_i]
            S.op("pool", lambda e: e.tensor_tensor(out=sq[:, :], in0=osb[:, :], in1=osb[:, :], op=ALU.mult), reads=[B_osb], writes=[B_sq])
            S.op("dve", lambda e: e.tensor_reduce(out=ssr[:, 0:8], in_=sq[:, :].rearrange("p (h e) -> p h e", h=H), axis=AX.X, op=ALU.add), reads=[B_sq], writes=[B_ssr])
            S.op("pool", lambda e: e.tensor_scalar(out=ssr[:, 0:8], in0=ssr[:, 0:8], scalar1=1.0 / HD, scalar2=EPS, op0=ALU.mult, op1=ALU.add), reads=[B_ssr], writes=[B_ssr])
            S.op("pool", lambda e: e.tensor_tensor(out=ssr[:, 8:16], in0=ssr[:, 0:8], in1=mhalf[:, 0:8], op=ALU.pow), reads=[B_ssr, B_const], writes=[B_ssr])
            o3 = osb[:, :].rearrange("p (h e) -> p h e", h=H)
            S.op("dve", lambda e: e.tensor_tensor(out=o3, in0=o3, in1=bc_in(ssr, 16, 8, 8, 128), op=ALU.mult), reads=[B_osb, B_ssr], writes=[B_osb])
            S.op("pool", lambda e: e.tensor_tensor(out=osb[:, :], in0=osb[:, :], in1=grt[:, :], op=ALU.mult), reads=[B_osb, B_c3], writes=[B_osb])
            S.op("dve", lambda e: e.tensor_tensor(out=rob[ob_][:, :], in0=osb[:, :], in1=rgt[rg_i][:, :], op=ALU.mult), reads=[B_osb, B_rgt[rg_i]], writes=[B_rob[ob_]])

        def TT(t):
            g, tt = t // 4, t % 4
            gb = g % 2; ob_ = t % NOS
            ts_ = slice(tt * 128, (tt + 1) * 128)
            for hh in range(H):
                S.op("pe", lambda e, hh=hh: e.transpose(out=ptr[:, hh, :], in_=rob[ob_][:, hh * 128:(hh + 1) * 128], identity=identb[:, :]),
                     reads=[B_rob[ob_], B_const], pwrites=[B_ptr])
            S.op("act", lambda e: e.activation(out=rTst[gb][:, :, ts_], in_=ptr[:, :, :], func=AF.Copy), reads=[B_ptr], pwrites=[B_rTst[gb]])
            if tt == 3:
                S.op("sp", lambda e: e.dma_start(out=L["mixT_s"][H * HD:2 * H * HD, g * 512:(g + 1) * 512].rearrange("(h e) t -> e h t", h=H), in_=rTst[gb][:, :, :]),
                     reads=[B_rTst[gb]], pwrites=[B_scr["mixT"]], dma=True)

        NTL = NT
        LD(0)
        M1(0)
        for i in range(NTL + 2):
            if i + 1 < NTL:
                LD(i + 1)
                M1(i + 1)
            if i < NTL:
                M2(i)
                UU(i)
            if 0 <= i - 1 < NTL:
                NN(i - 1)
            if 0 <= i - 2 < NTL:
                TT(i - 2)
        if debug:
            S.op("sp", lambda e: e.dma_start(out=dbg["d_mixT"][:, :], in_=L["mixT_s"][:, :]), reads=[B_scr["mixT"]], dma=True)
        st2b = S.emit()
    print("phase2b", st2b, flush=True)

    with contextlib.ExitStack() as st:
        sb = lambda n, s, d=F32: st.enter_context(nc.sbuf_tensor(n, list(s), d))
        psb = lambda n, s, d=F32: st.enter_context(nc.psum_tensor(n, list(s), d))
        xres = sb("xres", [128, 4, D]); B_xres = [Buf() for _ in range(4)]
        mT = sb("mT", [128, KD, 512], BF16); B_mT = Buf()
        h2T = sb("h2T", [128, KD, 512], BF16); B_h2T = Buf()
        HK = KF // 2
        actT = sb("actT", [128, HK, 512], BF16); B_actT = Buf()
        ws = [sb("ws%d" % i, [128, KD, 256], BF16) for i in range(4)]; B_ws = [Buf() for _ in range(4)]
        wd = [sb("wd%d" % i, [128, 11, 512], BF16) for i in range(2)]; B_wd = [Buf() for _ in range(2)]
        wpp = sb("wpp", [128, 2, D], BF16); bpg_t = sb("bpg_t", [128, D]); B_c4 = Buf()
        pT = sb("pT", [128, 2, 512], BF16); B_pT = Buf()
        pin = [sb("pin%d" % i, [128, PLE]) for i in range(2)]; B_pin = [Buf() for _ in range(2)]
        pinb = [sb("pinb%d" % i, [128, PLE], BF16) for i in range(2)]; B_pinb = [Buf() for _ in range(2)]
        stmp = [sb("stmp%d" % i, [128, 512]) for i in range(2)]; B_stmp = [Buf() for _ in range(2)]
        gtmp = [sb("gtmp%d" % i, [128, 256]) for i in range(2)]; B_gtmp = [Buf() for _ in range(2)]
        xn = sb("xn3", [128, D], BF16); B_xn = Buf()
        st8 = [sb("st8b_%d" % i, [128, 8]) for i in range(2)]; B_st8 = [Buf() for _ in range(2)]
        pb = [psb("pb%d" % i, [128, 512]) for i in range(8)]; B_pb = [Buf() for _ in range(8)]
        pbb = [pb[i][:, :].bitcast(BF16) for i in range(8)]
        S.op("sp", lambda e: e.dma_start(out=wpp[:, :, :], in_=L["wb_pp"].ap().rearrange("(k p) n -> p k n", p=128)), reads=[B_w["pp"]], pwrites=[B_c4], dma=True)
        S.op("sp", lambda e: e.dma_start(out=bpg_t[:, :], in_=L["b_pg"].ap().partition_broadcast(128)), pwrites=[B_c4], dma=True)
        c3 = {"ws": 0, "wd": 0, "a": 0, "t": 0, "n": 0}

        def norm_T(gtab, dstT, B_dst):
            for tt in range(4):
                si = c3["n"] % 2; c3["n"] += 1
                S.op("act", lambda e, tt=tt, si=si: e.activation(out=xn[:, :], in_=xres[:, tt, :], func=AF.Square, accum_out=st8[si][:, 0:1]),
                     reads=[B_xres[tt]], writes=[B_xn, B_st8[si]])
                S.op("pool", lambda e, si=si: e.tensor_scalar(out=st8[si][:, 1:2], in0=st8[si][:, 0:1], scalar1=1.0 / D, scalar2=EPS, op0=ALU.mult, op1=ALU.add),
                     reads=[B_st8[si]], writes=[B_st8[si]])
                S.op("pool", lambda e, si=si: e.tensor_tensor(out=st8[si][:, 2:3], in0=st8[si][:, 1:2], in1=mhalf[:, 0:1], op=ALU.pow), reads=[B_st8[si], B_const], writes=[B_st8[si]])
                S.op("act", lambda e, tt=tt, si=si: e.activation(out=xn[:, :], in_=xres[:, tt, :], func=AF.Copy, scale=st8[si][:, 2:3]), reads=[B_xres[tt], B_st8[si]], writes=[B_xn])
                for k4 in range(4):
                    bi = 4 + (c3["t"] % 2); c3["t"] += 1
                    for kk in range(4):
                        k = k4 * 4 + kk
                        S.op("pe", lambda e, k=k, kk=kk, bi=bi: e.transpose(out=pbb[bi][:, kk * 128:(kk + 1) * 128], in_=xn[:, k * 128:(k + 1) * 128], identity=identb[:, :]),
                             reads=[B_xn, B_const], pwrites=[B_pb[bi]])
                    for kk in range(4):
                        k = k4 * 4 + kk
                        S.op("dve", lambda e, k=k, kk=kk, bi=bi, tt=tt: e.tensor_scalar(out=dstT[:, k, tt * 128:(tt + 1) * 128], in0=pbb[bi][:, kk * 128:(kk + 1) * 128],
                                                                                     scalar1=gtab[:, k:k + 1], scalar2=None, op0=ALU.mult),
                             reads=[B_pb[bi], B_const], pwrites=[B_dst])

        for g in range(NG):
            for tt in range(4):
                S.op("sp", lambda e, g=g, tt=tt: e.dma_start(out=xres[:, tt, :], in_=L["x_own"][(g * 4 + tt) * 128:(g * 4 + tt + 1) * 128, :]), writes=[B_xres[tt]], dma=True)
            S.op("sp", lambda e, g=g: e.dma_start(out=mT[:, :, :], in_=L["mixT_s"][:, g * 512:(g + 1) * 512].rearrange("(k p) t -> p k t", p=128)),
                 reads=[B_scr["mixT"]], writes=[B_mT], dma=True)
            for c in range(8):
                wi = c3["ws"] % 4; c3["ws"] += 1
                S.op("sp", lambda e, c=c, wi=wi: e.dma_start(out=ws[wi][:, :, :], in_=L["wb_o"].ap()[:, c * 256:(c + 1) * 256].rearrange("(k p) n -> p k n", p=128)),
                     reads=[B_w["o"]], writes=[B_ws[wi]], dma=True)
                for tt in range(4):
                    ai = c3["a"] % 4; c3["a"] += 1
                    for k in range(KD):
                        S.op("pe", lambda e, k=k, tt=tt, wi=wi, ai=ai: e.matmul(pb[ai][:, 0:256], lhsT=mT[:, k, tt * 128:(tt + 1) * 128], rhs=ws[wi][:, k, :], start=(k == 0), stop=(k == KD - 1)),
                             reads=[B_mT, B_ws[wi]], pwrites=[B_pb[ai]])
                    S.op("dve", lambda e, tt=tt, c=c, ai=ai: e.tensor_tensor(out=xres[:, tt, c * 256:(c + 1) * 256], in0=xres[:, tt, c * 256:(c + 1) * 256], in1=pb[ai][:, 0:256], op=ALU.add),
                         reads=[B_pb[ai], B_xres[tt]], writes=[B_xres[tt]])
            norm_T(gffn_t, h2T, B_h2T)
            for half in range(2):
                for js in range(HK // 2):
                    col0 = (half * HK + js * 2) * 128
                    wg_i = c3["ws"] % 4; c3["ws"] += 1
                    wu_i = c3["ws"] % 4; c3["ws"] += 1
                    S.op("sp", lambda e, col0=col0, wg_i=wg_i: e.dma_start(out=ws[wg_i][:, :, :], in_=L["wb_g"].ap()[:, col0:col0 + 256].rearrange("(k p) n -> p k n", p=128)),
                         reads=[B_w["g"]], writes=[B_ws[wg_i]], dma=True)
                    S.op("sp", lambda e, col0=col0, wu_i=wu_i: e.dma_start(out=ws[wu_i][:, :, :], in_=L["wb_u"].ap()[:, col0:col0 + 256].rearrange("(k p) n -> p k n", p=128)),
                         reads=[B_w["u"]], writes=[B_ws[wu_i]], dma=True)
                    for jj in range(2):
                        jl = js * 2 + jj
                        gi = (jl % 2) * 2; ui = gi + 1
                        for k in range(KD):
                            S.op("pe", lambda e, k=k, jj=jj, wg_i=wg_i, gi=gi: e.matmul(pb[gi][:, :], lhsT=ws[wg_i][:, k, jj * 128:(jj + 1) * 128], rhs=h2T[:, k, :], start=(k == 0), stop=(k == KD - 1)),
                                 reads=[B_ws[wg_i], B_h2T], pwrites=[B_pb[gi]])
                        for k in range(KD):
                            S.op("pe", lambda e, k=k, jj=jj, wu_i=wu_i, ui=ui: e.matmul(pb[ui][:, :], lhsT=ws[wu_i][:, k, jj * 128:(jj + 1) * 128], rhs=h2T[:, k, :], start=(k == 0), stop=(k == KD - 1)),
                                 reads=[B_ws[wu_i], B_h2T], pwrites=[B_pb[ui]])
                        sj = jl % 2
                        S.op("act", lambda e, gi=gi, sj=sj: e.activation(out=stmp[sj][:, :], in_=pb[gi][:, :], func=AF.Silu), reads=[B_pb[gi]], writes=[B_stmp[sj]])
                        S.op("dve", lambda e, ui=ui, sj=sj, jl=jl: e.tensor_tensor(out=actT[:, jl, :], in0=stmp[sj][:, :], in1=pb[ui][:, :], op=ALU.mult),
                             reads=[B_stmp[sj], B_pb[ui]], pwrites=[B_actT])
                for c in range(4):
                    for sub in range(2):
                        di = c3["wd"] % 2; c3["wd"] += 1
                        r0 = (half * HK + sub * 11) * 128
                        S.op("sp", lambda e, r0=r0, c=c, di=di: e.dma_start(out=wd[di][:, :, :], in_=L["wb_d"].ap()[r0:r0 + 11 * 128, c * 512:(c + 1) * 512].rearrange("(k p) n -> p k n", p=128)),
                             reads=[B_w["d"]], writes=[B_wd[di]], dma=True)
                        for tt in range(4):
                            for k in range(11):
                                S.op("pe", lambda e, k=k, tt=tt, sub=sub, di=di: e.matmul(pb[4 + tt][:, :], lhsT=actT[:, sub * 11 + k, tt * 128:(tt + 1) * 128], rhs=wd[di][:, k, :],
                                                                                         start=(sub == 0 and k == 0), stop=(sub == 1 and k == 10)),
                                     reads=[B_actT, B_wd[di]], pwrites=[B_pb[4 + tt]])
                    for tt in range(4):
                        S.op("dve", lambda e, tt=tt, c=c: e.tensor_tensor(out=xres[:, tt, c * 512:(c + 1) * 512], in0=xres[:, tt, c * 512:(c + 1) * 512], in1=pb[4 + tt][:, :], op=ALU.add),
                             reads=[B_pb[4 + tt], B_xres[tt]], writes=[B_xres[tt]])
            norm_T(gple_t, mT, B_mT)
            for tt in range(4):
                pi = tt % 2
                S.op("sp", lambda e, g=g, tt=tt, pi=pi: e.dma_start(out=pin[pi][:, :], in_=L["p_own"][(g * 4 + tt) * 128:(g * 4 + tt + 1) * 128, :]), writes=[B_pin[pi]], dma=True)
                S.op("pool", lambda e, pi=pi: e.tensor_copy(out=pinb[pi][:, :], in_=pin[pi][:, :]), reads=[B_pin[pi]], writes=[B_pinb[pi]])
                for kp in range(2):
                    S.op("pe", lambda e, kp=kp, pi=pi: e.transpose(out=pbb[4][:, kp * 128:(kp + 1) * 128], in_=pinb[pi][:, kp * 128:(kp + 1) * 128], identity=identb[:, :]),
                         reads=[B_pinb[pi], B_const], pwrites=[B_pb[4]])
                S.op("dve", lambda e, tt=tt: e.tensor_copy(out=pT[:, :, tt * 128:(tt + 1) * 128], in_=pbb[4][:, 0:256].rearrange("p (k t) -> p k t", k=2)), reads=[B_pb[4]], pwrites=[B_pT])
            for c in range(8):
                wi = c3["ws"] % 4; c3["ws"] += 1
                S.op("sp", lambda e, c=c, wi=wi: e.dma_start(out=ws[wi][:, :, :], in_=L["wb_pg"].ap()[:, c * 256:(c + 1) * 256].rearrange("(k p) n -> p k n", p=128)),
                     reads=[B_w["pg"]], writes=[B_ws[wi]], dma=True)
                for tt in range(4):
                    ai = (c3["a"] % 2) * 2; c3["a"] += 1
                    bi_ = ai + 1
                    for k in range(KD):
                        S.op("pe", lambda e, k=k, tt=tt, wi=wi, ai=ai: e.matmul(pb[ai][:, 0:256], lhsT=mT[:, k, tt * 128:(tt + 1) * 128], rhs=ws[wi][:, k, :], start=(k == 0), stop=(k == KD - 1)),
                             reads=[B_mT, B_ws[wi]], pwrites=[B_pb[ai]])
                    for kp in range(2):
                        S.op("pe", lambda e, kp=kp, tt=tt, c=c, bi_=bi_: e.matmul(pb[bi_][:, 0:256], lhsT=pT[:, kp, tt * 128:(tt + 1) * 128], rhs=wpp[:, kp, c * 256:(c + 1) * 256], start=(kp == 0), stop=(kp == 1)),
                             reads=[B_pT, B_c4], pwrites=[B_pb[bi_]])
                    gj = tt % 2
                    S.op("dve", lambda e, ai=ai, c=c, gj=gj: e.tensor_tensor(out=gtmp[gj][:, :], in0=pb[ai][:, 0:256], in1=bpg_t[:, c * 256:(c + 1) * 256], op=ALU.add),
                         reads=[B_pb[ai], B_c4], writes=[B_gtmp[gj]])
                    S.op("act", lambda e, gj=gj: e.activation(out=gtmp[gj][:, :], in_=gtmp[gj][:, :], func=AF.Sigmoid), reads=[B_gtmp[gj]], writes=[B_gtmp[gj]])
                    S.op("dve", lambda e, bi_=bi_, gj=gj: e.tensor_tensor(out=gtmp[gj][:, :], in0=pb[bi_][:, 0:256], in1=gtmp[gj][:, :], op=ALU.mult),
                         reads=[B_pb[bi_], B_gtmp[gj]], writes=[B_gtmp[gj]])
                    S.op("pool", lambda e, tt=tt, c=c, gj=gj: e.tensor_tensor(out=xres[:, tt, c * 256:(c + 1) * 256], in0=xres[:, tt, c * 256:(c + 1) * 256], in1=gtmp[gj][:, :], op=ALU.add),
                         reads=[B_gtmp[gj], B_xres[tt]], writes=[B_xres[tt]])
            for tt in range(4):
                S.op("sp", lambda e, g=g, tt=tt: e.dma_start(out=L["out"][(g * 4 + tt) * 128:(g * 4 + tt + 1) * 128, :], in_=xres[:, tt, :]), reads=[B_xres[tt]], dma=True)
        st3 = S.emit()
    print("phase3", st3, flush=True)


def _tables(NBH, s):
    TH = NBH * 256
    NBK = 2 * NBH
    f32 = np.float32
    scale = f32(1.0 / np.sqrt(HD))
    inv_a = np.power(f32(10000.0), -(np.arange(0, HD, 2, dtype=f32) / f32(HD))).astype(f32)
    inv_r = np.power(f32(10000.0), -np.linspace(0.0, 1.0, HD // 2, dtype=f32)).astype(f32)

    def rope_tab(pos, inv):
        ang = (pos.astype(f32)[:, None] * inv[None, :]).astype(f32)
        return np.concatenate([np.cos(ang), np.sin(ang)], axis=1).astype(f32)

    pos_pre = np.arange(TH)
    pos_own = s * TH + np.arange(TH)
    t = {}
    t["rope_a_pre"] = rope_tab(pos_pre, inv_a); t["rope_a_own"] = rope_tab(pos_own, inv_a)
    t["rope_r_pre"] = rope_tab(pos_pre, inv_r); t["rope_r_own"] = rope_tab(pos_own, inv_r)
    gam = (1.0 - np.power(2.0, -5.0 - np.arange(H, dtype=np.float64)))
    lg = np.log(gam)
    j = np.arange(128, dtype=np.float64)
    t["c_kdec"] = (float(scale) * np.exp(lg[None, :] * (127.0 - j)[:, None])).astype(f32)
    t["c_qdec"] = np.exp(lg[None, :] * (j + 1.0)[:, None]).astype(f32)
    t["c_gch"] = np.broadcast_to(np.exp(lg * 128.0)[None, :], (128, H)).astype(f32).copy()
    m = j[:, None, None]; tq = j[None, None, :]
    dec = float(scale) * np.exp(-lg[None, :, None] * (m + 1.0)) * (m <= tq)
    t["c_decT"] = dec.astype(f32).reshape(128, H * 128)
    past = np.full((NBH, NBK), NEGP, dtype=f32)
    diag = np.full((NBH, NBK), -3.0e38, dtype=f32)
    for i in range(NBH):
        if s == 1:
            past[i, :NBH] = 0.0
        past[i, NBH:NBH + i] = 0.0
        diag[i, NBH + i] = 0.0
    t["c_past"] = np.broadcast_to(past.reshape(1, -1), (128, NBH * NBK)).copy()
    t["c_diag"] = np.broadcast_to(diag.reshape(1, -1), (128, NBH * NBK)).copy()
    tk = np.arange(128)[:, None, None]; pp = np.arange(4)[None, :, None]; cc = np.arange(512)[None, None, :]
    t["c_mask"] = ((pp * 128 + tk) <= cc).astype(f32).reshape(128, 4 * 512).astype(ml_dtypes.bfloat16)
    oh = np.zeros((128, NBK, 128), dtype=f32)
    for n in range(NBK):
        oh[n, n, :] = 1.0
    t["c_onehot"] = oh.reshape(128, NBK * 128).astype(ml_dtypes.bfloat16)
    t["c_identb"] = np.eye(128, dtype=f32).astype(ml_dtypes.bfloat16)
    t["c_identf"] = np.eye(128, dtype=f32)
    return t


def make_in_maps(inputs, NBH, n_batch):
    TH = NBH * 256
    x = np.asarray(inputs["x"], dtype=np.float32)
    p = np.asarray(inputs["p"], dtype=np.float32)
    shared = {
        "w_in": inputs["w_in"][0], "w_o": inputs["w_o"][0], "w_gate": inputs["w_gate"][0], "w_up": inputs["w_up"][0],
        "w_down": inputs["w_down"][0], "w_pg": inputs["w_ple_gate"][0], "w_pp": inputs["w_ple_proj"][0],
        "g_mix": inputs["g_mix"][0], "g_ffn": inputs["g_ffn"][0], "g_ple": inputs["g_ple"][0],
        "q_norm": inputs["q_norm"][0], "k_norm": inputs["k_norm"][0], "g_ret": inputs["g_ret"][0],
        "b_pg": inputs["b_ple_gate"][0],
    }
    shared = {k: np.ascontiguousarray(np.asarray(v, dtype=np.float32)) for k, v in shared.items()}
    tabs = [_tables(NBH, 0), _tables(NBH, 1)]
    maps = []
    for c in range(2 * n_batch):
        b, s = c // 2, c % 2
        m = dict(shared)
        m.update(tabs[s])
        m["x_own"] = np.ascontiguousarray(x[b, s * TH:(s + 1) * TH])
        m["x_pre"] = np.ascontiguousarray(x[b, 0:TH]) if s == 1 else np.zeros((TH, D), np.float32)
        m["p_own"] = np.ascontiguousarray(p[0, b, s * TH:(s + 1) * TH])
        maps.append(m)
    return maps


_NC_CACHE = {}


def kernel(**inputs):
    x = np.asarray(inputs["x"])
    B, T, _ = x.shape
    NBH = T // 512
    key = (NBH,)
    if key not in _NC_CACHE:
        _NC_CACHE[key] = build(NBH)
    nc = _NC_CACHE[key]
    maps = make_in_maps(inputs, NBH, B)
    res = run_bass_kernel_spmd(nc, maps, core_ids=list(range(2 * B)))
    TH = NBH * 256
    out = np.empty((B, T, D), np.float32)
    for c in range(2 * B):
        b, s = c // 2, c % 2
        out[b, s * TH:(s + 1) * TH] = np.asarray(res.results[c]["out"], dtype=np.float32)
    return out
```

```python
import numpy as np
import contextlib
import ml_dtypes
import concourse.bass as bass
import concourse.mybir as mybir
from concourse.bass_utils import run_bass_kernel_spmd
from concourse.alu_op_type import AluOpType as ALU

F32 = mybir.dt.float32
BF16 = mybir.dt.bfloat16
AF = mybir.ActivationFunctionType
AX = mybir.AxisListType

D = 2048
KD = 16
HD = 128
H = 8
INC = 7168
DFF = 5632
KF = 44
PLE = 256
EPS = 1e-6
BIG = 30000.0
NEGP = -1.0e9

N_DMA_SEMS = 32
N_SW_SEMS = 8
ENGS = ("pe", "act", "dve", "pool", "sp")


class Buf:
    __slots__ = ("w", "r", "pr", "name")

    def __init__(self, name=""):
        self.w = {}
        self.r = {}
        self.pr = set()
        self.name = name


class Op:
    __slots__ = ("eng", "fn", "deps", "idx", "pos", "inc", "tick", "is_dma", "dsem", "dval")


class Sched:
    def __init__(self, nc, stack):
        self.nc = nc
        self.sem = {e: stack.enter_context(nc.semaphore("s_" + e)) for e in ENGS if e != "sp"}
        self.dsems = [stack.enter_context(nc.semaphore("d%d" % i)) for i in range(N_DMA_SEMS)]
        self.drr_sw = 0
        self.tickbase = {e: 0 for e in self.sem}
        self.dcount = [0] * N_DMA_SEMS
        self.drr = 0
        self.waited = {e: {} for e in ENGS}
        self.ops = []
        self.pos = {e: 0 for e in ENGS}
        self.phase_dma = {}
        self.touched = set()
        self.lastop = {}

    def op(self, eng, fn, reads=(), writes=(), pwrites=(), dma=False):
        o = Op()
        o.eng = eng
        o.fn = fn
        o.idx = len(self.ops)
        o.is_dma = dma
        o.inc = False
        o.tick = 0
        o.pos = self.pos[eng]
        self.pos[eng] += 1
        deps = set()
        for b in reads:
            deps.update(b.w.values())
        for b in writes:
            deps.update(b.r.values())
            deps.update(b.w.values())
        for b in pwrites:
            deps.update(b.r.values())
            if b.r:
                deps.update(b.w.values())
            else:
                deps.update(b.pr)
        deps.discard(o.idx)
        o.deps = deps
        if dma:
            if eng == "pool":
                s = N_DMA_SEMS - N_SW_SEMS + self.drr_sw
                self.drr_sw = (self.drr_sw + 1) % N_SW_SEMS
            else:
                s = self.drr
                self.drr = (self.drr + 1) % (N_DMA_SEMS - N_SW_SEMS)
            if s in self.phase_dma:
                deps.add(self.phase_dma[s])
            self.dcount[s] += 1
            o.dsem = s
            o.dval = 16 * self.dcount[s]
            self.phase_dma[s] = o.idx
        key = ("dma", o.idx) if dma else eng
        if not dma:
            self.lastop[eng] = o.idx
        for b in reads:
            self.touched.add(b)
            b.r[key] = o.idx
        for b in writes:
            self.touched.add(b)
            b.pr = set(b.r.values()) | set(b.w.values())
            b.w = {key: o.idx}
            b.r = {}
        for b in pwrites:
            self.touched.add(b)
            if b.r:
                b.pr = set(b.r.values()) | set(b.w.values())
                b.w = {key: o.idx}
                b.r = {}
            else:
                b.w[key] = o.idx
        self.ops.append(o)
        return o

    def _needs(self, c, p):
        if p.is_dma:
            return True
        if p.eng == c.eng:
            if c.eng == "pe":
                return False
            if c.is_dma:
                return True
            if c.eng == "pool":
                return True
            return p.pos >= c.pos - 3
        return True

    def emit(self):
        nc = self.nc
        ops = self.ops
        f = Op()
        f.eng = "sp"; f.fn = None; f.idx = len(ops); f.is_dma = False
        f.inc = False; f.tick = 0; f.pos = self.pos["sp"]
        f.deps = set(self.phase_dma.values())
        for e, i in self.lastop.items():
            if e != "sp":
                f.deps.add(i)
        ops.append(f)
        for o in ops:
            for d in o.deps:
                p = ops[d]
                if (not p.is_dma) and self._needs(o, p):
                    p.inc = True
        cnt = dict(self.tickbase)
        for o in ops:
            if (not o.is_dma) and o.inc:
                cnt[o.eng] += 1
                o.tick = cnt[o.eng]
        per = {e: [] for e in ENGS}
        for o in ops:
            per[o.eng].append(o)
        waited = self.waited
        sem = self.sem
        dsems = self.dsems

        def run(e, eng):
            wd = waited[e]
            for o in per[e]:
                for d in sorted(o.deps):
                    p = ops[d]
                    if not self._needs(o, p):
                        continue
                    if p.is_dma:
                        k = ("d", p.dsem)
                        if wd.get(k, 0) < p.dval:
                            eng.wait_ge(dsems[p.dsem], p.dval)
                            wd[k] = p.dval
                    else:
                        k = p.eng
                        if wd.get(k, 0) < p.tick:
                            eng.wait_ge(sem[p.eng], p.tick)
                            wd[k] = p.tick
                if o.fn is None:
                    continue
                inst = o.fn(eng)
                if o.is_dma:
                    inst.then_inc(dsems[o.dsem], 16)
                elif o.inc:
                    inst.then_inc(sem[o.eng], 1)

        with nc.Block() as block:
            @block.tensor
            def _(eng):
                run("pe", eng)

            @block.scalar
            def _(eng):
                run("act", eng)

            @block.vector
            def _(eng):
                run("dve", eng)

            @block.gpsimd
            def _(eng):
                run("pool", eng)

            @block.sync
            def _(eng):
                run("sp", eng)

        self.tickbase = cnt
        self.ops = []
        self.pos = {e: 0 for e in ENGS}
        self.phase_dma = {}
        for b in self.touched:
            b.w = {}
            b.r = {}
            b.pr = set()
        self.touched = set()
        self.lastop = {}
        return {e: len(per[e]) for e in ENGS}


def bc_mid(t, row, off, n_mid, n_in):
    return bass.AP(t, off, [[row, 128], [0, n_mid], [1, n_in]])


def bc_in(t, row, off, n_mid, n_in):
    return bass.AP(t, off, [[row, 128], [1, n_mid], [0, n_in]])


def build(NBH, debug=False):
    TH = NBH * 256
    NT = TH // 128
    NG = TH // 512
    NBK = 2 * NBH
    TT = 2 * TH
    scale = 1.0 / np.sqrt(HD)

    nc = bass.Bass("TRN2", target_bir_lowering=False)
    din = lambda n, s, d=F32: nc.dram_tensor(n, list(s), d, kind="ExternalInput")
    x_pre = din("x_pre", [TH, D]); x_own = din("x_own", [TH, D]); p_own = din("p_own", [TH, PLE])
    w_in = din("w_in", [D, INC]); w_o = din("w_o", [D, D]); w_gate = din("w_gate", [D, DFF])
    w_up = din("w_up", [D, DFF]); w_down = din("w_down", [DFF, D]); w_pg = din("w_pg", [D, D])
    w_pp = din("w_pp", [PLE, D])
    g_mix = din("g_mix", [D]); g_ffn = din("g_ffn", [D]); g_ple = din("g_ple", [D])
    q_norm = din("q_norm", [HD]); k_norm = din("k_norm", [HD]); g_ret = din("g_ret", [H * HD])
    b_pg = din("b_pg", [D])
    rope_a_pre = din("rope_a_pre", [TH, 128]); rope_a_own = din("rope_a_own", [TH, 128])
    rope_r_pre = din("rope_r_pre", [TH, 128]); rope_r_own = din("rope_r_own", [TH, 128])
    c_kdec = din("c_kdec", [128, H]); c_qdec = din("c_qdec", [128, H]); c_gch = din("c_gch", [128, H])
    c_decT = din("c_decT", [128, H * 128])
    c_past = din("c_past", [128, NBH * NBK]); c_diag = din("c_diag", [128, NBH * NBK])
    c_mask = din("c_mask", [128, 4 * 512], BF16)
    c_onehot = din("c_onehot", [128, NBK * 128], BF16)
    c_identb = din("c_identb", [128, 128], BF16); c_identf = din("c_identf", [128, 128])
    out = nc.dram_tensor("out", [TH, D], F32, kind="ExternalOutput")

    dscr = lambda n, s, d=BF16: nc.dram_tensor(n, list(s), d)
    wb_in = dscr("wb_in", [D, INC]); wb_o = dscr("wb_o", [D, D]); wb_g = dscr("wb_g", [D, DFF])
    wb_u = dscr("wb_u", [D, DFF]); wb_d = dscr("wb_d", [DFF, D]); wb_pg = dscr("wb_pg", [D, D])
    wb_pp = dscr("wb_pp", [PLE, D])
    kT_s = dscr("kT_s", [H, 128, TT]); v_s = dscr("v_s", [TT, H * HD]); qT_s = dscr("qT_s", [H, 128, TH])
    rqT_s = dscr("rqT_s", [H, 128, TH]); rkT_s = dscr("rkT_s", [H, 128, TH])
    rkd_s = dscr("rkd_s", [TH, H * HD]); rv_s = dscr("rv_s", [TH, H * HD]); rgs_s = dscr("rgs_s", [TH, H * HD])
    mixT_s = dscr("mixT_s", [D, TH])
    dbg = {}
    if debug:
        for n, s in (("d_qT", [H, 128, TH]), ("d_kT", [H, 128, TT]), ("d_v", [TT, H * HD]), ("d_mixT", [D, TH]),
                     ("d_rqT", [H, 128, TH]), ("d_rkd", [TH, H * HD])):
            dbg[n] = nc.dram_tensor(n, s, BF16, kind="ExternalOutput")

    with contextlib.ExitStack() as gst:
        S = Sched(nc, gst)
        gsb = lambda n, s, d=F32: gst.enter_context(nc.sbuf_tensor(n, list(s), d))
        identb = gsb("identb", [128, 128], BF16); identf = gsb("identf", [128, 128])
        gmix_t = gsb("gmix_t", [128, KD]); gffn_t = gsb("gffn_t", [128, KD]); gple_t = gsb("gple_t", [128, KD])
        qn_t = gsb("qn_t", [128, HD]); kn_t = gsb("kn_t", [128, HD])
        kdec_t = gsb("kdec_t", [128, H]); qdec_t = gsb("qdec_t", [128, H]); gch_t = gsb("gch_t", [128, H])
        mhalf = gsb("mhalf", [128, 8])
        kmT = gsb("kmT", [128, H * NBK])
        Sst = gsb("Sst", [128, H * 128])
        B_const = Buf("const"); B_kmT = Buf("kmT"); B_S = Buf("S")
        B_w = {n: Buf(n) for n in ("in", "o", "g", "u", "d", "pg", "pp")}
        B_scr = {n: Buf(n) for n in ("kT", "v", "qT", "rqT", "rkT", "rkd", "rv", "rgs", "mixT")}

        def cast_w(src, dst, R, C, buf):
            for r0 in range(0, R, 1024):
                rr = min(1024, R - r0)
                for c0 in range(0, C, 2048):
                    cc = min(2048, C - c0)
                    S.op("pool", lambda e, r0=r0, rr=rr, c0=c0, cc=cc: e.dma_start(
                        out=dst[r0:r0 + rr, c0:c0 + cc], in_=src[r0:r0 + rr, c0:c0 + cc]),
                        pwrites=[buf], dma=True)

        cast_w(w_in, wb_in, D, INC, B_w["in"])
        ld = lambda o_, i_, **kw: S.op("sp", lambda e: e.dma_start(out=o_, in_=i_, **kw), pwrites=[B_const], dma=True)
        ld(identb[:, :], c_identb[:, :]); ld(identf[:, :], c_identf[:, :])
        for t_, g_ in ((gmix_t, g_mix), (gffn_t, g_ffn), (gple_t, g_ple)):
            ld(t_[:, :], g_.ap().rearrange("(k p) -> p k", p=128), allow_slow_non_contiguous=True)
        ld(qn_t[:, :], q_norm.ap().partition_broadcast(128)); ld(kn_t[:, :], k_norm.ap().partition_broadcast(128))
        ld(kdec_t[:, :], c_kdec[:, :]); ld(qdec_t[:, :], c_qdec[:, :]); ld(gch_t[:, :], c_gch[:, :])
        S.op("pool", lambda e: e.memset(mhalf[:, :], -0.5), pwrites=[B_const])
        S.op("pool", lambda e: e.memset(Sst[:, :], 0.0), writes=[B_S])
        S.op("pool", lambda e: e.memset(kmT[:, :], 0.0), writes=[B_kmT])
        st0 = S.emit()

        with contextlib.ExitStack() as st:
            sb = lambda n, s, d=F32: st.enter_context(nc.sbuf_tensor(n, list(s), d))
            psb = lambda n, s, d=F32: st.enter_context(nc.psum_tensor(n, list(s), d))
            NSL = 3
            wsl = [sb("wsl%d" % i, [128, KD, 512], BF16) for i in range(NSL)]; B_wsl = [Buf() for _ in range(NSL)]
            xt = [sb("xt%d" % i, [128, D]) for i in range(2)]; B_xt = [Buf() for _ in range(2)]
            xn = [sb("xn%d" % i, [128, D], BF16) for i in range(2)]; B_xn = [Buf() for _ in range(2)]
            st8 = [sb("st8_%d" % i, [128, 8]) for i in range(2)]; B_st8 = [Buf() for _ in range(2)]
            hT = [sb("hT%d" % i, [128, KD, 512], BF16) for i in range(2)]; B_hT = [Buf() for _ in range(2)]
            NTMP = 6
            tmp = [[sb("tmp%d_%d" % (i, j), [128, 512]) for j in range(3)] for i in range(NTMP)]
            B_tmp = [[Buf() for j in range(3)] for i in range(NTMP)]
            sm = [sb("sm%d" % i, [128, 8]) for i in range(NTMP)]; B_sm = [Buf() for _ in range(NTMP)]
            NOB = 8
            ob = [sb("ob%d" % i, [128, 512], BF16) for i in range(NOB)]; B_ob = [Buf() for _ in range(NOB)]
            NSTG = 3
            stage = [sb("stage%d" % i, [128, 4, 512], BF16) for i in range(NSTG)]; B_stage = [Buf() for _ in range(NSTG)]
            kdh = sb("kdh", [128, 4, H * HD], BF16); B_kdh = [Buf() for _ in range(4)]
            NVO = 4
            vob = [sb("vob%d" % i, [128, 512], BF16) for i in range(NVO)]; B_vob = [Buf() for _ in range(NVO)]
            NKO = 8
            kdo = [sb("kdo%d" % i, [128, 512], BF16) for i in range(NKO)]; B_kdo = [Buf() for _ in range(NKO)]
            ropa = [sb("ropa%d" % i, [128, 4, 128]) for i in range(2)]; B_ropa = [Buf() for _ in range(2)]
            ropr = [sb("ropr%d" % i, [128, 4, 128]) for i in range(2)]; B_ropr = [Buf() for _ in range(2)]
            NACC = 4
            pacc = [psb("pacc%d" % i, [128, 512]) for i in range(NACC)]; B_pacc = [Buf() for _ in range(NACC)]
            ptx = psb("ptx", [128, 8, 128], BF16); B_ptx = Buf()
            ptq = psb("ptq", [128, 8, 128], BF16); B_ptq = Buf()
            pU = [psb("pU%d" % i, [128, 512]) for i in range(2)]; B_pU = [Buf() for _ in range(2)]

            cast_w(w_o, wb_o, D, D, B_w["o"]); cast_w(w_pg, wb_pg, D, D, B_w["pg"]); cast_w(w_pp, wb_pp, PLE, D, B_w["pp"])

            ctr = {"tile": 0, "slab": 0, "pp": 0, "acc": 0, "vo": 0, "stg": 0, "ko": 0, "ob": 0}

            def rope(T_, BT_, tab, Btab, outb, Bout, tt, eng):
                zc = T_[0]; Bz = BT_[0]
                z1 = bass.AP(zc, 0, [[512, 128], [128, 4], [1, 64]])
                z2 = bass.AP(zc, 64, [[512, 128], [128, 4], [1, 64]])
                cs = bc_mid(tab, 512, tt * 128, 4, 64); sn = bc_mid(tab, 512, tt * 128 + 64, 4, 64)
                v4 = lambda t_, off: bass.AP(t_, off, [[512, 128], [64, 4], [1, 64]])
                a, b_, c_, d_ = v4(T_[1], 0), v4(T_[1], 256), v4(T_[2], 0), v4(T_[2], 256)
                o1 = bass.AP(outb, 0, [[512, 128], [128, 4], [1, 64]])
                o2 = bass.AP(outb, 64, [[512, 128], [128, 4], [1, 64]])
                S.op(eng, lambda e: e.tensor_tensor(out=a, in0=z1, in1=cs, op=ALU.mult), reads=[Bz, Btab], writes=[BT_[1]])
                S.op(eng, lambda e: e.tensor_tensor(out=b_, in0=z2, in1=sn, op=ALU.mult), reads=[Bz, Btab], pwrites=[BT_[1]])
                S.op(eng, lambda e: e.tensor_tensor(out=c_, in0=z2, in1=cs, op=ALU.mult), reads=[Bz, Btab], writes=[BT_[2]])
                S.op(eng, lambda e: e.tensor_tensor(out=d_, in0=z1, in1=sn, op=ALU.mult), reads=[Bz, Btab], pwrites=[BT_[2]])
                S.op(eng, lambda e: e.tensor_tensor(out=o1, in0=a, in1=b_, op=ALU.subtract), reads=[BT_[1]], pwrites=[Bout])
                S.op(eng, lambda e: e.tensor_tensor(out=o2, in0=c_, in1=d_, op=ALU.add), reads=[BT_[2]], pwrites=[Bout])

            seq = [(g, False) for g in range(NG)] + [(g, True) for g in range(NG)]

            def prologue_steps(q):
                g, own = seq[q]
                hb = q % 2
                xsrc = x_own if own else x_pre
                ra_src = rope_a_own if own else rope_a_pre
                rr_src = rope_r_own if own else rope_r_pre
                steps = []
                for tt in range(4):
                    t = g * 4 + tt
                    ti = ctr["tile"] % 2; ctr["tile"] += 1

                    def sa(tt=tt, t=t, ti=ti):
                        if tt == 0:
                            S.op("sp", lambda e: e.dma_start(out=ropa[hb][:, :, :], in_=ra_src[g * 512:(g + 1) * 512, :].rearrange("(t p) c -> p t c", p=128)), writes=[B_ropa[hb]], dma=True)
                            S.op("sp", lambda e: e.dma_start(out=ropr[hb][:, :, :], in_=rr_src[g * 512:(g + 1) * 512, :].rearrange("(t p) c -> p t c", p=128)), writes=[B_ropr[hb]], dma=True)
                        S.op("sp", lambda e: e.dma_start(out=xt[ti][:, :], in_=xsrc[t * 128:(t + 1) * 128, :]), writes=[B_xt[ti]], dma=True)
                        S.op("act", lambda e: e.activation(out=xn[ti][:, :], in_=xt[ti][:, :], func=AF.Square, accum_out=st8[ti][:, 0:1]),
                             reads=[B_xt[ti]], writes=[B_xn[ti], B_st8[ti]])
                        S.op("pool", lambda e: e.tensor_scalar(out=st8[ti][:, 1:2], in0=st8[ti][:, 0:1], scalar1=1.0 / D, scalar2=EPS, op0=ALU.mult, op1=ALU.add),
                             reads=[B_st8[ti]], writes=[B_st8[ti]])
                        S.op("pool", lambda e: e.tensor_tensor(out=st8[ti][:, 2:3], in0=st8[ti][:, 1:2], in1=mhalf[:, 0:1], op=ALU.pow),
                             reads=[B_st8[ti], B_const], writes=[B_st8[ti]])
                        S.op("act", lambda e: e.activation(out=xn[ti][:, :], in_=xt[ti][:, :], func=AF.Copy, scale=st8[ti][:, 2:3]),
                             reads=[B_xt[ti], B_st8[ti]], writes=[B_xn[ti]])

                    def sb_(tt=tt, ti=ti):
                        for k4 in range(4):
                            for kk in range(4):
                                k = k4 * 4 + kk
                                S.op("pe", lambda e, k=k, kk=kk: e.transpose(out=ptx[:, kk, :], in_=xn[ti][:, k * 128:(k + 1) * 128], identity=identb[:, :]),
                                     reads=[B_xn[ti], B_const], pwrites=[B_ptx])
                            for kk in range(4):
                                k = k4 * 4 + kk
                                S.op("dve", lambda e, k=k, kk=kk: e.tensor_scalar(out=hT[hb][:, k, tt * 128:(tt + 1) * 128], in0=ptx[:, kk, :],
                                                                               scalar1=gmix_t[:, k:k + 1], scalar2=None, op0=ALU.mult),
                                     reads=[B_ptx, B_const], pwrites=[B_hT[hb]])
                    steps.append(sa); steps.append(sb_)
                return steps

            units = []
            extra = {}
            chunk_list = []

            def slab_load(j):
                q, c = chunk_list[j]
                si = j % NSL
                S.op("sp", lambda e: e.dma_start(out=wsl[si][:, :, :], in_=wb_in.ap()[:, c * 512:(c + 1) * 512].rearrange("(k p) n -> p k n", p=128)),
                     reads=[B_w["in"]], writes=[B_wsl[si]], dma=True)

            for q, (g, own) in enumerate(seq):
                for c in (list(range(14)) if own else [2, 3, 4, 5, 8, 9, 10, 11]):
                    chunk_list.append((q, c))
            first_unit_of_group = {}
            for j, (q, c) in enumerate(chunk_list):
                g, own = seq[q]
                hb = q % 2
                tokoff = TH if own else 0
                typ = c // 2
                hc = c % 2
                si = j % NSL
                if q not in first_unit_of_group:
                    first_unit_of_group[q] = len(units)
                extra.setdefault(len(units), []).append(lambda j=j: slab_load(j + 2) if j + 2 < len(chunk_list) else None)
                need_stage = typ in (0, 1, 3) or (typ == 4 and own)
                if need_stage:
                    sidx = ctr["stg"] % NSTG; ctr["stg"] += 1
                for tt in range(4):
                    t = g * 4 + tt
                    tok0 = tokoff + t * 128
                    ai = ctr["acc"] % NACC; ctr["acc"] += 1
                    qk = typ in (0, 1, 3, 4)
                    attn = typ in (0, 1)
                    if qk:
                        pi = ctr["pp"] % NTMP; ctr["pp"] += 1
                        T_, BT_ = tmp[pi], B_tmp[pi]
                        oi = ctr["ob"] % NOB; ctr["ob"] += 1
                        ueng = "dve" if (ctr["ob"] % 2 == 0) else "pool"
                    need_vo = (not qk) or (typ == 4 and own)
                    if not qk:
                        vi = ctr["vo"] % NVO; ctr["vo"] += 1
                    elif need_vo:
                        vi = ctr["ko"] % NKO; ctr["ko"] += 1
                    A = B = C = Dd = None

                    def A(tt=tt, si=si, ai=ai, hb=hb, typ=typ, qk=qk, T_=(T_ if qk else None), BT_=(BT_ if qk else None), vi=(vi if need_vo else None)):
                        for k in range(KD):
                            S.op("pe", lambda e, k=k: e.matmul(pacc[ai][:, :], lhsT=hT[hb][:, k, tt * 128:(tt + 1) * 128], rhs=wsl[si][:, k, :], start=(k == 0), stop=(k == KD - 1)),
                                 reads=[B_hT[hb], B_wsl[si]], pwrites=[B_pacc[ai]])
                        if qk:
                            S.op("act", lambda e: e.activation(out=T_[0][:, :], in_=pacc[ai][:, :], func=AF.Copy), reads=[B_pacc[ai]], writes=[BT_[0]])
                        elif typ == 6:
                            S.op("act", lambda e: e.activation(out=vob[vi][:, :], in_=pacc[ai][:, :], func=AF.Silu), reads=[B_pacc[ai]], writes=[B_vob[vi]])
                        else:
                            S.op("act", lambda e: e.activation(out=vob[vi][:, :], in_=pacc[ai][:, :], func=AF.Copy), reads=[B_pacc[ai]], writes=[B_vob[vi]])

                    if not qk:
                        def B(tt=tt, t=t, tok0=tok0, hc=hc, typ=typ, own=own, vi=vi):
                            if typ == 2:
                                S.op("sp", lambda e: e.dma_start(out=v_s[tok0:tok0 + 128, hc * 512:(hc + 1) * 512], in_=vob[vi][:, :]), reads=[B_vob[vi]], pwrites=[B_scr["v"]], dma=True)
                            elif typ == 6:
                                S.op("sp", lambda e: e.dma_start(out=rgs_s[t * 128:(t + 1) * 128, hc * 512:(hc + 1) * 512], in_=vob[vi][:, :]), reads=[B_vob[vi]], pwrites=[B_scr["rgs"]], dma=True)
                            elif own:
                                S.op("sp", lambda e: e.dma_start(out=rv_s[t * 128:(t + 1) * 128, hc * 512:(hc + 1) * 512], in_=vob[vi][:, :]), reads=[B_vob[vi]], pwrites=[B_scr["rv"]], dma=True)
                            else:
                                for hh in range(4):
                                    h = hc * 4 + hh
                                    S.op("pe", lambda e, h=h, hh=hh: e.matmul(pU[hc][:, hh * 128:(hh + 1) * 128], lhsT=kdh[:, tt, h * 128:(h + 1) * 128],
                                                                          rhs=vob[vi][:, hh * 128:(hh + 1) * 128], start=True, stop=True),
                                         reads=[B_kdh[tt], B_vob[vi]], pwrites=[B_pU[hc]])
                                sv = bass.AP(Sst, hc * 512, [[H * 128, 128], [128, 4], [1, 128]])
                                S.op("pool", lambda e: e.tensor_tensor(out=sv, in0=sv, in1=bc_in(gch_t, H, hc * 4, 4, 128), op=ALU.mult), reads=[B_S, B_const], writes=[B_S])
                                S.op("dve", lambda e: e.tensor_tensor(out=Sst[:, hc * 512:(hc + 1) * 512], in0=Sst[:, hc * 512:(hc + 1) * 512], in1=pU[hc][:, :], op=ALU.add),
                                     reads=[B_S, B_pU[hc]], writes=[B_S])
                        units.append([A, B, None, None])
                        continue

                    tab, Btab = (ropa[hb], B_ropa[hb]) if attn else (ropr[hb], B_ropr[hb])
                    if attn:
                        def B(pi=pi, T_=T_, BT_=BT_, typ=typ, ueng=ueng):
                            gn = qn_t if typ == 0 else kn_t
                            zc, sq = T_[0], T_[1]
                            S.op("pool", lambda e: e.tensor_tensor(out=sq[:, :], in0=zc[:, :], in1=zc[:, :], op=ALU.mult), reads=[BT_[0]], writes=[BT_[1]])
                            S.op("dve", lambda e: e.tensor_reduce(out=sm[pi][:, 0:4], in_=sq[:, :].rearrange("p (h d) -> p h d", h=4), axis=AX.X, op=ALU.add),
                                 reads=[BT_[1]], writes=[B_sm[pi]])
                            S.op("pool", lambda e: e.tensor_scalar(out=sm[pi][:, 0:4], in0=sm[pi][:, 0:4], scalar1=1.0 / HD, scalar2=EPS, op0=ALU.mult, op1=ALU.add),
                                 reads=[B_sm[pi]], writes=[B_sm[pi]])
                            S.op("pool", lambda e: e.tensor_tensor(out=sm[pi][:, 4:8], in0=sm[pi][:, 0:4], in1=mhalf[:, 0:4], op=ALU.pow), reads=[B_sm[pi], B_const], writes=[B_sm[pi]])
                            z3 = zc[:, :].rearrange("p (h d) -> p h d", h=4)
                            S.op(ueng, lambda e: e.tensor_tensor(out=z3, in0=z3, in1=bc_in(sm[pi], 8, 4, 4, 128), op=ALU.mult), reads=[BT_[0], B_sm[pi]], writes=[BT_[0]])
                            S.op(ueng, lambda e: e.tensor_tensor(out=z3, in0=z3, in1=bc_mid(gn, HD, 0, 4, 128), op=ALU.mult), reads=[BT_[0], B_const], writes=[BT_[0]])

                    def C(pi=oi, T_=T_, BT_=BT_, tab=tab, Btab=Btab, tt=tt, typ=typ, own=own, hc=hc, vi=(vi if need_vo else None), ueng=ueng):
                        rope(T_, BT_, tab, Btab, ob[pi], B_ob[pi], tt, ueng)
                        if typ == 4:
                            o3 = ob[pi][:, :].rearrange("p (h d) -> p h d", h=4)
                            if own:
                                S.op(ueng, lambda e: e.tensor_tensor(out=kdo[vi][:, :].rearrange("p (h d) -> p h d", h=4), in0=o3, in1=bc_in(kdec_t, H, hc * 4, 4, 128), op=ALU.mult),
                                     reads=[B_ob[pi], B_const], writes=[B_kdo[vi]])
                            else:
                                S.op(ueng, lambda e: e.tensor_tensor(out=kdh[:, tt, hc * 512:(hc + 1) * 512].rearrange("p (h d) -> p h d", h=4), in0=o3,
                                                                     in1=bc_in(kdec_t, H, hc * 4, 4, 128), op=ALU.mult),
                                     reads=[B_ob[pi], B_const], pwrites=[B_kdh[tt]])

                    if need_stage:
                        def Dd(pi=oi, tt=tt, t=t, typ=typ, own=own, hc=hc, sidx=sidx, g=g, tokoff=tokoff, vi=(vi if need_vo else None)):
                            if typ == 4 and own:
                                S.op("sp", lambda e: e.dma_start(out=rkd_s[t * 128:(t + 1) * 128, hc * 512:(hc + 1) * 512], in_=kdo[vi][:, :]), reads=[B_kdo[vi]], pwrites=[B_scr["rkd"]], dma=True)
                            for hh in range(4):
                                S.op("pe", lambda e, hh=hh: e.transpose(out=ptq[:, hh, :], in_=ob[pi][:, hh * 128:(hh + 1) * 128], identity=identb[:, :]),
                                     reads=[B_ob[pi], B_const], pwrites=[B_ptq])
                            S.op("act", lambda e: e.activation(out=stage[sidx][:, :, tt * 128:(tt + 1) * 128], in_=ptq[:, 0:4, :], func=AF.Copy), reads=[B_ptq], pwrites=[B_stage[sidx]])
                            if tt == 3:
                                dst, bname, toff = {0: (qT_s, "qT", 0), 1: (kT_s, "kT", tokoff), 3: (rqT_s, "rqT", 0), 4: (rkT_s, "rkT", 0)}[typ]
                                c0 = toff + g * 512
                                S.op("sp", lambda e: e.dma_start(out=dst[hc * 4:(hc + 1) * 4, :, c0:c0 + 512].rearrange("h d t -> d h t"), in_=stage[sidx][:, :, :]),
                                     reads=[B_stage[sidx]], pwrites=[B_scr[bname]], dma=True)
                                if typ == 1:
                                    blk0 = (tokoff + g * 512) // 256
                                    kv = bass.AP(kmT, hc * 4 * NBK + blk0, [[H * NBK, 128], [NBK, 4], [1, 2]])
                                    S.op("dve", lambda e: e.tensor_reduce(out=kv, in_=stage[sidx][:, :, :].rearrange("p h (b t) -> p h b t", b=2), axis=AX.X, op=ALU.add),
                                         reads=[B_stage[sidx]], pwrites=[B_kmT])
                    units.append([A, (B if attn else None), C, (Dd if need_stage else None)])

            for q in range(len(seq)):
                u0 = first_unit_of_group[q]
                u1 = first_unit_of_group[q + 1] if q + 1 < len(seq) else len(units)
                if q + 1 < len(seq):
                    steps = prologue_steps(q + 1)
                    n = u1 - u0
                    for i_, stp in enumerate(steps):
                        extra.setdefault(u0 + 10 + (i_ * (n - 15)) // 8, []).append(stp)
            ctr["tile"] = 0
            pre0 = prologue_steps(0)
            slab_load(0)
            slab_load(1)
            for stp in pre0:
                stp()
            SKEW = (0, 1, 4, 9)
            NU = len(units)
            for i in range(NU + SKEW[3]):
                if i < NU:
                    for fn in extra.get(i, []):
                        fn()
                for s_ in range(4):
                    j = i - SKEW[s_]
                    if 0 <= j < NU and units[j][s_] is not None:
                        units[j][s_]()

            if debug:
                S.op("sp", lambda e: e.dma_start(out=dbg["d_qT"][:, :, :], in_=qT_s[:, :, :]), reads=[B_scr["qT"]], dma=True)
                S.op("sp", lambda e: e.dma_start(out=dbg["d_kT"][:, :, :], in_=kT_s[:, :, :]), reads=[B_scr["kT"]], dma=True)
                S.op("sp", lambda e: e.dma_start(out=dbg["d_v"][:, :], in_=v_s[:, :]), reads=[B_scr["v"]], dma=True)
                S.op("sp", lambda e: e.dma_start(out=dbg["d_rqT"][:, :, :], in_=rqT_s[:, :, :]), reads=[B_scr["rqT"]], dma=True)
                S.op("sp", lambda e: e.dma_start(out=dbg["d_rkd"][:, :], in_=rkd_s[:, :]), reads=[B_scr["rkd"]], dma=True)
            st1 = S.emit()
        print("phase0", st0, "phase1", st1, flush=True)
        import os as _os
        if _os.environ.get("STOP_AFTER") != "1":
            build_rest(nc, S, locals())
    return nc


def build_rest(nc, S, L):
    NBH, TH, NT, NG, NBK, TT, scale, debug, dbg = (L[k] for k in ("NBH", "TH", "NT", "NG", "NBK", "TT", "scale", "debug", "dbg"))
    identb, identf, gffn_t, gple_t, kdec_t, qdec_t, gch_t, mhalf, kmT, Sst = (L[k] for k in (
        "identb", "identf", "gffn_t", "gple_t", "kdec_t", "qdec_t", "gch_t", "mhalf", "kmT", "Sst"))
    B_const, B_kmT, B_S, B_w, B_scr, cast_w = (L[k] for k in ("B_const", "B_kmT", "B_S", "B_w", "B_scr", "cast_w"))
    scale = float(scale)

    with contextlib.ExitStack() as st:
        sb = lambda n, s, d=F32: st.enter_context(nc.sbuf_tensor(n, list(s), d))
        psb = lambda n, s, d=F32: st.enter_context(nc.psum_tensor(n, list(s), d))
        cast_jobs = []
        for src_, dst_, R_, C_, bn_ in ((L["w_gate"], L["wb_g"], D, DFF, "g"), (L["w_up"], L["wb_u"], D, DFF, "u"), (L["w_down"], L["wb_d"], DFF, D, "d")):
            for r0 in range(0, R_, 1024):
                rr = min(1024, R_ - r0)
                for c0 in range(0, C_, 2048):
                    cc = min(2048, C_ - c0)
                    cast_jobs.append(lambda src_=src_, dst_=dst_, r0=r0, rr=rr, c0=c0, cc=cc, bn_=bn_: S.op(
                        "pool", lambda e: e.dma_start(out=dst_[r0:r0 + rr, c0:c0 + cc], in_=src_[r0:r0 + rr, c0:c0 + cc]), pwrites=[B_w[bn_]], dma=True))
        kTh = [sb("kTh%d" % i, [128, TT], BF16) for i in range(2)]; B_kTh = [Buf() for _ in range(2)]
        V1 = [sb("V1_%d" % i, [128, 2 * NT, 129], BF16) for i in range(2)]; B_V1 = [Buf() for _ in range(2)]
        qTh = [sb("qTh%d" % i, [128, TH], BF16) for i in range(2)]; B_qTh = [Buf() for _ in range(2)]
        aTst = [sb("aTst%d" % i, [128, TH], BF16) for i in range(2)]; B_aTst = [Buf() for _ in range(2)]
        kmb = sb("kmb", [128, H * NBK], BF16); B_kmb = Buf()
        pastt = sb("pastt", [128, NBH * NBK]); diagt = sb("diagt", [128, NBH * NBK])
        maskc = sb("maskc", [128, 4 * 512], BF16); oneh = sb("oneh", [128, NBK * 128], BF16)
        B_c2 = Buf()
        NPT = 5
        PT2 = [sb("PT2_%d" % i, [128, 1024], BF16) for i in range(NPT)]; B_PT = [Buf() for _ in range(NPT)]
        gs = sb("gs", [128, 4 * NBK]); B_gs = Buf()
        m8 = sb("m8", [128, 4 * 8]); B_m8 = Buf()
        selb = sb("selb", [128, 4 * NBK]); B_selb = Buf()
        NP = NBH // 2
        selbTa = [sb("selbTa%d" % i, [128, NP * 512], BF16) for i in range(2)]; B_selbTa = [[Buf() for _ in range(NP)] for _ in range(2)]
        PaD = [sb("PaD%d" % i, [128, 1024]) for i in range(2)]; B_PaD = [Buf() for _ in range(2)]
        PaP = [sb("PaP%d" % i, [128, 1024]) for i in range(2)]; B_PaP = [Buf() for _ in range(2)]
        psm = [sb("psm%d" % i, [128, 512]) for i in range(3)]; B_psm = [Buf() for _ in range(3)]
        rcr = sb("rcr", [1, 512]); B_rcr = Buf()
        bcs = sb("bcs", [128, 512]); B_bcs = Buf()
        onesc = sb("onesc", [128, 1], BF16); onesr = sb("onesr", [1, 128], BF16)
        phl = [sb("phl%d" % i, [128, 512], BF16) for i in range(2)]; B_phl = Buf()
        rhl = [sb("rhl%d" % i, [1, 512], BF16) for i in range(2)]; B_rhl = Buf()
        pSTw = [psb("pSTw%d" % i, [128, 1024]) for i in range(2)]; B_pST = [Buf() for _ in range(2)]
        pacc_o = [psb("pacco%d" % i, [128, 512]) for i in range(2)]; B_pacc_o = [Buf() for _ in range(2)]
        pm1 = psb("pm1", [128, 512]); B_pm1 = Buf()
        pm2 = psb("pm2", [128, 512]); B_pm2 = Buf()
        pm2b = pm2[:, :].bitcast(BF16)

        ldc = lambda o_, i_: S.op("sp", lambda e: e.dma_start(out=o_, in_=i_), pwrites=[B_c2], dma=True)
        ldc(pastt[:, :], L["c_past"][:, :]); ldc(diagt[:, :], L["c_diag"][:, :]); ldc(maskc[:, :], L["c_mask"][:, :]); ldc(oneh[:, :], L["c_onehot"][:, :])
        S.op("dve", lambda e: e.tensor_copy(out=kmb[:, :], in_=kmT[:, :]), reads=[B_kmT], writes=[B_kmb])
        for i in range(2):
            for p_ in range(NP):
                S.op("pool", lambda e, i=i, p_=p_: e.memset(selbTa[i][:, p_ * 512:(p_ + 1) * 512], 0.0), writes=[B_selbTa[i][p_]])
        S.op("pool", lambda e: e.memset(onesc[:, :], 1.0), pwrites=[B_c2])
        S.op("pool", lambda e: e.memset(onesr[:, :], 1.0), pwrites=[B_c2])

        def head_loads(h):
            hb = h % 2
            S.op("sp", lambda e: e.dma_start(out=qTh[hb][:, :], in_=L["qT_s"][h, :, :]), reads=[B_scr["qT"]], writes=[B_qTh[hb]], dma=True)
            S.op("sp", lambda e: e.dma_start(out=kTh[hb][:, :], in_=L["kT_s"][h, :, :]), reads=[B_scr["kT"]], writes=[B_kTh[hb]], dma=True)
            nvs = max(1, (2 * NT) // 16)
            for vs in range(nvs):
                t0_, t1_ = vs * (2 * NT // nvs), (vs + 1) * (2 * NT // nvs)
                S.op("sp", lambda e, t0_=t0_, t1_=t1_: e.dma_start(out=V1[hb][:, t0_:t1_, 0:128],
                                                                 in_=L["v_s"][t0_ * 128:t1_ * 128, h * 128:(h + 1) * 128].rearrange("(t p) d -> p t d", p=128)),
                     reads=[B_scr["v"]], writes=[B_V1[hb]] if vs == 0 else [], pwrites=[] if vs == 0 else [B_V1[hb]], dma=True)

        def sel_steps(h):
            hb = h % 2
            steps = []
            row = NBH * NBK
            v4 = lambda t_: bass.AP(t_, 0, [[4 * NBK, 128], [2 * NBK, 2], [NBK, 2], [1, NBK]])
            g3 = lambda t_: bass.AP(t_, 0, [[4 * NBK, 128], [NBK, 4], [1, NBK]])
            pm1v = bass.AP(pm1, 0, [[512, 128], [2 * NBK, 2], [NBK, 2], [1, NBK]])
            thr = bass.AP(m8, 2, [[32, 128], [8, 4], [0, NBK]])
            for p in range(NP):
                pastv = bass.AP(pastt, 2 * p * NBK, [[row, 128], [NBK, 2], [0, 2], [1, NBK]])
                diagv = bass.AP(diagt, 2 * p * NBK, [[row, 128], [NBK, 2], [0, 2], [1, NBK]])

                def s1(p=p):
                    for qt in range(4):
                        tq0 = p * 512 + qt * 128
                        S.op("pe", lambda e, qt=qt, tq0=tq0: e.matmul(pm1[:, qt * NBK:(qt + 1) * NBK], lhsT=qTh[hb][:, tq0:tq0 + 128], rhs=kmb[:, h * NBK:(h + 1) * NBK],
                                                                  start=True, stop=True), reads=[B_qTh[hb], B_kmb], pwrites=[B_pm1])

                def s2(pastv=pastv):
                    S.op("dve", lambda e: e.tensor_tensor(out=v4(gs), in0=pm1v, in1=pastv, op=ALU.add), reads=[B_pm1, B_c2], writes=[B_gs])
                    for qt in range(4):
                        S.op("dve", lambda e, qt=qt: e.max(out=m8[:, qt * 8:(qt + 1) * 8], in_=gs[:, qt * NBK:(qt + 1) * NBK]), reads=[B_gs], pwrites=[B_m8])

                def s3(pastv=pastv, diagv=diagv):
                    S.op("dve", lambda e: e.tensor_tensor(out=g3(selb), in0=g3(gs), in1=thr, op=ALU.is_ge), reads=[B_gs, B_m8], writes=[B_selb])
                    S.op("dve", lambda e: e.tensor_scalar(out=selb[:, :], in0=selb[:, :], scalar1=BIG, scalar2=-BIG, op0=ALU.mult, op1=ALU.add), reads=[B_selb], writes=[B_selb])
                    S.op("dve", lambda e: e.tensor_tensor(out=v4(selb), in0=v4(selb), in1=pastv, op=ALU.min), reads=[B_selb, B_c2], writes=[B_selb])
                    S.op("dve", lambda e: e.tensor_tensor(out=v4(selb), in0=v4(selb), in1=diagv, op=ALU.max), reads=[B_selb, B_c2], writes=[B_selb])

                def s4():
                    for qt in range(4):
                        S.op("pe", lambda e, qt=qt: e.transpose(out=pm2[0:NBK, qt * 128:(qt + 1) * 128], in_=selb[:, qt * NBK:(qt + 1) * NBK], identity=identf[:, :]),
                             reads=[B_selb, B_const], pwrites=[B_pm2])

                def s5(p=p):
                    S.op("dve", lambda e: e.tensor_copy(out=selbTa[hb][0:NBK, p * 512:(p + 1) * 512], in_=pm2[0:NBK, :]), reads=[B_pm2], writes=[B_selbTa[hb][p]])

                def s45(s4=s4, s5=s5):
                    s4(); s5()
                steps += [s1, s2, s3, s45]
            return steps

        U = []
        import os as _os
        HLIM = int(_os.environ.get("A_HEADS", H))
        for h in range(HLIM):
            hb = h % 2
            for p in range(NP):
                keys = [(kt, kt // 2, None) for kt in range(NT)]
                for ko in range(4 * p + 4):
                    keys.append((NT + ko, NBH + ko // 2, (ko - 4 * p) if ko >= 4 * p else None))
                nk2 = len(keys) // 2
                for u in range(nk2):
                    U.append(dict(h=h, hb=hb, p=p, keys=keys[2 * u:2 * u + 2], first=(u == 0), last=(u == nk2 - 1), idx=len(U), uinpair=u, pairidx=h * NP + p))

        def do_ST(un):
            h, hb, p = un["h"], un["hb"], un["p"]
            si = un["idx"] % 2; pj = un["idx"] % NPT
            for half, (ktile, n, pidx) in enumerate(un["keys"]):
                osl = slice(half * 512, (half + 1) * 512)
                S.op("pe", lambda e, ktile=ktile, osl=osl: e.matmul(pSTw[si][:, osl], lhsT=kTh[hb][:, ktile * 128:(ktile + 1) * 128], rhs=qTh[hb][:, p * 512:(p + 1) * 512],
                                                                start=True, stop=False), reads=[B_kTh[hb], B_qTh[hb]], pwrites=[B_pST[si]])
                S.op("pe", lambda e, n=n, osl=osl: e.matmul(pSTw[si][:, osl], lhsT=oneh[:, n * 128:(n + 1) * 128], rhs=selbTa[hb][:, p * 512:(p + 1) * 512], start=False, stop=True),
                     reads=[B_c2, B_selbTa[hb][p]], pwrites=[B_pST[si]])
            S.op("act", lambda e: e.activation(out=PT2[pj][:, :], in_=pSTw[si][:, :], func=AF.Exp, scale=scale), reads=[B_pST[si]], writes=[B_PT[pj]])
            for half, (ktile, n, pidx) in enumerate(un["keys"]):
                if pidx is not None:
                    osl = slice(half * 512, (half + 1) * 512)
                    S.op("pool", lambda e, osl=osl, pidx=pidx: e.tensor_tensor(out=PT2[pj][:, osl], in0=PT2[pj][:, osl], in1=maskc[:, pidx * 512:(pidx + 1) * 512], op=ALU.mult),
                         reads=[B_PT[pj], B_c2], writes=[B_PT[pj]])

        deferred = {}

        def do_PV(un):
            h, hb, p = un["h"], un["hb"], un["p"]
            pj = un["idx"] % NPT
            pb = un["pairidx"] % 2
            for half, (ktile, n, pidx) in enumerate(un["keys"]):
                S.op("pe", lambda e, half=half, ktile=ktile: e.matmul(pacc_o[pb][:, :], lhsT=V1[hb][:, ktile, 0:128], rhs=PT2[pj][:, half * 512:(half + 1) * 512],
                                                                start=(un["first"] and half == 0), stop=(un["last"] and half == 1)),
                     reads=[B_PT[pj], B_V1[hb]], pwrites=[B_pacc_o[pb]])
            u = un["uinpair"]
            eng, Pa, BPa = ("dve", PaD[pb], B_PaD[pb])
            if u < 1:
                S.op(eng, lambda e: e.tensor_copy(out=Pa[:, :], in_=PT2[pj][:, :]), reads=[B_PT[pj]], writes=[BPa])
            else:
                S.op(eng, lambda e: e.tensor_tensor(out=Pa[:, :], in0=Pa[:, :], in1=PT2[pj][:, :], op=ALU.add), reads=[B_PT[pj], BPa], writes=[BPa])
            if un["last"]:
                def ep1(pb=pb):
                    S.op("dve", lambda e: e.tensor_tensor(out=psm[2][:, :], in0=PaD[pb][:, 0:512], in1=PaD[pb][:, 512:1024], op=ALU.add), reads=[B_PaD[pb]], writes=[B_psm[2]])
                    S.op("dve", lambda e: e.tensor_copy(out=phl[0][:, :], in_=psm[2][:, :]), reads=[B_psm[2]], writes=[B_phl])
                    S.op("dve", lambda e: e.tensor_tensor(out=phl[1][:, :], in0=psm[2][:, :], in1=phl[0][:, :], op=ALU.subtract), reads=[B_psm[2], B_phl], pwrites=[B_phl])

                def ep2(pb=pb, hb=hb, p=p, h=h):
                    S.op("pe", lambda e: e.matmul(pm2[0:1, 0:512], lhsT=onesc[:, 0:1], rhs=phl[0][:, :], start=True, stop=False), reads=[B_phl, B_c2], pwrites=[B_pm2])
                    S.op("pe", lambda e: e.matmul(pm2[0:1, 0:512], lhsT=onesc[:, 0:1], rhs=phl[1][:, :], start=False, stop=True), reads=[B_phl, B_c2], pwrites=[B_pm2])
                    S.op("dve", lambda e: e.reciprocal(out=rcr[:, :], in_=pm2[0:1, 0:512]), reads=[B_pm2], writes=[B_rcr])
                    S.op("dve", lambda e: e.tensor_copy(out=rhl[0][:, :], in_=rcr[:, :]), reads=[B_rcr], writes=[B_rhl])
                    S.op("dve", lambda e: e.tensor_tensor(out=rhl[1][:, :], in0=rcr[:, :], in1=rhl[0][:, :], op=ALU.subtract), reads=[B_rcr, B_rhl], pwrites=[B_rhl])
                    S.op("pe", lambda e: e.matmul(pm2[:, 0:512], lhsT=onesr[0:1, :], rhs=rhl[0][0:1, :], start=True, stop=False), reads=[B_rhl, B_c2], pwrites=[B_pm2])
                    S.op("pe", lambda e: e.matmul(pm2[:, 0:512], lhsT=onesr[0:1, :], rhs=rhl[1][0:1, :], start=False, stop=True), reads=[B_rhl, B_c2], pwrites=[B_pm2])
                    S.op("act", lambda e: e.activation(out=bcs[:, :], in_=pm2[:, 0:512], func=AF.Copy), reads=[B_pm2], writes=[B_bcs])
                    S.op("dve", lambda e: e.tensor_tensor(out=aTst[hb][:, p * 512:(p + 1) * 512], in0=pacc_o[pb][:, :], in1=bcs[:, :], op=ALU.mult),
                         reads=[B_pacc_o[pb], B_bcs], pwrites=[B_aTst[hb]])
                    if p == NP - 1:
                        S.op("sp", lambda e: e.dma_start(out=L["mixT_s"][h * 128:(h + 1) * 128, :], in_=aTst[hb][:, :]), reads=[B_aTst[hb]], pwrites=[B_scr["mixT"]], dma=True)
                deferred.setdefault(un["idx"] + 2, []).append(ep1)
                deferred.setdefault(un["idx"] + 4, []).append(ep2)

        head_loads(0)
        for stp in sel_steps(0):
            stp()
        pending = []
        per_head = len(U) // HLIM
        cast_stride = max(1, (len(U) * 3 // 4) // max(1, len(cast_jobs)))
        LOOK = int(_os.environ.get("LOOK", "2"))
        pos_in_head = 0
        stride = 1

        def after_pv(j):
            nonlocal pending, pos_in_head, stride
            for fn in deferred.pop(j, []):
                fn()

        def enter_head(hcur):
            nonlocal pending, pos_in_head, stride
            if hcur + 1 < HLIM:
                head_loads(hcur + 1)
                pending = sel_steps(hcur + 1)
                stride = max(1, (per_head - 8) // max(1, len(pending)))
                pos_in_head = 0

        for i, un in enumerate(U):
            if cast_jobs and i % cast_stride == 0:
                cast_jobs.pop(0)()
            do_ST(un)
            j = i - LOOK
            if j >= 0:
                do_PV(U[j])
                after_pv(j)
            if i == 0:
                enter_head(0)
            elif j >= 0 and U[j]["last"] and U[j]["p"] == NP - 1:
                enter_head(U[j]["h"] + 1)
            if pending:
                pos_in_head += 1
                nxt_new_head = (i + 1 < len(U)) and (U[i + 1]["h"] != un["h"])
                flush = nxt_new_head or (i + 1 == len(U))
                if pos_in_head % stride == 0 or flush:
                    nstep = len(pending) if flush else 1
                    for _ in range(nstep):
                        pending.pop(0)()
        for j in range(max(0, len(U) - LOOK), len(U)):
            do_PV(U[j])
            after_pv(j)
        for k_ in sorted(deferred):
            for fn in deferred[k_]:
                fn()
        while cast_jobs:
            cast_jobs.pop(0)()
        st2a = S.emit()
    print("phase2a", st2a, flush=True)
    import os as _os
    if _os.environ.get("STOP_AFTER") == "2":
        return

    with contextlib.ExitStack() as st:
        sb = lambda n, s, d=F32: st.enter_context(nc.sbuf_tensor(n, list(s), d))
        psb = lambda n, s, d=F32: st.enter_context(nc.psum_tensor(n, list(s), d))
        rq4 = [sb("rq4_%d" % i, [128, H, 512], BF16) for i in range(2)]; B_rq4 = [Buf() for _ in range(2)]
        rk4 = [sb("rk4_%d" % i, [128, H, 512], BF16) for i in range(2)]; B_rk4 = [Buf() for _ in range(2)]
        NKB = 3
        kd = [sb("kd%d" % i, [128, H * HD], BF16) for i in range(NKB)]; B_kd = [Buf() for _ in range(NKB)]
        rvt = [sb("rvt%d" % i, [128, H * HD], BF16) for i in range(NKB)]; B_rvt = [Buf() for _ in range(NKB)]
        NRG = 4
        rgt = [sb("rgt%d" % i, [128, H * HD], BF16) for i in range(NRG)]; B_rgt = [Buf() for _ in range(NRG)]
        decT = sb("decT", [128, H * 128]); grt = sb("grt", [128, H * HD]); B_c3 = Buf()
        Sb = sb("Sb", [128, H * HD], BF16); B_Sb = Buf()
        sTm = [sb("sTm%d" % i, [128, H, 128], BF16) for i in range(2)]; B_sTm = [Buf() for _ in range(2)]
        NOS = 3
        osb_ = [sb("osb%d" % i, [128, H * HD]) for i in range(NOS)]; B_osb_ = [Buf() for _ in range(NOS)]
        sq_ = [sb("sq2_%d" % i, [128, H * HD]) for i in range(2)]; B_sq_ = [Buf() for _ in range(2)]
        ssr_ = [sb("ssr%d" % i, [128, 16]) for i in range(2)]; B_ssr_ = [Buf() for _ in range(2)]
        rob = [sb("rob%d" % i, [128, H * HD], BF16) for i in range(NOS)]; B_rob = [Buf() for _ in range(NOS)]
        rTst = [sb("rTst%d" % i, [128, H, 512], BF16) for i in range(2)]; B_rTst = [Buf() for _ in range(2)]
        psT = [psb("psT%d" % i, [128, 512]) for i in range(2)]; B_psT = [Buf() for _ in range(2)]
        pO = [psb("pO%d" % i, [128, 512]) for i in range(2)]; B_pO = [Buf() for _ in range(2)]
        pU = [psb("pU2_%d" % i, [128, 512]) for i in range(2)]; B_pU = [Buf() for _ in range(2)]
        ptr = psb("ptr2", [128, H, 128], BF16); B_ptr = Buf()
        S.op("sp", lambda e: e.dma_start(out=decT[:, :], in_=L["c_decT"][:, :]), pwrites=[B_c3], dma=True)
        S.op("sp", lambda e: e.dma_start(out=grt[:, :], in_=L["g_ret"].ap().partition_broadcast(128)), pwrites=[B_c3], dma=True)

        def LD(t):
            g, tt = t // 4, t % 4
            gb = g % 2
            if tt == 0:
                S.op("sp", lambda e: e.dma_start(out=rq4[gb][:, :, :], in_=L["rqT_s"][:, :, g * 512:(g + 1) * 512].rearrange("h d t -> d h t")),
                     reads=[B_scr["rqT"]], writes=[B_rq4[gb]], dma=True)
                S.op("sp", lambda e: e.dma_start(out=rk4[gb][:, :, :], in_=L["rkT_s"][:, :, g * 512:(g + 1) * 512].rearrange("h d t -> d h t")),
                     reads=[B_scr["rkT"]], writes=[B_rk4[gb]], dma=True)
            for dst, Bd, src, nm, nb in ((kd, B_kd, "rkd_s", "rkd", NKB), (rvt, B_rvt, "rv_s", "rv", NKB), (rgt, B_rgt, "rgs_s", "rgs", NRG)):
                S.op("sp", lambda e, dst=dst, src=src, nb=nb: e.dma_start(out=dst[t % nb][:, :], in_=L[src][t * 128:(t + 1) * 128, :]),
                     reads=[B_scr[nm]], writes=[Bd[t % nb]], dma=True)

        def M1(t):
            g, tt = t // 4, t % 4
            gb = g % 2; tb = t % 2
            ts_ = slice(tt * 128, (tt + 1) * 128)
            for hh in range(H):
                bk, col = hh // 4, (hh % 4) * 128
                S.op("pe", lambda e, hh=hh, bk=bk, col=col: e.matmul(psT[bk][:, col:col + 128], lhsT=rk4[gb][:, hh, ts_], rhs=rq4[gb][:, hh, ts_], start=True, stop=True),
                     reads=[B_rk4[gb], B_rq4[gb]], pwrites=[B_psT[bk]])
            for bk in range(2):
                S.op("dve", lambda e, bk=bk: e.tensor_tensor(out=sTm[tb][:, bk * 4:(bk + 1) * 4, :], in0=psT[bk][:, :].rearrange("p (h t) -> p h t", h=4),
                                                        in1=decT[:, bk * 512:(bk + 1) * 512].rearrange("p (h t) -> p h t", h=4), op=ALU.mult),
                     reads=[B_psT[bk], B_c3], pwrites=[B_sTm[tb]])

        def M2(t):
            g, tt = t // 4, t % 4
            gb = g % 2; tb = t % 2; kb = t % NKB; ob_ = t % NOS
            ts_ = slice(tt * 128, (tt + 1) * 128)
            S.op("pool", lambda e: e.tensor_copy(out=Sb[:, :], in_=Sst[:, :]), reads=[B_S], writes=[B_Sb])
            for hh in range(H):
                bk, col = hh // 4, (hh % 4) * 128
                S.op("pe", lambda e, hh=hh, bk=bk, col=col: e.matmul(pO[bk][:, col:col + 128], lhsT=sTm[tb][:, hh, :], rhs=rvt[kb][:, hh * 128:(hh + 1) * 128], start=True, stop=False),
                     reads=[B_sTm[tb], B_rvt[kb]], pwrites=[B_pO[bk]])
                S.op("pe", lambda e, hh=hh, bk=bk, col=col: e.matmul(pO[bk][:, col:col + 128], lhsT=rq4[gb][:, hh, ts_], rhs=Sb[:, hh * 128:(hh + 1) * 128], start=False, stop=True),
                     reads=[B_rq4[gb], B_Sb], pwrites=[B_pO[bk]])
            for bk in range(2):
                S.op("dve", lambda e, bk=bk: e.tensor_tensor(out=osb_[ob_][:, bk * 512:(bk + 1) * 512].rearrange("p (h e) -> p h e", h=4), in0=pO[bk][:, :].rearrange("p (h e) -> p h e", h=4),
                                                        in1=bc_in(qdec_t, H, bk * 4, 4, 128), op=ALU.mult),
                     reads=[B_pO[bk], B_const], pwrites=[B_osb_[ob_]])

        def UU(t):
            kb = t % NKB
            for hh in range(H):
                bk, col = hh // 4, (hh % 4) * 128
                S.op("pe", lambda e, hh=hh, bk=bk, col=col: e.matmul(pU[bk][:, col:col + 128], lhsT=kd[kb][:, hh * 128:(hh + 1) * 128], rhs=rvt[kb][:, hh * 128:(hh + 1) * 128], start=True, stop=True),
                     reads=[B_kd[kb], B_rvt[kb]], pwrites=[B_pU[bk]])
            s3 = Sst[:, :].rearrange("p (h e) -> p h e", h=H)
            S.op("pool", lambda e: e.tensor_tensor(out=s3, in0=s3, in1=bc_in(gch_t, H, 0, 8, 128), op=ALU.mult), reads=[B_S, B_const], writes=[B_S])
            for bk in range(2):
                S.op("dve", lambda e, bk=bk: e.tensor_tensor(out=Sst[:, bk * 512:(bk + 1) * 512], in0=Sst[:, bk * 512:(bk + 1) * 512], in1=pU[bk][:, :], op=ALU.add),
                     reads=[B_S, B_pU[bk]], writes=[B_S])

        def NN(t):
            ob_ = t % NOS; sb_i = t % 2; rg_i = t % NRG
            osb, sq, ssr = osb_[ob_], sq_[sb_i], ssr_[sb_i]
            B_osb, B_sq, B_ssr = B_osb_[ob_], B_sq_[sb_i], B_ssr_[sb_i]
            S.op("pool", lambda e: e.tensor_tensor(out=sq[:, :], in0=osb[:, :], in1=osb[:, :], op=ALU.mult), reads=[B_osb], writes=[B_sq])
            S.op("dve", lambda e: e.tensor_reduce(out=ssr[:, 0:8], in_=sq[:, :].rearrange("p (h e) -> p h e", h=H), axis=AX.X, op=ALU.add), reads=[B_sq], writes=[B_ssr])
            S.op("pool", lambda e: e.tensor_scalar(out=ssr[:, 0:8], in0=ssr[:, 0:8], scalar1=1.0 / HD, scalar2=EPS, op0=ALU.mult, op1=ALU.add), reads=[B_ssr], writes=[B_ssr])
            S.op("pool", lambda e: e.tensor_tensor(out=ssr[:, 8:16], in0=ssr[:, 0:8], in1=mhalf[:, 0:8], op=ALU.pow), reads=[B_ssr, B_const], writes=[B_ssr])
            o3 = osb[:, :].rearrange("p (h e) -> p h e", h=H)
            S.op("dve", lambda e: e.tensor_tensor(out=o3, in0=o3, in1=bc_in(ssr, 16, 8, 8, 128), op=ALU.mult), reads=[B_osb, B_ssr], writes=[B_osb])
            S.op("pool", lambda e: e.tensor_tensor(out=osb[:, :], in0=osb[:, :], in1=grt[:, :], op=ALU.mult), reads=[B_osb, B_c3], writes=[B_osb])
            S.op("dve", lambda e: e.tensor_tensor(out=rob[ob_][:, :], in0=osb[:, :], in1=rgt[rg_i][:, :], op=ALU.mult), reads=[B_osb, B_rgt[rg_i]], writes=[B_rob[ob_]])

        def TT(t):
            g, tt = t // 4, t % 4
            gb = g % 2; ob_ = t % NOS
            ts_ = slice(tt * 128, (tt + 1) * 128)
            for hh in range(H):
                S.op("pe", lambda e, hh=hh: e.transpose(out=ptr[:, hh, :], in_=rob[ob_][:, hh * 128:(hh + 1) * 128], identity=identb[:, :]),
                     reads=[B_rob[ob_], B_const], pwrites=[B_ptr])
            S.op("act", lambda e: e.activation(out=rTst[gb][:, :, ts_], in_=ptr[:, :, :], func=AF.Copy), reads=[B_ptr], pwrites=[B_rTst[gb]])
            if tt == 3:
                S.op("sp", lambda e: e.dma_start(out=L["mixT_s"][H * HD:2 * H * HD, g * 512:(g + 1) * 512].rearrange("(h e) t -> e h t", h=H), in_=rTst[gb][:, :, :]),
                     reads=[B_rTst[gb]], pwrites=[B_scr["mixT"]], dma=True)

        NTL = NT
        LD(0)
        M1(0)
        for i in range(NTL + 2):
            if i + 1 < NTL:
                LD(i + 1)
                M1(i + 1)
            if i < NTL:
                M2(i)
                UU(i)
            if 0 <= i - 1 < NTL:
                NN(i - 1)
            if 0 <= i - 2 < NTL:
                TT(i - 2)
        if debug:
            S.op("sp", lambda e: e.dma_start(out=dbg["d_mixT"][:, :], in_=L["mixT_s"][:, :]), reads=[B_scr["mixT"]], dma=True)
        st2b = S.emit()
    print("phase2b", st2b, flush=True)

    with contextlib.ExitStack() as st:
        sb = lambda n, s, d=F32: st.enter_context(nc.sbuf_tensor(n, list(s), d))
        psb = lambda n, s, d=F32: st.enter_context(nc.psum_tensor(n, list(s), d))
        xres = sb("xres", [128, 4, D]); B_xres = [Buf() for _ in range(4)]
        mT = sb("mT", [128, KD, 512], BF16); B_mT = Buf()
        h2T = sb("h2T", [128, KD, 512], BF16); B_h2T = Buf()
        HK = KF // 2
        actT = sb("actT", [128, HK, 512], BF16); B_actT = Buf()
        ws = [sb("ws%d" % i, [128, KD, 256], BF16) for i in range(4)]; B_ws = [Buf() for _ in range(4)]
        wd = [sb("wd%d" % i, [128, 11, 512], BF16) for i in range(2)]; B_wd = [Buf() for _ in range(2)]
        wpp = sb("wpp", [128, 2, D], BF16); bpg_t = sb("bpg_t", [128, D]); B_c4 = Buf()
        pT = sb("pT", [128, 2, 512], BF16); B_pT = Buf()
        pin = [sb("pin%d" % i, [128, PLE]) for i in range(2)]; B_pin = [Buf() for _ in range(2)]
        pinb = [sb("pinb%d" % i, [128, PLE], BF16) for i in range(2)]; B_pinb = [Buf() for _ in range(2)]
        stmp = [sb("stmp%d" % i, [128, 512]) for i in range(2)]; B_stmp = [Buf() for _ in range(2)]
        gtmp = [sb("gtmp%d" % i, [128, 256]) for i in range(2)]; B_gtmp = [Buf() for _ in range(2)]
        xn = sb("xn3", [128, D], BF16); B_xn = Buf()
        st8 = [sb("st8b_%d" % i, [128, 8]) for i in range(2)]; B_st8 = [Buf() for _ in range(2)]
        pb = [psb("pb%d" % i, [128, 512]) for i in range(8)]; B_pb = [Buf() for _ in range(8)]
        pbb = [pb[i][:, :].bitcast(BF16) for i in range(8)]
        S.op("sp", lambda e: e.dma_start(out=wpp[:, :, :], in_=L["wb_pp"].ap().rearrange("(k p) n -> p k n", p=128)), reads=[B_w["pp"]], pwrites=[B_c4], dma=True)
        S.op("sp", lambda e: e.dma_start(out=bpg_t[:, :], in_=L["b_pg"].ap().partition_broadcast(128)), pwrites=[B_c4], dma=True)
        c3 = {"ws": 0, "wd": 0, "a": 0, "t": 0, "n": 0}

        def norm_T(gtab, dstT, B_dst):
            for tt in range(4):
                si = c3["n"] % 2; c3["n"] += 1
                S.op("act", lambda e, tt=tt, si=si: e.activation(out=xn[:, :], in_=xres[:, tt, :], func=AF.Square, accum_out=st8[si][:, 0:1]),
                     reads=[B_xres[tt]], writes=[B_xn, B_st8[si]])
                S.op("pool", lambda e, si=si: e.tensor_scalar(out=st8[si][:, 1:2], in0=st8[si][:, 0:1], scalar1=1.0 / D, scalar2=EPS, op0=ALU.mult, op1=ALU.add),
                     reads=[B_st8[si]], writes=[B_st8[si]])
                S.op("pool", lambda e, si=si: e.tensor_tensor(out=st8[si][:, 2:3], in0=st8[si][:, 1:2], in1=mhalf[:, 0:1], op=ALU.pow), reads=[B_st8[si], B_const], writes=[B_st8[si]])
                S.op("act", lambda e, tt=tt, si=si: e.activation(out=xn[:, :], in_=xres[:, tt, :], func=AF.Copy, scale=st8[si][:, 2:3]), reads=[B_xres[tt], B_st8[si]], writes=[B_xn])
                for k4 in range(4):
                    bi = 4 + (c3["t"] % 2); c3["t"] += 1
                    for kk in range(4):
                        k = k4 * 4 + kk
                        S.op("pe", lambda e, k=k, kk=kk, bi=bi: e.transpose(out=pbb[bi][:, kk * 128:(kk + 1) * 128], in_=xn[:, k * 128:(k + 1) * 128], identity=identb[:, :]),
                             reads=[B_xn, B_const], pwrites=[B_pb[bi]])
                    for kk in range(4):
                        k = k4 * 4 + kk
                        S.op("dve", lambda e, k=k, kk=kk, bi=bi, tt=tt: e.tensor_scalar(out=dstT[:, k, tt * 128:(tt + 1) * 128], in0=pbb[bi][:, kk * 128:(kk + 1) * 128],
                                                                                     scalar1=gtab[:, k:k + 1], scalar2=None, op0=ALU.mult),
                             reads=[B_pb[bi], B_const], pwrites=[B_dst])

        for g in range(NG):
            for tt in range(4):
                S.op("sp", lambda e, g=g, tt=tt: e.dma_start(out=xres[:, tt, :], in_=L["x_own"][(g * 4 + tt) * 128:(g * 4 + tt + 1) * 128, :]), writes=[B_xres[tt]], dma=True)
            S.op("sp", lambda e, g=g: e.dma_start(out=mT[:, :, :], in_=L["mixT_s"][:, g * 512:(g + 1) * 512].rearrange("(k p) t -> p k t", p=128)),
                 reads=[B_scr["mixT"]], writes=[B_mT], dma=True)
            for c in range(8):
                wi = c3["ws"] % 4; c3["ws"] += 1
                S.op("sp", lambda e, c=c, wi=wi: e.dma_start(out=ws[wi][:, :, :], in_=L["wb_o"].ap()[:, c * 256:(c + 1) * 256].rearrange("(k p) n -> p k n", p=128)),
                     reads=[B_w["o"]], writes=[B_ws[wi]], dma=True)
                for tt in range(4):
                    ai = c3["a"] % 4; c3["a"] += 1
                    for k in range(KD):
                        S.op("pe", lambda e, k=k, tt=tt, wi=wi, ai=ai: e.matmul(pb[ai][:, 0:256], lhsT=mT[:, k, tt * 128:(tt + 1) * 128], rhs=ws[wi][:, k, :], start=(k == 0), stop=(k == KD - 1)),
                             reads=[B_mT, B_ws[wi]], pwrites=[B_pb[ai]])
                    S.op("dve", lambda e, tt=tt, c=c, ai=ai: e.tensor_tensor(out=xres[:, tt, c * 256:(c + 1) * 256], in0=xres[:, tt, c * 256:(c + 1) * 256], in1=pb[ai][:, 0:256], op=ALU.add),
                         reads=[B_pb[ai], B_xres[tt]], writes=[B_xres[tt]])
            norm_T(gffn_t, h2T, B_h2T)
            for half in range(2):
                for js in range(HK // 2):
                    col0 = (half * HK + js * 2) * 128
                    wg_i = c3["ws"] % 4; c3["ws"] += 1
                    wu_i = c3["ws"] % 4; c3["ws"] += 1
                    S.op("sp", lambda e, col0=col0, wg_i=wg_i: e.dma_start(out=ws[wg_i][:, :, :], in_=L["wb_g"].ap()[:, col0:col0 + 256].rearrange("(k p) n -> p k n", p=128)),
                         reads=[B_w["g"]], writes=[B_ws[wg_i]], dma=True)
                    S.op("sp", lambda e, col0=col0, wu_i=wu_i: e.dma_start(out=ws[wu_i][:, :, :], in_=L["wb_u"].ap()[:, col0:col0 + 256].rearrange("(k p) n -> p k n", p=128)),
                         reads=[B_w["u"]], writes=[B_ws[wu_i]], dma=True)
                    for jj in range(2):
                        jl = js * 2 + jj
                        gi = (jl % 2) * 2; ui = gi + 1
                        for k in range(KD):
                            S.op("pe", lambda e, k=k, jj=jj, wg_i=wg_i, gi=gi: e.matmul(pb[gi][:, :], lhsT=ws[wg_i][:, k, jj * 128:(jj + 1) * 128], rhs=h2T[:, k, :], start=(k == 0), stop=(k == KD - 1)),
                                 reads=[B_ws[wg_i], B_h2T], pwrites=[B_pb[gi]])
                        for k in range(KD):
                            S.op("pe", lambda e, k=k, jj=jj, wu_i=wu_i, ui=ui: e.matmul(pb[ui][:, :], lhsT=ws[wu_i][:, k, jj * 128:(jj + 1) * 128], rhs=h2T[:, k, :], start=(k == 0), stop=(k == KD - 1)),
                                 reads=[B_ws[wu_i], B_h2T], pwrites=[B_pb[ui]])
                        sj = jl % 2
                        S.op("act", lambda e, gi=gi, sj=sj: e.activation(out=stmp[sj][:, :], in_=pb[gi][:, :], func=AF.Silu), reads=[B_pb[gi]], writes=[B_stmp[sj]])
                        S.op("dve", lambda e, ui=ui, sj=sj, jl=jl: e.tensor_tensor(out=actT[:, jl, :], in0=stmp[sj][:, :], in1=pb[ui][:, :], op=ALU.mult),
                             reads=[B_stmp[sj], B_pb[ui]], pwrites=[B_actT])
                for c in range(4):
                    for sub in range(2):
                        di = c3["wd"] % 2; c3["wd"] += 1
                        r0 = (half * HK + sub * 11) * 128
                        S.op("sp", lambda e, r0=r0, c=c, di=di: e.dma_start(out=wd[di][:, :, :], in_=L["wb_d"].ap()[r0:r0 + 11 * 128, c * 512:(c + 1) * 512].rearrange("(k p) n -> p k n", p=128)),
                             reads=[B_w["d"]], writes=[B_wd[di]], dma=True)
                        for tt in range(4):
                            for k in range(11):
                                S.op("pe", lambda e, k=k, tt=tt, sub=sub, di=di: e.matmul(pb[4 + tt][:, :], lhsT=actT[:, sub * 11 + k, tt * 128:(tt + 1) * 128], rhs=wd[di][:, k, :],
                                                                                         start=(sub == 0 and k == 0), stop=(sub == 1 and k == 10)),
                                     reads=[B_actT, B_wd[di]], pwrites=[B_pb[4 + tt]])
                    for tt in range(4):
                        S.op("dve", lambda e, tt=tt, c=c: e.tensor_tensor(out=xres[:, tt, c * 512:(c + 1) * 512], in0=xres[:, tt, c * 512:(c + 1) * 512], in1=pb[4 + tt][:, :], op=ALU.add),
                             reads=[B_pb[4 + tt], B_xres[tt]], writes=[B_xres[tt]])
            norm_T(gple_t, mT, B_mT)
            for tt in range(4):
                pi = tt % 2
                S.op("sp", lambda e, g=g, tt=tt, pi=pi: e.dma_start(out=pin[pi][:, :], in_=L["p_own"][(g * 4 + tt) * 128:(g * 4 + tt + 1) * 128, :]), writes=[B_pin[pi]], dma=True)
                S.op("pool", lambda e, pi=pi: e.tensor_copy(out=pinb[pi][:, :], in_=pin[pi][:, :]), reads=[B_pin[pi]], writes=[B_pinb[pi]])
                for kp in range(2):
                    S.op("pe", lambda e, kp=kp, pi=pi: e.transpose(out=pbb[4][:, kp * 128:(kp + 1) * 128], in_=pinb[pi][:, kp * 128:(kp + 1) * 128], identity=identb[:, :]),
                         reads=[B_pinb[pi], B_const], pwrites=[B_pb[4]])
                S.op("dve", lambda e, tt=tt: e.tensor_copy(out=pT[:, :, tt * 128:(tt + 1) * 128], in_=pbb[4][:, 0:256].rearrange("p (k t) -> p k t", k=2)), reads=[B_pb[4]], pwrites=[B_pT])
            for c in range(8):
                wi = c3["ws"] % 4; c3["ws"] += 1
                S.op("sp", lambda e, c=c, wi=wi: e.dma_start(out=ws[wi][:, :, :], in_=L["wb_pg"].ap()[:, c * 256:(c + 1) * 256].rearrange("(k p) n -> p k n", p=128)),
                     reads=[B_w["pg"]], writes=[B_ws[wi]], dma=True)
                for tt in range(4):
                    ai = (c3["a"] % 2) * 2; c3["a"] += 1
                    bi_ = ai + 1
                    for k in range(KD):
                        S.op("pe", lambda e, k=k, tt=tt, wi=wi, ai=ai: e.matmul(pb[ai][:, 0:256], lhsT=mT[:, k, tt * 128:(tt + 1) * 128], rhs=ws[wi][:, k, :], start=(k == 0), stop=(k == KD - 1)),
                             reads=[B_mT, B_ws[wi]], pwrites=[B_pb[ai]])
                    for kp in range(2):
                        S.op("pe", lambda e, kp=kp, tt=tt, c=c, bi_=bi_: e.matmul(pb[bi_][:, 0:256], lhsT=pT[:, kp, tt * 128:(tt + 1) * 128], rhs=wpp[:, kp, c * 256:(c + 1) * 256], start=(kp == 0), stop=(kp == 1)),
                             reads=[B_pT, B_c4], pwrites=[B_pb[bi_]])
                    gj = tt % 2
                    S.op("dve", lambda e, ai=ai, c=c, gj=gj: e.tensor_tensor(out=gtmp[gj][:, :], in0=pb[ai][:, 0:256], in1=bpg_t[:, c * 256:(c + 1) * 256], op=ALU.add),
                         reads=[B_pb[ai], B_c4], writes=[B_gtmp[gj]])
                    S.op("act", lambda e, gj=gj: e.activation(out=gtmp[gj][:, :], in_=gtmp[gj][:, :], func=AF.Sigmoid), reads=[B_gtmp[gj]], writes=[B_gtmp[gj]])
                    S.op("dve", lambda e, bi_=bi_, gj=gj: e.tensor_tensor(out=gtmp[gj][:, :], in0=pb[bi_][:, 0:256], in1=gtmp[gj][:, :], op=ALU.mult),
                         reads=[B_pb[bi_], B_gtmp[gj]], writes=[B_gtmp[gj]])
                    S.op("pool", lambda e, tt=tt, c=c, gj=gj: e.tensor_tensor(out=xres[:, tt, c * 256:(c + 1) * 256], in0=xres[:, tt, c * 256:(c + 1) * 256], in1=gtmp[gj][:, :], op=ALU.add),
                         reads=[B_gtmp[gj], B_xres[tt]], writes=[B_xres[tt]])
            for tt in range(4):
                S.op("sp", lambda e, g=g, tt=tt: e.dma_start(out=L["out"][(g * 4 + tt) * 128:(g * 4 + tt + 1) * 128, :], in_=xres[:, tt, :]), reads=[B_xres[tt]], dma=True)
        st3 = S.emit()
    print("phase3", st3, flush=True)


def _tables(NBH, s):
    TH = NBH * 256
    NBK = 2 * NBH
    f32 = np.float32
    scale = f32(1.0 / np.sqrt(HD))
    inv_a = np.power(f32(10000.0), -(np.arange(0, HD, 2, dtype=f32) / f32(HD))).astype(f32)
    inv_r = np.power(f32(10000.0), -np.linspace(0.0, 1.0, HD // 2, dtype=f32)).astype(f32)

    def rope_tab(pos, inv):
        ang = (pos.astype(f32)[:, None] * inv[None, :]).astype(f32)
        return np.concatenate([np.cos(ang), np.sin(ang)], axis=1).astype(f32)

    pos_pre = np.arange(TH)
    pos_own = s * TH + np.arange(TH)
    t = {}
    t["rope_a_pre"] = rope_tab(pos_pre, inv_a); t["rope_a_own"] = rope_tab(pos_own, inv_a)
    t["rope_r_pre"] = rope_tab(pos_pre, inv_r); t["rope_r_own"] = rope_tab(pos_own, inv_r)
    gam = (1.0 - np.power(2.0, -5.0 - np.arange(H, dtype=np.float64)))
    lg = np.log(gam)
    j = np.arange(128, dtype=np.float64)
    t["c_kdec"] = (float(scale) * np.exp(lg[None, :] * (127.0 - j)[:, None])).astype(f32)
    t["c_qdec"] = np.exp(lg[None, :] * (j + 1.0)[:, None]).astype(f32)
    t["c_gch"] = np.broadcast_to(np.exp(lg * 128.0)[None, :], (128, H)).astype(f32).copy()
    m = j[:, None, None]; tq = j[None, None, :]
    dec = float(scale) * np.exp(-lg[None, :, None] * (m + 1.0)) * (m <= tq)
    t["c_decT"] = dec.astype(f32).reshape(128, H * 128)
    past = np.full((NBH, NBK), NEGP, dtype=f32)
    diag = np.full((NBH, NBK), -3.0e38, dtype=f32)
    for i in range(NBH):
        if s == 1:
            past[i, :NBH] = 0.0
        past[i, NBH:NBH + i] = 0.0
        diag[i, NBH + i] = 0.0
    t["c_past"] = np.broadcast_to(past.reshape(1, -1), (128, NBH * NBK)).copy()
    t["c_diag"] = np.broadcast_to(diag.reshape(1, -1), (128, NBH * NBK)).copy()
    tk = np.arange(128)[:, None, None]; pp = np.arange(4)[None, :, None]; cc = np.arange(512)[None, None, :]
    t["c_mask"] = ((pp * 128 + tk) <= cc).astype(f32).reshape(128, 4 * 512).astype(ml_dtypes.bfloat16)
    oh = np.zeros((128, NBK, 128), dtype=f32)
    for n in range(NBK):
        oh[n, n, :] = 1.0
    t["c_onehot"] = oh.reshape(128, NBK * 128).astype(ml_dtypes.bfloat16)
    t["c_identb"] = np.eye(128, dtype=f32).astype(ml_dtypes.bfloat16)
    t["c_identf"] = np.eye(128, dtype=f32)
    return t


def make_in_maps(inputs, NBH, n_batch):
    TH = NBH * 256
    x = np.asarray(inputs["x"], dtype=np.float32)
    p = np.asarray(inputs["p"], dtype=np.float32)
    shared = {
        "w_in": inputs["w_in"][0], "w_o": inputs["w_o"][0], "w_gate": inputs["w_gate"][0], "w_up": inputs["w_up"][0],
        "w_down": inputs["w_down"][0], "w_pg": inputs["w_ple_gate"][0], "w_pp": inputs["w_ple_proj"][0],
        "g_mix": inputs["g_mix"][0], "g_ffn": inputs["g_ffn"][0], "g_ple": inputs["g_ple"][0],
        "q_norm": inputs["q_norm"][0], "k_norm": inputs["k_norm"][0], "g_ret": inputs["g_ret"][0],
        "b_pg": inputs["b_ple_gate"][0],
    }
    shared = {k: np.ascontiguousarray(np.asarray(v, dtype=np.float32)) for k, v in shared.items()}
    tabs = [_tables(NBH, 0), _tables(NBH, 1)]
    maps = []
    for c in range(2 * n_batch):
        b, s = c // 2, c % 2
        m = dict(shared)
        m.update(tabs[s])
        m["x_own"] = np.ascontiguousarray(x[b, s * TH:(s + 1) * TH])
        m["x_pre"] = np.ascontiguousarray(x[b, 0:TH]) if s == 1 else np.zeros((TH, D), np.float32)
        m["p_own"] = np.ascontiguousarray(p[0, b, s * TH:(s + 1) * TH])
        maps.append(m)
    return maps


_NC_CACHE = {}


def kernel(**inputs):
    x = np.asarray(inputs["x"])
    B, T, _ = x.shape
    NBH = T // 512
    key = (NBH,)
    if key not in _NC_CACHE:
        _NC_CACHE[key] = build(NBH)
    nc = _NC_CACHE[key]
    maps = make_in_maps(inputs, NBH, B)
    res = run_bass_kernel_spmd(nc, maps, core_ids=list(range(2 * B)))
    TH = NBH * 256
    out = np.empty((B, T, D), np.float32)
    for c in range(2 * B):
        b, s = c // 2, c % 2
        out[b, s * TH:(s + 1) * TH] = np.asarray(res.results[c]["out"], dtype=np.float32)
    return out
```

```python
import numpy as np
import contextlib
import ml_dtypes
import concourse.bass as bass
import concourse.mybir as mybir
from concourse.bass_utils import run_bass_kernel_spmd
from concourse.alu_op_type import AluOpType as ALU

F32 = mybir.dt.float32
BF16 = mybir.dt.bfloat16
AF = mybir.ActivationFunctionType
AX = mybir.AxisListType

D = 2048
KD = 16
HD = 128
H = 8
INC = 7168
DFF = 5632
KF = 44
PLE = 256
EPS = 1e-6
BIG = 30000.0
NEGP = -1.0e9

N_DMA_SEMS = 32
N_SW_SEMS = 8
ENGS = ("pe", "act", "dve", "pool", "sp")


class Buf:
    __slots__ = ("w", "r", "pr", "name")

    def __init__(self, name=""):
        self.w = {}
        self.r = {}
        self.pr = set()
        self.name = name


class Op:
    __slots__ = ("eng", "fn", "deps", "idx", "pos", "inc", "tick", "is_dma", "dsem", "dval")


class Sched:
    def __init__(self, nc, stack):
        self.nc = nc
        self.sem = {e: stack.enter_context(nc.semaphore("s_" + e)) for e in ENGS if e != "sp"}
        self.dsems = [stack.enter_context(nc.semaphore("d%d" % i)) for i in range(N_DMA_SEMS)]
        self.drr_sw = 0
        self.tickbase = {e: 0 for e in self.sem}
        self.dcount = [0] * N_DMA_SEMS
        self.drr = 0
        self.waited = {e: {} for e in ENGS}
        self.ops = []
        self.pos = {e: 0 for e in ENGS}
        self.phase_dma = {}
        self.touched = set()
        self.lastop = {}

    def op(self, eng, fn, reads=(), writes=(), pwrites=(), dma=False):
        o = Op()
        o.eng = eng
        o.fn = fn
        o.idx = len(self.ops)
        o.is_dma = dma
        o.inc = False
        o.tick = 0
        o.pos = self.pos[eng]
        self.pos[eng] += 1
        deps = set()
        for b in reads:
            deps.update(b.w.values())
        for b in writes:
            deps.update(b.r.values())
            deps.update(b.w.values())
        for b in pwrites:
            deps.update(b.r.values())
            if b.r:
                deps.update(b.w.values())
            else:
                deps.update(b.pr)
        deps.discard(o.idx)
        o.deps = deps
        if dma:
            if eng == "pool":
                s = N_DMA_SEMS - N_SW_SEMS + self.drr_sw
                self.drr_sw = (self.drr_sw + 1) % N_SW_SEMS
            else:
                s = self.drr
                self.drr = (self.drr + 1) % (N_DMA_SEMS - N_SW_SEMS)
            if s in self.phase_dma:
                deps.add(self.phase_dma[s])
            self.dcount[s] += 1
            o.dsem = s
            o.dval = 16 * self.dcount[s]
            self.phase_dma[s] = o.idx
        key = ("dma", o.idx) if dma else eng
        if not dma:
            self.lastop[eng] = o.idx
        for b in reads:
            self.touched.add(b)
            b.r[key] = o.idx
        for b in writes:
            self.touched.add(b)
            b.pr = set(b.r.values()) | set(b.w.values())
            b.w = {key: o.idx}
            b.r = {}
        for b in pwrites:
            self.touched.add(b)
            if b.r:
                b.pr = set(b.r.values()) | set(b.w.values())
                b.w = {key: o.idx}
                b.r = {}
            else:
                b.w[key] = o.idx
        self.ops.append(o)
        return o

    def _needs(self, c, p):
        if p.is_dma:
            return True
        if p.eng == c.eng:
            if c.eng == "pe":
                return False
            if c.is_dma:
                return True
            return True
        return True

    def emit(self):
        nc = self.nc
        ops = self.ops
        f = Op()
        f.eng = "sp"; f.fn = None; f.idx = len(ops); f.is_dma = False
        f.inc = False; f.tick = 0; f.pos = self.pos["sp"]
        f.deps = set(self.phase_dma.values())
        for e, i in self.lastop.items():
            if e != "sp":
                f.deps.add(i)
        ops.append(f)
        for o in ops:
            for d in o.deps:
                p = ops[d]
                if (not p.is_dma) and self._needs(o, p):
                    p.inc = True
        cnt = dict(self.tickbase)
        for o in ops:
            if (not o.is_dma) and o.inc:
                cnt[o.eng] += 1
                o.tick = cnt[o.eng]
        per = {e: [] for e in ENGS}
        for o in ops:
            per[o.eng].append(o)
        waited = self.waited
        sem = self.sem
        dsems = self.dsems

        def run(e, eng):
            wd = waited[e]
            for o in per[e]:
                for d in sorted(o.deps):
                    p = ops[d]
                    if not self._needs(o, p):
                        continue
                    if p.is_dma:
                        k = ("d", p.dsem)
                        if wd.get(k, 0) < p.dval:
                            eng.wait_ge(dsems[p.dsem], p.dval)
                            wd[k] = p.dval
                    else:
                        k = p.eng
                        if wd.get(k, 0) < p.tick:
                            eng.wait_ge(sem[p.eng], p.tick)
                            wd[k] = p.tick
                if o.fn is None:
                    continue
                inst = o.fn(eng)
                if o.is_dma:
                    inst.then_inc(dsems[o.dsem], 16)
                elif o.inc:
                    inst.then_inc(sem[o.eng], 1)

        with nc.Block() as block:
            @block.tensor
            def _(eng):
                run("pe", eng)

            @block.scalar
            def _(eng):
                run("act", eng)

            @block.vector
            def _(eng):
                run("dve", eng)

            @block.gpsimd
            def _(eng):
                run("pool", eng)

            @block.sync
            def _(eng):
                run("sp", eng)

        self.tickbase = cnt
        self.ops = []
        self.pos = {e: 0 for e in ENGS}
        self.phase_dma = {}
        for b in self.touched:
            b.w = {}
            b.r = {}
            b.pr = set()
        self.touched = set()
        self.lastop = {}
        return {e: len(per[e]) for e in ENGS}


def bc_mid(t, row, off, n_mid, n_in):
    return bass.AP(t, off, [[row, 128], [0, n_mid], [1, n_in]])


def bc_in(t, row, off, n_mid, n_in):
    return bass.AP(t, off, [[row, 128], [1, n_mid], [0, n_in]])


def build(NBH, debug=False):
    TH = NBH * 256
    NT = TH // 128
    NG = TH // 512
    NBK = 2 * NBH
    TT = 2 * TH
    scale = 1.0 / np.sqrt(HD)

    nc = bass.Bass("TRN2", target_bir_lowering=False)
    din = lambda n, s, d=F32: nc.dram_tensor(n, list(s), d, kind="ExternalInput")
    x_pre = din("x_pre", [TH, D]); x_own = din("x_own", [TH, D]); p_own = din("p_own", [TH, PLE])
    w_in = din("w_in", [D, INC]); w_o = din("w_o", [D, D]); w_gate = din("w_gate", [D, DFF])
    w_up = din("w_up", [D, DFF]); w_down = din("w_down", [DFF, D]); w_pg = din("w_pg", [D, D])
    w_pp = din("w_pp", [PLE, D])
    g_mix = din("g_mix", [D]); g_ffn = din("g_ffn", [D]); g_ple = din("g_ple", [D])
    q_norm = din("q_norm", [HD]); k_norm = din("k_norm", [HD]); g_ret = din("g_ret", [H * HD])
    b_pg = din("b_pg", [D])
    rope_a_pre = din("rope_a_pre", [TH, 128]); rope_a_own = din("rope_a_own", [TH, 128])
    rope_r_pre = din("rope_r_pre", [TH, 128]); rope_r_own = din("rope_r_own", [TH, 128])
    c_kdec = din("c_kdec", [128, H]); c_qdec = din("c_qdec", [128, H]); c_gch = din("c_gch", [128, H])
    c_decT = din("c_decT", [128, H * 128])
    c_past = din("c_past", [128, NBH * NBK]); c_diag = din("c_diag", [128, NBH * NBK])
    c_mask = din("c_mask", [128, 4 * 512], BF16)
    c_onehot = din("c_onehot", [128, NBK * 128], BF16)
    c_identb = din("c_identb", [128, 128], BF16); c_identf = din("c_identf", [128, 128])
    out = nc.dram_tensor("out", [TH, D], F32, kind="ExternalOutput")

    dscr = lambda n, s, d=BF16: nc.dram_tensor(n, list(s), d)
    wb_in = dscr("wb_in", [D, INC]); wb_o = dscr("wb_o", [D, D]); wb_g = dscr("wb_g", [D, DFF])
    wb_u = dscr("wb_u", [D, DFF]); wb_d = dscr("wb_d", [DFF, D]); wb_pg = dscr("wb_pg", [D, D])
    wb_pp = dscr("wb_pp", [PLE, D])
    kT_s = dscr("kT_s", [H, 128, TT]); v_s = dscr("v_s", [TT, H * HD]); qT_s = dscr("qT_s", [H, 128, TH])
    rqT_s = dscr("rqT_s", [H, 128, TH]); rkT_s = dscr("rkT_s", [H, 128, TH])
    rkd_s = dscr("rkd_s", [TH, H * HD]); rv_s = dscr("rv_s", [TH, H * HD]); rgs_s = dscr("rgs_s", [TH, H * HD])
    mixT_s = dscr("mixT_s", [D, TH])
    dbg = {}
    if debug:
        for n, s in (("d_qT", [H, 128, TH]), ("d_kT", [H, 128, TT]), ("d_v", [TT, H * HD]), ("d_mixT", [D, TH]),
                     ("d_rqT", [H, 128, TH]), ("d_rkd", [TH, H * HD])):
            dbg[n] = nc.dram_tensor(n, s, BF16, kind="ExternalOutput")

    with contextlib.ExitStack() as gst:
        S = Sched(nc, gst)
        gsb = lambda n, s, d=F32: gst.enter_context(nc.sbuf_tensor(n, list(s), d))
        identb = gsb("identb", [128, 128], BF16); identf = gsb("identf", [128, 128])
        gmix_t = gsb("gmix_t", [128, KD]); gffn_t = gsb("gffn_t", [128, KD]); gple_t = gsb("gple_t", [128, KD])
        qn_t = gsb("qn_t", [128, HD]); kn_t = gsb("kn_t", [128, HD])
        kdec_t = gsb("kdec_t", [128, H]); qdec_t = gsb("qdec_t", [128, H]); gch_t = gsb("gch_t", [128, H])
        mhalf = gsb("mhalf", [128, 8])
        kmT = gsb("kmT", [128, H * NBK])
        Sst = gsb("Sst", [128, H * 128])
        B_const = Buf("const"); B_kmT = Buf("kmT"); B_S = Buf("S")
        B_w = {n: Buf(n) for n in ("in", "o", "g", "u", "d", "pg", "pp")}
        B_scr = {n: Buf(n) for n in ("kT", "v", "qT", "rqT", "rkT", "rkd", "rv", "rgs", "mixT")}

        def cast_w(src, dst, R, C, buf):
            for r0 in range(0, R, 1024):
                rr = min(1024, R - r0)
                for c0 in range(0, C, 2048):
                    cc = min(2048, C - c0)
                    S.op("pool", lambda e, r0=r0, rr=rr, c0=c0, cc=cc: e.dma_start(
                        out=dst[r0:r0 + rr, c0:c0 + cc], in_=src[r0:r0 + rr, c0:c0 + cc]),
                        pwrites=[buf], dma=True)

        cast_w(w_in, wb_in, D, INC, B_w["in"])
        ld = lambda o_, i_, **kw: S.op("sp", lambda e: e.dma_start(out=o_, in_=i_, **kw), pwrites=[B_const], dma=True)
        ld(identb[:, :], c_identb[:, :]); ld(identf[:, :], c_identf[:, :])
        for t_, g_ in ((gmix_t, g_mix), (gffn_t, g_ffn), (gple_t, g_ple)):
            ld(t_[:, :], g_.ap().rearrange("(k p) -> p k", p=128), allow_slow_non_contiguous=True)
        ld(qn_t[:, :], q_norm.ap().partition_broadcast(128)); ld(kn_t[:, :], k_norm.ap().partition_broadcast(128))
        ld(kdec_t[:, :], c_kdec[:, :]); ld(qdec_t[:, :], c_qdec[:, :]); ld(gch_t[:, :], c_gch[:, :])
        S.op("pool", lambda e: e.memset(mhalf[:, :], -0.5), pwrites=[B_const])
        S.op("pool", lambda e: e.memset(Sst[:, :], 0.0), writes=[B_S])
        S.op("pool", lambda e: e.memset(kmT[:, :], 0.0), writes=[B_kmT])
        st0 = S.emit()

        with contextlib.ExitStack() as st:
            sb = lambda n, s, d=F32: st.enter_context(nc.sbuf_tensor(n, list(s), d))
            psb = lambda n, s, d=F32: st.enter_context(nc.psum_tensor(n, list(s), d))
            NSL = 3
            wsl = [sb("wsl%d" % i, [128, KD, 512], BF16) for i in range(NSL)]; B_wsl = [Buf() for _ in range(NSL)]
            xt = [sb("xt%d" % i, [128, D]) for i in range(2)]; B_xt = [Buf() for _ in range(2)]
            xn = [sb("xn%d" % i, [128, D], BF16) for i in range(2)]; B_xn = [Buf() for _ in range(2)]
            st8 = [sb("st8_%d" % i, [128, 8]) for i in range(2)]; B_st8 = [Buf() for _ in range(2)]
            hT = [sb("hT%d" % i, [128, KD, 512], BF16) for i in range(2)]; B_hT = [Buf() for _ in range(2)]
            NTMP = 6
            tmp = [[sb("tmp%d_%d" % (i, j), [128, 512]) for j in range(3)] for i in range(NTMP)]
            B_tmp = [[Buf() for j in range(3)] for i in range(NTMP)]
            sm = [sb("sm%d" % i, [128, 8]) for i in range(NTMP)]; B_sm = [Buf() for _ in range(NTMP)]
            NOB = 8
            ob = [sb("ob%d" % i, [128, 512], BF16) for i in range(NOB)]; B_ob = [Buf() for _ in range(NOB)]
            NSTG = 3
            stage = [sb("stage%d" % i, [128, 4, 512], BF16) for i in range(NSTG)]; B_stage = [Buf() for _ in range(NSTG)]
            kdh = sb("kdh", [128, 4, H * HD], BF16); B_kdh = [Buf() for _ in range(4)]
            NVO = 4
            vob = [sb("vob%d" % i, [128, 512], BF16) for i in range(NVO)]; B_vob = [Buf() for _ in range(NVO)]
            NKO = 8
            kdo = [sb("kdo%d" % i, [128, 512], BF16) for i in range(NKO)]; B_kdo = [Buf() for _ in range(NKO)]
            ropa = [sb("ropa%d" % i, [128, 4, 128]) for i in range(2)]; B_ropa = [Buf() for _ in range(2)]
            ropr = [sb("ropr%d" % i, [128, 4, 128]) for i in range(2)]; B_ropr = [Buf() for _ in range(2)]
            NACC = 4
            pacc = [psb("pacc%d" % i, [128, 512]) for i in range(NACC)]; B_pacc = [Buf() for _ in range(NACC)]
            ptx = psb("ptx", [128, 8, 128], BF16); B_ptx = Buf()
            ptq = psb("ptq", [128, 8, 128], BF16); B_ptq = Buf()
            pU = [psb("pU%d" % i, [128, 512]) for i in range(2)]; B_pU = [Buf() for _ in range(2)]

            cast_w(w_o, wb_o, D, D, B_w["o"]); cast_w(w_pg, wb_pg, D, D, B_w["pg"]); cast_w(w_pp, wb_pp, PLE, D, B_w["pp"])

            ctr = {"tile": 0, "slab": 0, "pp": 0, "acc": 0, "vo": 0, "stg": 0, "ko": 0, "ob": 0}

            def rope(T_, BT_, tab, Btab, outb, Bout, tt, eng):
                zc = T_[0]; Bz = BT_[0]
                z1 = bass.AP(zc, 0, [[512, 128], [128, 4], [1, 64]])
                z2 = bass.AP(zc, 64, [[512, 128], [128, 4], [1, 64]])
                cs = bc_mid(tab, 512, tt * 128, 4, 64); sn = bc_mid(tab, 512, tt * 128 + 64, 4, 64)
                v4 = lambda t_, off: bass.AP(t_, off, [[512, 128], [64, 4], [1, 64]])
                a, b_, c_, d_ = v4(T_[1], 0), v4(T_[1], 256), v4(T_[2], 0), v4(T_[2], 256)
                o1 = bass.AP(outb, 0, [[512, 128], [128, 4], [1, 64]])
                o2 = bass.AP(outb, 64, [[512, 128], [128, 4], [1, 64]])
                S.op(eng, lambda e: e.tensor_tensor(out=a, in0=z1, in1=cs, op=ALU.mult), reads=[Bz, Btab], writes=[BT_[1]])
                S.op(eng, lambda e: e.tensor_tensor(out=b_, in0=z2, in1=sn, op=ALU.mult), reads=[Bz, Btab], pwrites=[BT_[1]])
                S.op(eng, lambda e: e.tensor_tensor(out=c_, in0=z2, in1=cs, op=ALU.mult), reads=[Bz, Btab], writes=[BT_[2]])
                S.op(eng, lambda e: e.tensor_tensor(out=d_, in0=z1, in1=sn, op=ALU.mult), reads=[Bz, Btab], pwrites=[BT_[2]])
                S.op(eng, lambda e: e.tensor_tensor(out=o1, in0=a, in1=b_, op=ALU.subtract), reads=[BT_[1]], pwrites=[Bout])
                S.op(eng, lambda e: e.tensor_tensor(out=o2, in0=c_, in1=d_, op=ALU.add), reads=[BT_[2]], pwrites=[Bout])

            seq = [(g, False) for g in range(NG)] + [(g, True) for g in range(NG)]

            def prologue_steps(q):
                g, own = seq[q]
                hb = q % 2
                xsrc = x_own if own else x_pre
                ra_src = rope_a_own if own else rope_a_pre
                rr_src = rope_r_own if own else rope_r_pre
                steps = []
                for tt in range(4):
                    t = g * 4 + tt
                    ti = ctr["tile"] % 2; ctr["tile"] += 1

                    def sa(tt=tt, t=t, ti=ti):
                        if tt == 0:
                            S.op("sp", lambda e: e.dma_start(out=ropa[hb][:, :, :], in_=ra_src[g * 512:(g + 1) * 512, :].rearrange("(t p) c -> p t c", p=128)), writes=[B_ropa[hb]], dma=True)
                            S.op("sp", lambda e: e.dma_start(out=ropr[hb][:, :, :], in_=rr_src[g * 512:(g + 1) * 512, :].rearrange("(t p) c -> p t c", p=128)), writes=[B_ropr[hb]], dma=True)
                        S.op("sp", lambda e: e.dma_start(out=xt[ti][:, :], in_=xsrc[t * 128:(t + 1) * 128, :]), writes=[B_xt[ti]], dma=True)
                        S.op("act", lambda e: e.activation(out=xn[ti][:, :], in_=xt[ti][:, :], func=AF.Square, accum_out=st8[ti][:, 0:1]),
                             reads=[B_xt[ti]], writes=[B_xn[ti], B_st8[ti]])
                        S.op("pool", lambda e: e.tensor_scalar(out=st8[ti][:, 1:2], in0=st8[ti][:, 0:1], scalar1=1.0 / D, scalar2=EPS, op0=ALU.mult, op1=ALU.add),
                             reads=[B_st8[ti]], writes=[B_st8[ti]])
                        S.op("pool", lambda e: e.tensor_tensor(out=st8[ti][:, 2:3], in0=st8[ti][:, 1:2], in1=mhalf[:, 0:1], op=ALU.pow),
                             reads=[B_st8[ti], B_const], writes=[B_st8[ti]])
                        S.op("act", lambda e: e.activation(out=xn[ti][:, :], in_=xt[ti][:, :], func=AF.Copy, scale=st8[ti][:, 2:3]),
                             reads=[B_xt[ti], B_st8[ti]], writes=[B_xn[ti]])

                    def sb_(tt=tt, ti=ti):
                        for k4 in range(4):
                            for kk in range(4):
                                k = k4 * 4 + kk
                                S.op("pe", lambda e, k=k, kk=kk: e.transpose(out=ptx[:, kk, :], in_=xn[ti][:, k * 128:(k + 1) * 128], identity=identb[:, :]),
                                     reads=[B_xn[ti], B_const], pwrites=[B_ptx])
                            for kk in range(4):
                                k = k4 * 4 + kk
                                S.op("dve", lambda e, k=k, kk=kk: e.tensor_scalar(out=hT[hb][:, k, tt * 128:(tt + 1) * 128], in0=ptx[:, kk, :],
                                                                               scalar1=gmix_t[:, k:k + 1], scalar2=None, op0=ALU.mult),
                                     reads=[B_ptx, B_const], pwrites=[B_hT[hb]])
                    steps.append(sa); steps.append(sb_)
                return steps

            units = []
            extra = {}
            chunk_list = []

            def slab_load(j):
                q, c = chunk_list[j]
                si = j % NSL
                S.op("sp", lambda e: e.dma_start(out=wsl[si][:, :, :], in_=wb_in.ap()[:, c * 512:(c + 1) * 512].rearrange("(k p) n -> p k n", p=128)),
                     reads=[B_w["in"]], writes=[B_wsl[si]], dma=True)

            for q, (g, own) in enumerate(seq):
                for c in (list(range(14)) if own else [2, 3, 4, 5, 8, 9, 10, 11]):
                    chunk_list.append((q, c))
            first_unit_of_group = {}
            for j, (q, c) in enumerate(chunk_list):
                g, own = seq[q]
                hb = q % 2
                tokoff = TH if own else 0
                typ = c // 2
                hc = c % 2
                si = j % NSL
                if q not in first_unit_of_group:
                    first_unit_of_group[q] = len(units)
                extra.setdefault(len(units), []).append(lambda j=j: slab_load(j + 2) if j + 2 < len(chunk_list) else None)
                need_stage = typ in (0, 1, 3) or (typ == 4 and own)
                if need_stage:
                    sidx = ctr["stg"] % NSTG; ctr["stg"] += 1
                for tt in range(4):
                    t = g * 4 + tt
                    tok0 = tokoff + t * 128
                    ai = ctr["acc"] % NACC; ctr["acc"] += 1
                    qk = typ in (0, 1, 3, 4)
                    attn = typ in (0, 1)
                    if qk:
                        pi = ctr["pp"] % NTMP; ctr["pp"] += 1
                        T_, BT_ = tmp[pi], B_tmp[pi]
                        oi = ctr["ob"] % NOB; ctr["ob"] += 1
                        ueng = "dve"
                    need_vo = (not qk) or (typ == 4 and own)
                    if not qk:
                        vi = ctr["vo"] % NVO; ctr["vo"] += 1
                    elif need_vo:
                        vi = ctr["ko"] % NKO; ctr["ko"] += 1
                    A = B = C = Dd = None

                    def A(tt=tt, si=si, ai=ai, hb=hb, typ=typ, qk=qk, T_=(T_ if qk else None), BT_=(BT_ if qk else None), vi=(vi if need_vo else None)):
                        for k in range(KD):
                            S.op("pe", lambda e, k=k: e.matmul(pacc[ai][:, :], lhsT=hT[hb][:, k, tt * 128:(tt + 1) * 128], rhs=wsl[si][:, k, :], start=(k == 0), stop=(k == KD - 1)),
                                 reads=[B_hT[hb], B_wsl[si]], pwrites=[B_pacc[ai]])
                        if qk:
                            S.op("act", lambda e: e.activation(out=T_[0][:, :], in_=pacc[ai][:, :], func=AF.Copy), reads=[B_pacc[ai]], writes=[BT_[0]])
                        elif typ == 6:
                            S.op("act", lambda e: e.activation(out=vob[vi][:, :], in_=pacc[ai][:, :], func=AF.Silu), reads=[B_pacc[ai]], writes=[B_vob[vi]])
                        else:
                            S.op("act", lambda e: e.activation(out=vob[vi][:, :], in_=pacc[ai][:, :], func=AF.Copy), reads=[B_pacc[ai]], writes=[B_vob[vi]])

                    if not qk:
                        def B(tt=tt, t=t, tok0=tok0, hc=hc, typ=typ, own=own, vi=vi):
                            if typ == 2:
                                S.op("sp", lambda e: e.dma_start(out=v_s[tok0:tok0 + 128, hc * 512:(hc + 1) * 512], in_=vob[vi][:, :]), reads=[B_vob[vi]], pwrites=[B_scr["v"]], dma=True)
                            elif typ == 6:
                                S.op("sp", lambda e: e.dma_start(out=rgs_s[t * 128:(t + 1) * 128, hc * 512:(hc + 1) * 512], in_=vob[vi][:, :]), reads=[B_vob[vi]], pwrites=[B_scr["rgs"]], dma=True)
                            elif own:
                                S.op("sp", lambda e: e.dma_start(out=rv_s[t * 128:(t + 1) * 128, hc * 512:(hc + 1) * 512], in_=vob[vi][:, :]), reads=[B_vob[vi]], pwrites=[B_scr["rv"]], dma=True)
                            else:
                                for hh in range(4):
                                    h = hc * 4 + hh
                                    S.op("pe", lambda e, h=h, hh=hh: e.matmul(pU[hc][:, hh * 128:(hh + 1) * 128], lhsT=kdh[:, tt, h * 128:(h + 1) * 128],
                                                                          rhs=vob[vi][:, hh * 128:(hh + 1) * 128], start=True, stop=True),
                                         reads=[B_kdh[tt], B_vob[vi]], pwrites=[B_pU[hc]])
                                sv = bass.AP(Sst, hc * 512, [[H * 128, 128], [128, 4], [1, 128]])
                                S.op("pool", lambda e: e.tensor_tensor(out=sv, in0=sv, in1=bc_in(gch_t, H, hc * 4, 4, 128), op=ALU.mult), reads=[B_S, B_const], writes=[B_S])
                                S.op("dve", lambda e: e.tensor_tensor(out=Sst[:, hc * 512:(hc + 1) * 512], in0=Sst[:, hc * 512:(hc + 1) * 512], in1=pU[hc][:, :], op=ALU.add),
                                     reads=[B_S, B_pU[hc]], writes=[B_S])
                        units.append([A, B, None, None])
                        continue

                    tab, Btab = (ropa[hb], B_ropa[hb]) if attn else (ropr[hb], B_ropr[hb])
                    if attn:
                        def B(pi=pi, T_=T_, BT_=BT_, typ=typ, ueng=ueng):
                            gn = qn_t if typ == 0 else kn_t
                            zc, sq = T_[0], T_[1]
                            S.op("act", lambda e: e.activation(out=sq[:, :], in_=zc[:, :], func=AF.Square), reads=[BT_[0]], writes=[BT_[1]])
                            S.op("dve", lambda e: e.tensor_reduce(out=sm[pi][:, 0:4], in_=sq[:, :].rearrange("p (h d) -> p h d", h=4), axis=AX.X, op=ALU.add),
                                 reads=[BT_[1]], writes=[B_sm[pi]])
                            S.op("pool", lambda e: e.tensor_scalar(out=sm[pi][:, 0:4], in0=sm[pi][:, 0:4], scalar1=1.0 / HD, scalar2=EPS, op0=ALU.mult, op1=ALU.add),
                                 reads=[B_sm[pi]], writes=[B_sm[pi]])
                            S.op("pool", lambda e: e.tensor_tensor(out=sm[pi][:, 4:8], in0=sm[pi][:, 0:4], in1=mhalf[:, 0:4], op=ALU.pow), reads=[B_sm[pi], B_const], writes=[B_sm[pi]])
                            z3 = zc[:, :].rearrange("p (h d) -> p h d", h=4)
                            S.op(ueng, lambda e: e.tensor_tensor(out=z3, in0=z3, in1=bc_in(sm[pi], 8, 4, 4, 128), op=ALU.mult), reads=[BT_[0], B_sm[pi]], writes=[BT_[0]])
                            S.op(ueng, lambda e: e.tensor_tensor(out=z3, in0=z3, in1=bc_mid(gn, HD, 0, 4, 128), op=ALU.mult), reads=[BT_[0], B_const], writes=[BT_[0]])

                    def C(pi=oi, T_=T_, BT_=BT_, tab=tab, Btab=Btab, tt=tt, typ=typ, own=own, hc=hc, vi=(vi if need_vo else None), ueng=ueng):
                        rope(T_, BT_, tab, Btab, ob[pi], B_ob[pi], tt, ueng)
                        if typ == 4:
                            o3 = ob[pi][:, :].rearrange("p (h d) -> p h d", h=4)
                            if own:
                                S.op(ueng, lambda e: e.tensor_tensor(out=kdo[vi][:, :].rearrange("p (h d) -> p h d", h=4), in0=o3, in1=bc_in(kdec_t, H, hc * 4, 4, 128), op=ALU.mult),
                                     reads=[B_ob[pi], B_const], writes=[B_kdo[vi]])
                            else:
                                S.op(ueng, lambda e: e.tensor_tensor(out=kdh[:, tt, hc * 512:(hc + 1) * 512].rearrange("p (h d) -> p h d", h=4), in0=o3,
                                                                     in1=bc_in(kdec_t, H, hc * 4, 4, 128), op=ALU.mult),
                                     reads=[B_ob[pi], B_const], pwrites=[B_kdh[tt]])

                    if need_stage:
                        def Dd(pi=oi, tt=tt, t=t, typ=typ, own=own, hc=hc, sidx=sidx, g=g, tokoff=tokoff, vi=(vi if need_vo else None)):
                            if typ == 4 and own:
                                S.op("sp", lambda e: e.dma_start(out=rkd_s[t * 128:(t + 1) * 128, hc * 512:(hc + 1) * 512], in_=kdo[vi][:, :]), reads=[B_kdo[vi]], pwrites=[B_scr["rkd"]], dma=True)
                            for hh in range(4):
                                S.op("pe", lambda e, hh=hh: e.transpose(out=ptq[:, hh, :], in_=ob[pi][:, hh * 128:(hh + 1) * 128], identity=identb[:, :]),
                                     reads=[B_ob[pi], B_const], pwrites=[B_ptq])
                            S.op("act", lambda e: e.activation(out=stage[sidx][:, :, tt * 128:(tt + 1) * 128], in_=ptq[:, 0:4, :], func=AF.Copy), reads=[B_ptq], pwrites=[B_stage[sidx]])
                            if tt == 3:
                                dst, bname, toff = {0: (qT_s, "qT", 0), 1: (kT_s, "kT", tokoff), 3: (rqT_s, "rqT", 0), 4: (rkT_s, "rkT", 0)}[typ]
                                c0 = toff + g * 512
                                S.op("sp", lambda e: e.dma_start(out=dst[hc * 4:(hc + 1) * 4, :, c0:c0 + 512].rearrange("h d t -> d h t"), in_=stage[sidx][:, :, :]),
                                     reads=[B_stage[sidx]], pwrites=[B_scr[bname]], dma=True)
                                if typ == 1:
                                    blk0 = (tokoff + g * 512) // 256
                                    kv = bass.AP(kmT, hc * 4 * NBK + blk0, [[H * NBK, 128], [NBK, 4], [1, 2]])
                                    S.op("dve", lambda e: e.tensor_reduce(out=kv, in_=stage[sidx][:, :, :].rearrange("p h (b t) -> p h b t", b=2), axis=AX.X, op=ALU.add),
                                         reads=[B_stage[sidx]], pwrites=[B_kmT])
                    units.append([A, (B if attn else None), C, (Dd if need_stage else None)])

            for q in range(len(seq)):
                u0 = first_unit_of_group[q]
                u1 = first_unit_of_group[q + 1] if q + 1 < len(seq) else len(units)
                if q + 1 < len(seq):
                    steps = prologue_steps(q + 1)
                    n = u1 - u0
                    for i_, stp in enumerate(steps):
                        extra.setdefault(u0 + 10 + (i_ * (n - 15)) // 8, []).append(stp)
            ctr["tile"] = 0
            pre0 = prologue_steps(0)
            slab_load(0)
            slab_load(1)
            for stp in pre0:
                stp()
            SKEW = (0, 1, 4, 9)
            NU = len(units)
            for i in range(NU + SKEW[3]):
                if i < NU:
                    for fn in extra.get(i, []):
                        fn()
                for s_ in range(4):
                    j = i - SKEW[s_]
                    if 0 <= j < NU and units[j][s_] is not None:
                        units[j][s_]()

            if debug:
                S.op("sp", lambda e: e.dma_start(out=dbg["d_qT"][:, :, :], in_=qT_s[:, :, :]), reads=[B_scr["qT"]], dma=True)
                S.op("sp", lambda e: e.dma_start(out=dbg["d_kT"][:, :, :], in_=kT_s[:, :, :]), reads=[B_scr["kT"]], dma=True)
                S.op("sp", lambda e: e.dma_start(out=dbg["d_v"][:, :], in_=v_s[:, :]), reads=[B_scr["v"]], dma=True)
                S.op("sp", lambda e: e.dma_start(out=dbg["d_rqT"][:, :, :], in_=rqT_s[:, :, :]), reads=[B_scr["rqT"]], dma=True)
                S.op("sp", lambda e: e.dma_start(out=dbg["d_rkd"][:, :], in_=rkd_s[:, :]), reads=[B_scr["rkd"]], dma=True)
            st1 = S.emit()
        print("phase0", st0, "phase1", st1, flush=True)
        import os as _os
        if _os.environ.get("STOP_AFTER") != "1":
            build_rest(nc, S, locals())
    return nc


def build_rest(nc, S, L):
    NBH, TH, NT, NG, NBK, TT, scale, debug, dbg = (L[k] for k in ("NBH", "TH", "NT", "NG", "NBK", "TT", "scale", "debug", "dbg"))
    identb, identf, gffn_t, gple_t, kdec_t, qdec_t, gch_t, mhalf, kmT, Sst = (L[k] for k in (
        "identb", "identf", "gffn_t", "gple_t", "kdec_t", "qdec_t", "gch_t", "mhalf", "kmT", "Sst"))
    B_const, B_kmT, B_S, B_w, B_scr, cast_w = (L[k] for k in ("B_const", "B_kmT", "B_S", "B_w", "B_scr", "cast_w"))
    scale = float(scale)

    with contextlib.ExitStack() as st:
        sb = lambda n, s, d=F32: st.enter_context(nc.sbuf_tensor(n, list(s), d))
        psb = lambda n, s, d=F32: st.enter_context(nc.psum_tensor(n, list(s), d))
        cast_jobs = []
        for src_, dst_, R_, C_, bn_ in ((L["w_gate"], L["wb_g"], D, DFF, "g"), (L["w_up"], L["wb_u"], D, DFF, "u"), (L["w_down"], L["wb_d"], DFF, D, "d")):
            for r0 in range(0, R_, 1024):
                rr = min(1024, R_ - r0)
                for c0 in range(0, C_, 2048):
                    cc = min(2048, C_ - c0)
                    cast_jobs.append(lambda src_=src_, dst_=dst_, r0=r0, rr=rr, c0=c0, cc=cc, bn_=bn_: S.op(
                        "pool", lambda e: e.dma_start(out=dst_[r0:r0 + rr, c0:c0 + cc], in_=src_[r0:r0 + rr, c0:c0 + cc]), pwrites=[B_w[bn_]], dma=True))
        kTh = [sb("kTh%d" % i, [128, TT], BF16) for i in range(2)]; B_kTh = [Buf() for _ in range(2)]
        V1 = [sb("V1_%d" % i, [128, 2 * NT, 129], BF16) for i in range(2)]; B_V1 = [Buf() for _ in range(2)]
        qTh = [sb("qTh%d" % i, [128, TH], BF16) for i in range(2)]; B_qTh = [Buf() for _ in range(2)]
        aTst = [sb("aTst%d" % i, [128, TH], BF16) for i in range(2)]; B_aTst = [Buf() for _ in range(2)]
        kmb = sb("kmb", [128, H * NBK], BF16); B_kmb = Buf()
        pastt = sb("pastt", [128, NBH * NBK]); diagt = sb("diagt", [128, NBH * NBK])
        maskc = sb("maskc", [128, 4 * 512], BF16); oneh = sb("oneh", [128, NBK * 128], BF16)
        B_c2 = Buf()
        NPT = 5
        PT2 = [sb("PT2_%d" % i, [128, 1024], BF16) for i in range(NPT)]; B_PT = [Buf() for _ in range(NPT)]
        gs = sb("gs", [128, 4 * NBK]); B_gs = Buf()
        m8 = sb("m8", [128, 4 * 8]); B_m8 = Buf()
        selb = sb("selb", [128, 4 * NBK]); B_selb = Buf()
        NP = NBH // 2
        selbTa = [sb("selbTa%d" % i, [128, NP * 512], BF16) for i in range(2)]; B_selbTa = [[Buf() for _ in range(NP)] for _ in range(2)]
        PaD = [sb("PaD%d" % i, [128, 1024]) for i in range(2)]; B_PaD = [Buf() for _ in range(2)]
        PaP = [sb("PaP%d" % i, [128, 1024]) for i in range(2)]; B_PaP = [Buf() for _ in range(2)]
        psm = [sb("psm%d" % i, [128, 512]) for i in range(3)]; B_psm = [Buf() for _ in range(3)]
        rcr = sb("rcr", [1, 512]); B_rcr = Buf()
        bcs = sb("bcs", [128, 512]); B_bcs = Buf()
        onesc = sb("onesc", [128, 1], BF16); onesr = sb("onesr", [1, 128], BF16)
        phl = [sb("phl%d" % i, [128, 512], BF16) for i in range(2)]; B_phl = Buf()
        rhl = [sb("rhl%d" % i, [1, 512], BF16) for i in range(2)]; B_rhl = Buf()
        pSTw = [psb("pSTw%d" % i, [128, 1024]) for i in range(2)]; B_pST = [Buf() for _ in range(2)]
        pacc_o = [psb("pacco%d" % i, [128, 512]) for i in range(2)]; B_pacc_o = [Buf() for _ in range(2)]
        pm1 = psb("pm1", [128, 512]); B_pm1 = Buf()
        pm2 = psb("pm2", [128, 512]); B_pm2 = Buf()
        pm2b = pm2[:, :].bitcast(BF16)

        ldc = lambda o_, i_: S.op("sp", lambda e: e.dma_start(out=o_, in_=i_), pwrites=[B_c2], dma=True)
        ldc(pastt[:, :], L["c_past"][:, :]); ldc(diagt[:, :], L["c_diag"][:, :]); ldc(maskc[:, :], L["c_mask"][:, :]); ldc(oneh[:, :], L["c_onehot"][:, :])
        S.op("dve", lambda e: e.tensor_copy(out=kmb[:, :], in_=kmT[:, :]), reads=[B_kmT], writes=[B_kmb])
        for i in range(2):
            for p_ in range(NP):
                S.op("pool", lambda e, i=i, p_=p_: e.memset(selbTa[i][:, p_ * 512:(p_ + 1) * 512], 0.0), writes=[B_selbTa[i][p_]])
        S.op("pool", lambda e: e.memset(onesc[:, :], 1.0), pwrites=[B_c2])
        S.op("pool", lambda e: e.memset(onesr[:, :], 1.0), pwrites=[B_c2])

        def head_loads(h):
            hb = h % 2
            S.op("sp", lambda e: e.dma_start(out=qTh[hb][:, :], in_=L["qT_s"][h, :, :]), reads=[B_scr["qT"]], writes=[B_qTh[hb]], dma=True)
            S.op("sp", lambda e: e.dma_start(out=kTh[hb][:, :], in_=L["kT_s"][h, :, :]), reads=[B_scr["kT"]], writes=[B_kTh[hb]], dma=True)
            nvs = max(1, (2 * NT) // 16)
            for vs in range(nvs):
                t0_, t1_ = vs * (2 * NT // nvs), (vs + 1) * (2 * NT // nvs)
                S.op("sp", lambda e, t0_=t0_, t1_=t1_: e.dma_start(out=V1[hb][:, t0_:t1_, 0:128],
                                                                 in_=L["v_s"][t0_ * 128:t1_ * 128, h * 128:(h + 1) * 128].rearrange("(t p) d -> p t d", p=128)),
                     reads=[B_scr["v"]], writes=[B_V1[hb]] if vs == 0 else [], pwrites=[] if vs == 0 else [B_V1[hb]], dma=True)

        def sel_steps(h):
            hb = h % 2
            steps = []
            row = NBH * NBK
            v4 = lambda t_: bass.AP(t_, 0, [[4 * NBK, 128], [2 * NBK, 2], [NBK, 2], [1, NBK]])
            g3 = lambda t_: bass.AP(t_, 0, [[4 * NBK, 128], [NBK, 4], [1, NBK]])
            pm1v = bass.AP(pm1, 0, [[512, 128], [2 * NBK, 2], [NBK, 2], [1, NBK]])
            thr = bass.AP(m8, 2, [[32, 128], [8, 4], [0, NBK]])
            for p in range(NP):
                pastv = bass.AP(pastt, 2 * p * NBK, [[row, 128], [NBK, 2], [0, 2], [1, NBK]])
                diagv = bass.AP(diagt, 2 * p * NBK, [[row, 128], [NBK, 2], [0, 2], [1, NBK]])

                def s1(p=p):
                    for qt in range(4):
                        tq0 = p * 512 + qt * 128
                        S.op("pe", lambda e, qt=qt, tq0=tq0: e.matmul(pm1[:, qt * NBK:(qt + 1) * NBK], lhsT=qTh[hb][:, tq0:tq0 + 128], rhs=kmb[:, h * NBK:(h + 1) * NBK],
                                                                  start=True, stop=True), reads=[B_qTh[hb], B_kmb], pwrites=[B_pm1])

                def s2(pastv=pastv):
                    S.op("dve", lambda e: e.tensor_tensor(out=v4(gs), in0=pm1v, in1=pastv, op=ALU.add), reads=[B_pm1, B_c2], writes=[B_gs])
                    for qt in range(4):
                        S.op("dve", lambda e, qt=qt: e.max(out=m8[:, qt * 8:(qt + 1) * 8], in_=gs[:, qt * NBK:(qt + 1) * NBK]), reads=[B_gs], pwrites=[B_m8])

                def s3(pastv=pastv, diagv=diagv):
                    S.op("dve", lambda e: e.tensor_tensor(out=g3(selb), in0=g3(gs), in1=thr, op=ALU.is_ge), reads=[B_gs, B_m8], writes=[B_selb])
                    S.op("dve", lambda e: e.tensor_scalar(out=selb[:, :], in0=selb[:, :], scalar1=BIG, scalar2=-BIG, op0=ALU.mult, op1=ALU.add), reads=[B_selb], writes=[B_selb])
                    S.op("dve", lambda e: e.tensor_tensor(out=v4(selb), in0=v4(selb), in1=pastv, op=ALU.min), reads=[B_selb, B_c2], writes=[B_selb])
                    S.op("dve", lambda e: e.tensor_tensor(out=v4(selb), in0=v4(selb), in1=diagv, op=ALU.max), reads=[B_selb, B_c2], writes=[B_selb])

                def s4():
                    for qt in range(4):
                        S.op("pe", lambda e, qt=qt: e.transpose(out=pm2[0:NBK, qt * 128:(qt + 1) * 128], in_=selb[:, qt * NBK:(qt + 1) * NBK], identity=identf[:, :]),
                             reads=[B_selb, B_const], pwrites=[B_pm2])

                def s5(p=p):
                    S.op("dve", lambda e: e.tensor_copy(out=selbTa[hb][0:NBK, p * 512:(p + 1) * 512], in_=pm2[0:NBK, :]), reads=[B_pm2], writes=[B_selbTa[hb][p]])

                def s45(s4=s4, s5=s5):
                    s4(); s5()
                steps += [s1, s2, s3, s45]
            return steps

        U = []
        import os as _os
        HLIM = int(_os.environ.get("A_HEADS", H))
        for h in range(HLIM):
            hb = h % 2
            for p in range(NP):
                keys = [(kt, kt // 2, None) for kt in range(NT)]
                for ko in range(4 * p + 4):
                    keys.append((NT + ko, NBH + ko // 2, (ko - 4 * p) if ko >= 4 * p else None))
                nk2 = len(keys) // 2
                for u in range(nk2):
                    U.append(dict(h=h, hb=hb, p=p, keys=keys[2 * u:2 * u + 2], first=(u == 0), last=(u == nk2 - 1), idx=len(U), uinpair=u, pairidx=h * NP + p))

        def do_ST(un):
            h, hb, p = un["h"], un["hb"], un["p"]
            si = un["idx"] % 2; pj = un["idx"] % NPT
            for half, (ktile, n, pidx) in enumerate(un["keys"]):
                osl = slice(half * 512, (half + 1) * 512)
                S.op("pe", lambda e, ktile=ktile, osl=osl: e.matmul(pSTw[si][:, osl], lhsT=kTh[hb][:, ktile * 128:(ktile + 1) * 128], rhs=qTh[hb][:, p * 512:(p + 1) * 512],
                                                                start=True, stop=False), reads=[B_kTh[hb], B_qTh[hb]], pwrites=[B_pST[si]])
                S.op("pe", lambda e, n=n, osl=osl: e.matmul(pSTw[si][:, osl], lhsT=oneh[:, n * 128:(n + 1) * 128], rhs=selbTa[hb][:, p * 512:(p + 1) * 512], start=False, stop=True),
                     reads=[B_c2, B_selbTa[hb][p]], pwrites=[B_pST[si]])
            S.op("act", lambda e: e.activation(out=PT2[pj][:, :], in_=pSTw[si][:, :], func=AF.Exp, scale=scale), reads=[B_pST[si]], writes=[B_PT[pj]])
            for half, (ktile, n, pidx) in enumerate(un["keys"]):
                if pidx is not None:
                    osl = slice(half * 512, (half + 1) * 512)
                    S.op("pool", lambda e, osl=osl, pidx=pidx: e.tensor_tensor(out=PT2[pj][:, osl], in0=PT2[pj][:, osl], in1=maskc[:, pidx * 512:(pidx + 1) * 512], op=ALU.mult),
                         reads=[B_PT[pj], B_c2], writes=[B_PT[pj]])

        deferred = {}

        def do_PV(un):
            h, hb, p = un["h"], un["hb"], un["p"]
            pj = un["idx"] % NPT
            pb = un["pairidx"] % 2
            for half, (ktile, n, pidx) in enumerate(un["keys"]):
                S.op("pe", lambda e, half=half, ktile=ktile: e.matmul(pacc_o[pb][:, :], lhsT=V1[hb][:, ktile, 0:128], rhs=PT2[pj][:, half * 512:(half + 1) * 512],
                                                                start=(un["first"] and half == 0), stop=(un["last"] and half == 1)),
                     reads=[B_PT[pj], B_V1[hb]], pwrites=[B_pacc_o[pb]])
            u = un["uinpair"]
            eng, Pa, BPa = ("dve", PaD[pb], B_PaD[pb])
            if u < 1:
                S.op(eng, lambda e: e.tensor_copy(out=Pa[:, :], in_=PT2[pj][:, :]), reads=[B_PT[pj]], writes=[BPa])
            else:
                S.op(eng, lambda e: e.tensor_tensor(out=Pa[:, :], in0=Pa[:, :], in1=PT2[pj][:, :], op=ALU.add), reads=[B_PT[pj], BPa], writes=[BPa])
            if un["last"]:
                def ep1(pb=pb):
                    S.op("dve", lambda e: e.tensor_tensor(out=psm[2][:, :], in0=PaD[pb][:, 0:512], in1=PaD[pb][:, 512:1024], op=ALU.add), reads=[B_PaD[pb]], writes=[B_psm[2]])
                    S.op("dve", lambda e: e.tensor_copy(out=phl[0][:, :], in_=psm[2][:, :]), reads=[B_psm[2]], writes=[B_phl])
                    S.op("dve", lambda e: e.tensor_tensor(out=phl[1][:, :], in0=psm[2][:, :], in1=phl[0][:, :], op=ALU.subtract), reads=[B_psm[2], B_phl], pwrites=[B_phl])

                def ep2(pb=pb, hb=hb, p=p, h=h):
                    S.op("pe", lambda e: e.matmul(pm2[0:1, 0:512], lhsT=onesc[:, 0:1], rhs=phl[0][:, :], start=True, stop=False), reads=[B_phl, B_c2], pwrites=[B_pm2])
                    S.op("pe", lambda e: e.matmul(pm2[0:1, 0:512], lhsT=onesc[:, 0:1], rhs=phl[1][:, :], start=False, stop=True), reads=[B_phl, B_c2], pwrites=[B_pm2])
                    S.op("dve", lambda e: e.reciprocal(out=rcr[:, :], in_=pm2[0:1, 0:512]), reads=[B_pm2], writes=[B_rcr])
                    S.op("dve", lambda e: e.tensor_copy(out=rhl[0][:, :], in_=rcr[:, :]), reads=[B_rcr], writes=[B_rhl])
                    S.op("dve", lambda e: e.tensor_tensor(out=rhl[1][:, :], in0=rcr[:, :], in1=rhl[0][:, :], op=ALU.subtract), reads=[B_rcr, B_rhl], pwrites=[B_rhl])
                    S.op("pe", lambda e: e.matmul(pm2[:, 0:512], lhsT=onesr[0:1, :], rhs=rhl[0][0:1, :], start=True, stop=False), reads=[B_rhl, B_c2], pwrites=[B_pm2])
                    S.op("pe", lambda e: e.matmul(pm2[:, 0:512], lhsT=onesr[0:1, :], rhs=rhl[1][0:1, :], start=False, stop=True), reads=[B_rhl, B_c2], pwrites=[B_pm2])
                    S.op("act", lambda e: e.activation(out=bcs[:, :], in_=pm2[:, 0:512], func=AF.Copy), reads=[B_pm2], writes=[B_bcs])
                    S.op("dve", lambda e: e.tensor_tensor(out=aTst[hb][:, p * 512:(p + 1) * 512], in0=pacc_o[pb][:, :], in1=bcs[:, :], op=ALU.mult),
                         reads=[B_pacc_o[pb], B_bcs], pwrites=[B_aTst[hb]])
                    if p == NP - 1:
                        S.op("sp", lambda e: e.dma_start(out=L["mixT_s"][h * 128:(h + 1) * 128, :], in_=aTst[hb][:, :]), reads=[B_aTst[hb]], pwrites=[B_scr["mixT"]], dma=True)
                deferred.setdefault(un["idx"] + 2, []).append(ep1)
                deferred.setdefault(un["idx"] + 4, []).append(ep2)

        head_loads(0)
        for stp in sel_steps(0):
            stp()
        pending = []
        per_head = len(U) // HLIM
        cast_stride = max(1, (len(U) * 3 // 4) // max(1, len(cast_jobs)))
        LOOK = int(_os.environ.get("LOOK", "2"))
        pos_in_head = 0
        stride = 1

        def after_pv(j):
            nonlocal pending, pos_in_head, stride
            for fn in deferred.pop(j, []):
                fn()

        def enter_head(hcur):
            nonlocal pending, pos_in_head, stride
            if hcur + 1 < HLIM:
                head_loads(hcur + 1)
                pending = sel_steps(hcur + 1)
                stride = max(1, (per_head - 8) // max(1, len(pending)))
                pos_in_head = 0

        for i, un in enumerate(U):
            if cast_jobs and i % cast_stride == 0:
                cast_jobs.pop(0)()
            do_ST(un)
            j = i - LOOK
            if j >= 0:
                do_PV(U[j])
                after_pv(j)
            if i == 0:
                enter_head(0)
            elif j >= 0 and U[j]["last"] and U[j]["p"] == NP - 1:
                enter_head(U[j]["h"] + 1)
            if pending:
                pos_in_head += 1
                nxt_new_head = (i + 1 < len(U)) and (U[i + 1]["h"] != un["h"])
                flush = nxt_new_head or (i + 1 == len(U))
                if pos_in_head % stride == 0 or flush:
                    nstep = len(pending) if flush else 1
                    for _ in range(nstep):
                        pending.pop(0)()
        for j in range(max(0, len(U) - LOOK), len(U)):
            do_PV(U[j])
            after_pv(j)
        for k_ in sorted(deferred):
            for fn in deferred[k_]:
                fn()
        while cast_jobs:
            cast_jobs.pop(0)()
        st2a = S.emit()
    print("phase2a", st2a, flush=True)
    import os as _os
    if _os.environ.get("STOP_AFTER") == "2":
        return

    with contextlib.ExitStack() as st:
        sb = lambda n, s, d=F32: st.enter_context(nc.sbuf_tensor(n, list(s), d))
        psb = lambda n, s, d=F32: st.enter_context(nc.psum_tensor(n, list(s), d))
        rq4 = [sb("rq4_%d" % i, [128, H, 512], BF16) for i in range(2)]; B_rq4 = [Buf() for _ in range(2)]
        rk4 = [sb("rk4_%d" % i, [128, H, 512], BF16) for i in range(2)]; B_rk4 = [Buf() for _ in range(2)]
        NKB = 3
        kd = [sb("kd%d" % i, [128, H * HD], BF16) for i in range(NKB)]; B_kd = [Buf() for _ in range(NKB)]
        rvt = [sb("rvt%d" % i, [128, H * HD], BF16) for i in range(NKB)]; B_rvt = [Buf() for _ in range(NKB)]
        NRG = 4
        rgt = [sb("rgt%d" % i, [128, H * HD], BF16) for i in range(NRG)]; B_rgt = [Buf() for _ in range(NRG)]
        decT = sb("decT", [128, H * 128]); grt = sb("grt", [128, H * HD]); B_c3 = Buf()
        Sb = sb("Sb", [128, H * HD], BF16); B_Sb = Buf()
        sTm = [sb("sTm%d" % i, [128, H, 128], BF16) for i in range(2)]; B_sTm = [Buf() for _ in range(2)]
        NOS = 3
        osb_ = [sb("osb%d" % i, [128, H * HD]) for i in range(NOS)]; B_osb_ = [Buf() for _ in range(NOS)]
        sq_ = [sb("sq2_%d" % i, [128, H * HD]) for i in range(2)]; B_sq_ = [Buf() for _ in range(2)]
        ssr_ = [sb("ssr%d" % i, [128, 16]) for i in range(2)]; B_ssr_ = [Buf() for _ in range(2)]
        rob = [sb("rob%d" % i, [128, H * HD], BF16) for i in range(NOS)]; B_rob = [Buf() for _ in range(NOS)]
        rTst = [sb("rTst%d" % i, [128, H, 512], BF16) for i in range(2)]; B_rTst = [Buf() for _ in range(2)]
        psT = [psb("psT%d" % i, [128, 512]) for i in range(2)]; B_psT = [Buf() for _ in range(2)]
        pO = [psb("pO%d" % i, [128, 512]) for i in range(2)]; B_pO = [Buf() for _ in range(2)]
        pU = [psb("pU2_%d" % i, [128, 512]) for i in range(2)]; B_pU = [Buf() for _ in range(2)]
        ptr = psb("ptr2", [128, H, 128], BF16); B_ptr = Buf()
        S.op("sp", lambda e: e.dma_start(out=decT[:, :], in_=L["c_decT"][:, :]), pwrites=[B_c3], dma=True)
        S.op("sp", lambda e: e.dma_start(out=grt[:, :], in_=L["g_ret"].ap().partition_broadcast(128)), pwrites=[B_c3], dma=True)

        def LD(t):
            g, tt = t // 4, t % 4
            gb = g % 2
            if tt == 0:
                S.op("sp", lambda e: e.dma_start(out=rq4[gb][:, :, :], in_=L["rqT_s"][:, :, g * 512:(g + 1) * 512].rearrange("h d t -> d h t")),
                     reads=[B_scr["rqT"]], writes=[B_rq4[gb]], dma=True)
                S.op("sp", lambda e: e.dma_start(out=rk4[gb][:, :, :], in_=L["rkT_s"][:, :, g * 512:(g + 1) * 512].rearrange("h d t -> d h t")),
                     reads=[B_scr["rkT"]], writes=[B_rk4[gb]], dma=True)
            for dst, Bd, src, nm, nb in ((kd, B_kd, "rkd_s", "rkd", NKB), (rvt, B_rvt, "rv_s", "rv", NKB), (rgt, B_rgt, "rgs_s", "rgs", NRG)):
                S.op("sp", lambda e, dst=dst, src=src, nb=nb: e.dma_start(out=dst[t % nb][:, :], in_=L[src][t * 128:(t + 1) * 128, :]),
                     reads=[B_scr[nm]], writes=[Bd[t % nb]], dma=True)

        def M1(t):
            g, tt = t // 4, t % 4
            gb = g % 2; tb = t % 2
            ts_ = slice(tt * 128, (tt + 1) * 128)
            for hh in range(H):
                bk, col = hh // 4, (hh % 4) * 128
                S.op("pe", lambda e, hh=hh, bk=bk, col=col: e.matmul(psT[bk][:, col:col + 128], lhsT=rk4[gb][:, hh, ts_], rhs=rq4[gb][:, hh, ts_], start=True, stop=True),
                     reads=[B_rk4[gb], B_rq4[gb]], pwrites=[B_psT[bk]])
            for bk in range(2):
                S.op("dve", lambda e, bk=bk: e.tensor_tensor(out=sTm[tb][:, bk * 4:(bk + 1) * 4, :], in0=psT[bk][:, :].rearrange("p (h t) -> p h t", h=4),
                                                        in1=decT[:, bk * 512:(bk + 1) * 512].rearrange("p (h t) -> p h t", h=4), op=ALU.mult),
                     reads=[B_psT[bk], B_c3], pwrites=[B_sTm[tb]])

        def M2(t):
            g, tt = t // 4, t % 4
            gb = g % 2; tb = t % 2; kb = t % NKB; ob_ = t % NOS
            ts_ = slice(tt * 128, (tt + 1) * 128)
            S.op("pool", lambda e: e.tensor_copy(out=Sb[:, :], in_=Sst[:, :]), reads=[B_S], writes=[B_Sb])
            for hh in range(H):
                bk, col = hh // 4, (hh % 4) * 128
                S.op("pe", lambda e, hh=hh, bk=bk, col=col: e.matmul(pO[bk][:, col:col + 128], lhsT=sTm[tb][:, hh, :], rhs=rvt[kb][:, hh * 128:(hh + 1) * 128], start=True, stop=False),
                     reads=[B_sTm[tb], B_rvt[kb]], pwrites=[B_pO[bk]])
                S.op("pe", lambda e, hh=hh, bk=bk, col=col: e.matmul(pO[bk][:, col:col + 128], lhsT=rq4[gb][:, hh, ts_], rhs=Sb[:, hh * 128:(hh + 1) * 128], start=False, stop=True),
                     reads=[B_rq4[gb], B_Sb], pwrites=[B_pO[bk]])
            for bk in range(2):
                S.op("dve", lambda e, bk=bk: e.tensor_tensor(out=osb_[ob_][:, bk * 512:(bk + 1) * 512].rearrange("p (h e) -> p h e", h=4), in0=pO[bk][:, :].rearrange("p (h e) -> p h e", h=4),
                                                        in1=bc_in(qdec_t, H, bk * 4, 4, 128), op=ALU.mult),
                     reads=[B_pO[bk], B_const], pwrites=[B_osb_[ob_]])

        def UU(t):
            kb = t % NKB
            for hh in range(H):
                bk, col = hh // 4, (hh % 4) * 128
                S.op("pe", lambda e, hh=hh, bk=bk, col=col: e.matmul(pU[bk][:, col:col + 128], lhsT=kd[kb][:, hh * 128:(hh + 1) * 128], rhs=rvt[kb][:, hh * 128:(hh + 1) * 128], start=True, stop=True),
                     reads=[B_kd[kb], B_rvt[kb]], pwrites=[B_pU[bk]])
            s3 = Sst[:, :].rearrange("p (h e) -> p h e", h=H)
            S.op("pool", lambda e: e.tensor_tensor(out=s3, in0=s3, in1=bc_in(gch_t, H, 0, 8, 128), op=ALU.mult), reads=[B_S, B_const], writes=[B_S])
            for bk in range(2):
                S.op("dve", lambda e, bk=bk: e.tensor_tensor(out=Sst[:, bk * 512:(bk + 1) * 512], in0=Sst[:, bk * 512:(bk + 1) * 512], in1=pU[bk][:, :], op=ALU.add),
                     reads=[B_S, B_pU[bk]], writes=[B_S])

        def NN(t):
            ob_ = t % NOS; sb_i = t % 2; rg_i = t % NRG
            osb, sq, ssr = osb_[ob_], sq_[sb_i], ssr_[sb_i]
            B_osb, B_sq, B_ssr = B_osb_[ob_], B_sq_[sb_i], B_ssr_[sb_i]
            S.op("pool", lambda e: e.tensor_tensor(out=sq[:, :], in0=osb[:, :], in1=osb[:, :], op=ALU.mult), reads=[B_osb], writes=[B_sq])
            S.op("dve", lambda e: e.tensor_reduce(out=ssr[:, 0:8], in_=sq[:, :].rearrange("p (h e) -> p h e", h=H), axis=AX.X, op=ALU.add), reads=[B_sq], writes=[B_ssr])
            S.op("pool", lambda e: e.tensor_scalar(out=ssr[:, 0:8], in0=ssr[:, 0:8], scalar1=1.0 / HD, scalar2=EPS, op0=ALU.mult, op1=ALU.add), reads=[B_ssr], writes=[B_ssr])
            S.op("pool", lambda e: e.tensor_tensor(out=ssr[:, 8:16], in0=ssr[:, 0:8], in1=mhalf[:, 0:8], op=ALU.pow), reads=[B_ssr, B_const], writes=[B_ssr])
            o3 = osb[:, :].rearrange("p (h e) -> p h e", h=H)
            S.op("dve", lambda e: e.tensor_tensor(out=o3, in0=o3, in1=bc_in(ssr, 16, 8, 8, 128), op=ALU.mult), reads=[B_osb, B_ssr], writes=[B_osb])
            S.op("pool", lambda e: e.tensor_tensor(out=osb[:, :], in0=osb[:, :], in1=grt[:, :], op=ALU.mult), reads=[B_osb, B_c3], writes=[B_osb])
            S.op("dve", lambda e: e.tensor_tensor(out=rob[ob_][:, :], in0=osb[:, :], in1=rgt[rg_i][:, :], op=ALU.mult), reads=[B_osb, B_rgt[rg_i]], writes=[B_rob[ob_]])

        def TT(t):
            g, tt = t // 4, t % 4
            gb = g % 2; ob_ = t % NOS
            ts_ = slice(tt * 128, (tt + 1) * 128)
            for hh in range(H):
                S.op("pe", lambda e, hh=hh: e.transpose(out=ptr[:, hh, :], in_=rob[ob_][:, hh * 128:(hh + 1) * 128], identity=identb[:, :]),
                     reads=[B_rob[ob_], B_const], pwrites=[B_ptr])
            S.op("act", lambda e: e.activation(out=rTst[gb][:, :, ts_], in_=ptr[:, :, :], func=AF.Copy), reads=[B_ptr], pwrites=[B_rTst[gb]])
            if tt == 3:
                S.op("sp", lambda e: e.dma_start(out=L["mixT_s"][H * HD:2 * H * HD, g * 512:(g + 1) * 512].rearrange("(h e) t -> e h t", h=H), in_=rTst[gb][:, :, :]),
                     reads=[B_rTst[gb]], pwrites=[B_scr["mixT"]], dma=True)

        NTL = NT
        LD(0)
        M1(0)
        for i in range(NTL + 2):
            if i + 1 < NTL:
                LD(i + 1)
                M1(i + 1)
            if i < NTL:
                M2(i)
                UU(i)
            if 0 <= i - 1 < NTL:
                NN(i - 1)
            if 0 <= i - 2 < NTL:
                TT(i - 2)
        if debug:
            S.op("sp", lambda e: e.dma_start(out=dbg["d_mixT"][:, :], in_=L["mixT_s"][:, :]), reads=[B_scr["mixT"]], dma=True)
        st2b = S.emit()
    print("phase2b", st2b, flush=True)

    with contextlib.ExitStack() as st:
        sb = lambda n, s, d=F32: st.enter_context(nc.sbuf_tensor(n, list(s), d))
        psb = lambda n, s, d=F32: st.enter_context(nc.psum_tensor(n, list(s), d))
        xres = sb("xres", [128, 4, D]); B_xres = [Buf() for _ in range(4)]
        mT = sb("mT", [128, KD, 512], BF16); B_mT = Buf()
        h2T = sb("h2T", [128, KD, 512], BF16); B_h2T = Buf()
        HK = KF // 2
        actT = sb("actT", [128, HK, 512], BF16); B_actT = Buf()
        ws = [sb("ws%d" % i, [128, KD, 256], BF16) for i in range(4)]; B_ws = [Buf() for _ in range(4)]
        wd = [sb("wd%d" % i, [128, 11, 512], BF16) for i in range(2)]; B_wd = [Buf() for _ in range(2)]
        wpp = sb("wpp", [128, 2, D], BF16); bpg_t = sb("bpg_t", [128, D]); B_c4 = Buf()
        pT = sb("pT", [128, 2, 512], BF16); B_pT = Buf()
        pin = [sb("pin%d" % i, [128, PLE]) for i in range(2)]; B_pin = [Buf() for _ in range(2)]
        pinb = [sb("pinb%d" % i, [128, PLE], BF16) for i in range(2)]; B_pinb = [Buf() for _ in range(2)]
        stmp = [sb("stmp%d" % i, [128, 512]) for i in range(2)]; B_stmp = [Buf() for _ in range(2)]
        gtmp = [sb("gtmp%d" % i, [128, 256]) for i in range(2)]; B_gtmp = [Buf() for _ in range(2)]
        xn = sb("xn3", [128, D], BF16); B_xn = Buf()
        st8 = [sb("st8b_%d" % i, [128, 8]) for i in range(2)]; B_st8 = [Buf() for _ in range(2)]
        pb = [psb("pb%d" % i, [128, 512]) for i in range(8)]; B_pb = [Buf() for _ in range(8)]
        pbb = [pb[i][:, :].bitcast(BF16) for i in range(8)]
        S.op("sp", lambda e: e.dma_start(out=wpp[:, :, :], in_=L["wb_pp"].ap().rearrange("(k p) n -> p k n", p=128)), reads=[B_w["pp"]], pwrites=[B_c4], dma=True)
        S.op("sp", lambda e: e.dma_start(out=bpg_t[:, :], in_=L["b_pg"].ap().partition_broadcast(128)), pwrites=[B_c4], dma=True)
        c3 = {"ws": 0, "wd": 0, "a": 0, "t": 0, "n": 0}

        def norm_T(gtab, dstT, B_dst):
            for tt in range(4):
                si = c3["n"] % 2; c3["n"] += 1
                S.op("act", lambda e, tt=tt, si=si: e.activation(out=xn[:, :], in_=xres[:, tt, :], func=AF.Square, accum_out=st8[si][:, 0:1]),
                     reads=[B_xres[tt]], writes=[B_xn, B_st8[si]])
                S.op("pool", lambda e, si=si: e.tensor_scalar(out=st8[si][:, 1:2], in0=st8[si][:, 0:1], scalar1=1.0 / D, scalar2=EPS, op0=ALU.mult, op1=ALU.add),
                     reads=[B_st8[si]], writes=[B_st8[si]])
                S.op("pool", lambda e, si=si: e.tensor_tensor(out=st8[si][:, 2:3], in0=st8[si][:, 1:2], in1=mhalf[:, 0:1], op=ALU.pow), reads=[B_st8[si], B_const], writes=[B_st8[si]])
                S.op("act", lambda e, tt=tt, si=si: e.activation(out=xn[:, :], in_=xres[:, tt, :], func=AF.Copy, scale=st8[si][:, 2:3]), reads=[B_xres[tt], B_st8[si]], writes=[B_xn])
                for k4 in range(4):
                    bi = 4 + (c3["t"] % 2); c3["t"] += 1
                    for kk in range(4):
                        k = k4 * 4 + kk
                        S.op("pe", lambda e, k=k, kk=kk, bi=bi: e.transpose(out=pbb[bi][:, kk * 128:(kk + 1) * 128], in_=xn[:, k * 128:(k + 1) * 128], identity=identb[:, :]),
                             reads=[B_xn, B_const], pwrites=[B_pb[bi]])
                    for kk in range(4):
                        k = k4 * 4 + kk
                        S.op("dve", lambda e, k=k, kk=kk, bi=bi, tt=tt: e.tensor_scalar(out=dstT[:, k, tt * 128:(tt + 1) * 128], in0=pbb[bi][:, kk * 128:(kk + 1) * 128],
                                                                                     scalar1=gtab[:, k:k + 1], scalar2=None, op0=ALU.mult),
                             reads=[B_pb[bi], B_const], pwrites=[B_dst])

        for g in range(NG):
            for tt in range(4):
                S.op("sp", lambda e, g=g, tt=tt: e.dma_start(out=xres[:, tt, :], in_=L["x_own"][(g * 4 + tt) * 128:(g * 4 + tt + 1) * 128, :]), writes=[B_xres[tt]], dma=True)
            S.op("sp", lambda e, g=g: e.dma_start(out=mT[:, :, :], in_=L["mixT_s"][:, g * 512:(g + 1) * 512].rearrange("(k p) t -> p k t", p=128)),
                 reads=[B_scr["mixT"]], writes=[B_mT], dma=True)
            for c in range(8):
                wi = c3["ws"] % 4; c3["ws"] += 1
                S.op("sp", lambda e, c=c, wi=wi: e.dma_start(out=ws[wi][:, :, :], in_=L["wb_o"].ap()[:, c * 256:(c + 1) * 256].rearrange("(k p) n -> p k n", p=128)),
                     reads=[B_w["o"]], writes=[B_ws[wi]], dma=True)
                for tt in range(4):
                    ai = c3["a"] % 4; c3["a"] += 1
                    for k in range(KD):
                        S.op("pe", lambda e, k=k, tt=tt, wi=wi, ai=ai: e.matmul(pb[ai][:, 0:256], lhsT=mT[:, k, tt * 128:(tt + 1) * 128], rhs=ws[wi][:, k, :], start=(k == 0), stop=(k == KD - 1)),
                             reads=[B_mT, B_ws[wi]], pwrites=[B_pb[ai]])
                    S.op("dve", lambda e, tt=tt, c=c, ai=ai: e.tensor_tensor(out=xres[:, tt, c * 256:(c + 1) * 256], in0=xres[:, tt, c * 256:(c + 1) * 256], in1=pb[ai][:, 0:256], op=ALU.add),
                         reads=[B_pb[ai], B_xres[tt]], writes=[B_xres[tt]])
            norm_T(gffn_t, h2T, B_h2T)
            for half in range(2):
                for js in range(HK // 2):
                    col0 = (half * HK + js * 2) * 128
                    wg_i = c3["ws"] % 4; c3["ws"] += 1
                    wu_i = c3["ws"] % 4; c3["ws"] += 1
                    S.op("sp", lambda e, col0=col0, wg_i=wg_i: e.dma_start(out=ws[wg_i][:, :, :], in_=L["wb_g"].ap()[:, col0:col0 + 256].rearrange("(k p) n -> p k n", p=128)),
                         reads=[B_w["g"]], writes=[B_ws[wg_i]], dma=True)
                    S.op("sp", lambda e, col0=col0, wu_i=wu_i: e.dma_start(out=ws[wu_i][:, :, :], in_=L["wb_u"].ap()[:, col0:col0 + 256].rearrange("(k p) n -> p k n", p=128)),
                         reads=[B_w["u"]], writes=[B_ws[wu_i]], dma=True)
                    for jj in range(2):
                        jl = js * 2 + jj
                        gi = (jl % 2) * 2; ui = gi + 1
                        for k in range(KD):
                            S.op("pe", lambda e, k=k, jj=jj, wg_i=wg_i, gi=gi: e.matmul(pb[gi][:, :], lhsT=ws[wg_i][:, k, jj * 128:(jj + 1) * 128], rhs=h2T[:, k, :], start=(k == 0), stop=(k == KD - 1)),
                                 reads=[B_ws[wg_i], B_h2T], pwrites=[B_pb[gi]])
                        for k in range(KD):
                            S.op("pe", lambda e, k=k, jj=jj, wu_i=wu_i, ui=ui: e.matmul(pb[ui][:, :], lhsT=ws[wu_i][:, k, jj * 128:(jj + 1) * 128], rhs=h2T[:, k, :], start=(k == 0), stop=(k == KD - 1)),
                                 reads=[B_ws[wu_i], B_h2T], pwrites=[B_pb[ui]])
                        sj = jl % 2
                        S.op("act", lambda e, gi=gi, sj=sj: e.activation(out=stmp[sj][:, :], in_=pb[gi][:, :], func=AF.Silu), reads=[B_pb[gi]], writes=[B_stmp[sj]])
                        S.op("dve", lambda e, ui=ui, sj=sj, jl=jl: e.tensor_tensor(out=actT[:, jl, :], in0=stmp[sj][:, :], in1=pb[ui][:, :], op=ALU.mult),
                             reads=[B_stmp[sj], B_pb[ui]], pwrites=[B_actT])
                for c in range(4):
                    for sub in range(2):
                        di = c3["wd"] % 2; c3["wd"] += 1
                        r0 = (half * HK + sub * 11) * 128
                        S.op("sp", lambda e, r0=r0, c=c, di=di: e.dma_start(out=wd[di][:, :, :], in_=L["wb_d"].ap()[r0:r0 + 11 * 128, c * 512:(c + 1) * 512].rearrange("(k p) n -> p k n", p=128)),
                             reads=[B_w["d"]], writes=[B_wd[di]], dma=True)
                        for tt in range(4):
                            for k in range(11):
                                S.op("pe", lambda e, k=k, tt=tt, sub=sub, di=di: e.matmul(pb[4 + tt][:, :], lhsT=actT[:, sub * 11 + k, tt * 128:(tt + 1) * 128], rhs=wd[di][:, k, :],
                                                                                         start=(sub == 0 and k == 0), stop=(sub == 1 and k == 10)),
                                     reads=[B_actT, B_wd[di]], pwrites=[B_pb[4 + tt]])
                    for tt in range(4):
                        S.op("dve", lambda e, tt=tt, c=c: e.tensor_tensor(out=xres[:, tt, c * 512:(c + 1) * 512], in0=xres[:, tt, c * 512:(c + 1) * 512], in1=pb[4 + tt][:, :], op=ALU.add),
                             reads=[B_pb[4 + tt], B_xres[tt]], writes=[B_xres[tt]])
            norm_T(gple_t, mT, B_mT)
            for tt in range(4):
                pi = tt % 2
                S.op("sp", lambda e, g=g, tt=tt, pi=pi: e.dma_start(out=pin[pi][:, :], in_=L["p_own"][(g * 4 + tt) * 128:(g * 4 + tt + 1) * 128, :]), writes=[B_pin[pi]], dma=True)
                S.op("pool", lambda e, pi=pi: e.tensor_copy(out=pinb[pi][:, :], in_=pin[pi][:, :]), reads=[B_pin[pi]], writes=[B_pinb[pi]])
                for kp in range(2):
                    S.op("pe", lambda e, kp=kp, pi=pi: e.transpose(out=pbb[4][:, kp * 128:(kp + 1) * 128], in_=pinb[pi][:, kp * 128:(kp + 1) * 128], identity=identb[:, :]),
                         reads=[B_pinb[pi], B_const], pwrites=[B_pb[4]])
                S.op("dve", lambda e, tt=tt: e.tensor_copy(out=pT[:, :, tt * 128:(tt + 1) * 128], in_=pbb[4][:, 0:256].rearrange("p (k t) -> p k t", k=2)), reads=[B_pb[4]], pwrites=[B_pT])
            for c in range(8):
                wi = c3["ws"] % 4; c3["ws"] += 1
                S.op("sp", lambda e, c=c, wi=wi: e.dma_start(out=ws[wi][:, :, :], in_=L["wb_pg"].ap()[:, c * 256:(c + 1) * 256].rearrange("(k p) n -> p k n", p=128)),
                     reads=[B_w["pg"]], writes=[B_ws[wi]], dma=True)
                for tt in range(4):
                    ai = (c3["a"] % 2) * 2; c3["a"] += 1
                    bi_ = ai + 1
                    for k in range(KD):
                        S.op("pe", lambda e, k=k, tt=tt, wi=wi, ai=ai: e.matmul(pb[ai][:, 0:256], lhsT=mT[:, k, tt * 128:(tt + 1) * 128], rhs=ws[wi][:, k, :], start=(k == 0), stop=(k == KD - 1)),
                             reads=[B_mT, B_ws[wi]], pwrites=[B_pb[ai]])
                    for kp in range(2):
                        S.op("pe", lambda e, kp=kp, tt=tt, c=c, bi_=bi_: e.matmul(pb[bi_][:, 0:256], lhsT=pT[:, kp, tt * 128:(tt + 1) * 128], rhs=wpp[:, kp, c * 256:(c + 1) * 256], start=(kp == 0), stop=(kp == 1)),
                             reads=[B_pT, B_c4], pwrites=[B_pb[bi_]])
                    gj = tt % 2
                    S.op("dve", lambda e, ai=ai, c=c, gj=gj: e.tensor_tensor(out=gtmp[gj][:, :], in0=pb[ai][:, 0:256], in1=bpg_t[:, c * 256:(c + 1) * 256], op=ALU.add),
                         reads=[B_pb[ai], B_c4], writes=[B_gtmp[gj]])
                    S.op("act", lambda e, gj=gj: e.activation(out=gtmp[gj][:, :], in_=gtmp[gj][:, :], func=AF.Sigmoid), reads=[B_gtmp[gj]], writes=[B_gtmp[gj]])
                    S.op("dve", lambda e, bi_=bi_, gj=gj: e.tensor_tensor(out=gtmp[gj][:, :], in0=pb[bi_][:, 0:256], in1=gtmp[gj][:, :], op=ALU.mult),
                         reads=[B_pb[bi_], B_gtmp[gj]], writes=[B_gtmp[gj]])
                    S.op("pool", lambda e, tt=tt, c=c, gj=gj: e.tensor_tensor(out=xres[:, tt, c * 256:(c + 1) * 256], in0=xres[:, tt, c * 256:(c + 1) * 256], in1=gtmp[gj][:, :], op=ALU.add),
                         reads=[B_gtmp[gj], B_xres[tt]], writes=[B_xres[tt]])
            for tt in range(4):
                S.op("sp", lambda e, g=g, tt=tt: e.dma_start(out=L["out"][(g * 4 + tt) * 128:(g * 4 + tt + 1) * 128, :], in_=xres[:, tt, :]), reads=[B_xres[tt]], dma=True)
        st3 = S.emit()
    print("phase3", st3, flush=True)


def _tables(NBH, s):
    TH = NBH * 256
    NBK = 2 * NBH
    f32 = np.float32
    scale = f32(1.0 / np.sqrt(HD))
    inv_a = np.power(f32(10000.0), -(np.arange(0, HD, 2, dtype=f32) / f32(HD))).astype(f32)
    inv_r = np.power(f32(10000.0), -np.linspace(0.0, 1.0, HD // 2, dtype=f32)).astype(f32)

    def rope_tab(pos, inv):
        ang = (pos.astype(f32)[:, None] * inv[None, :]).astype(f32)
        return np.concatenate([np.cos(ang), np.sin(ang)], axis=1).astype(f32)

    pos_pre = np.arange(TH)
    pos_own = s * TH + np.arange(TH)
    t = {}
    t["rope_a_pre"] = rope_tab(pos_pre, inv_a); t["rope_a_own"] = rope_tab(pos_own, inv_a)
    t["rope_r_pre"] = rope_tab(pos_pre, inv_r); t["rope_r_own"] = rope_tab(pos_own, inv_r)
    gam = (1.0 - np.power(2.0, -5.0 - np.arange(H, dtype=np.float64)))
    lg = np.log(gam)
    j = np.arange(128, dtype=np.float64)
    t["c_kdec"] = (float(scale) * np.exp(lg[None, :] * (127.0 - j)[:, None])).astype(f32)
    t["c_qdec"] = np.exp(lg[None, :] * (j + 1.0)[:, None]).astype(f32)
    t["c_gch"] = np.broadcast_to(np.exp(lg * 128.0)[None, :], (128, H)).astype(f32).copy()
    m = j[:, None, None]; tq = j[None, None, :]
    dec = float(scale) * np.exp(-lg[None, :, None] * (m + 1.0)) * (m <= tq)
    t["c_decT"] = dec.astype(f32).reshape(128, H * 128)
    past = np.full((NBH, NBK), NEGP, dtype=f32)
    diag = np.full((NBH, NBK), -3.0e38, dtype=f32)
    for i in range(NBH):
        if s == 1:
            past[i, :NBH] = 0.0
        past[i, NBH:NBH + i] = 0.0
        diag[i, NBH + i] = 0.0
    t["c_past"] = np.broadcast_to(past.reshape(1, -1), (128, NBH * NBK)).copy()
    t["c_diag"] = np.broadcast_to(diag.reshape(1, -1), (128, NBH * NBK)).copy()
    tk = np.arange(128)[:, None, None]; pp = np.arange(4)[None, :, None]; cc = np.arange(512)[None, None, :]
    t["c_mask"] = ((pp * 128 + tk) <= cc).astype(f32).reshape(128, 4 * 512).astype(ml_dtypes.bfloat16)
    oh = np.zeros((128, NBK, 128), dtype=f32)
    for n in range(NBK):
        oh[n, n, :] = 1.0
    t["c_onehot"] = oh.reshape(128, NBK * 128).astype(ml_dtypes.bfloat16)
    t["c_identb"] = np.eye(128, dtype=f32).astype(ml_dtypes.bfloat16)
    t["c_identf"] = np.eye(128, dtype=f32)
    return t


def make_in_maps(inputs, NBH, n_batch):
    TH = NBH * 256
    x = np.asarray(inputs["x"], dtype=np.float32)
    p = np.asarray(inputs["p"], dtype=np.float32)
    shared = {
        "w_in": inputs["w_in"][0], "w_o": inputs["w_o"][0], "w_gate": inputs["w_gate"][0], "w_up": inputs["w_up"][0],
        "w_down": inputs["w_down"][0], "w_pg": inputs["w_ple_gate"][0], "w_pp": inputs["w_ple_proj"][0],
        "g_mix": inputs["g_mix"][0], "g_ffn": inputs["g_ffn"][0], "g_ple": inputs["g_ple"][0],
        "q_norm": inputs["q_norm"][0], "k_norm": inputs["k_norm"][0], "g_ret": inputs["g_ret"][0],
        "b_pg": inputs["b_ple_gate"][0],
    }
    shared = {k: np.ascontiguousarray(np.asarray(v, dtype=np.float32)) for k, v in shared.items()}
    tabs = [_tables(NBH, 0), _tables(NBH, 1)]
    maps = []
    for c in range(2 * n_batch):
        b, s = c // 2, c % 2
        m = dict(shared)
        m.update(tabs[s])
        m["x_own"] = np.ascontiguousarray(x[b, s * TH:(s + 1) * TH])
        m["x_pre"] = np.ascontiguousarray(x[b, 0:TH]) if s == 1 else np.zeros((TH, D), np.float32)
        m["p_own"] = np.ascontiguousarray(p[0, b, s * TH:(s + 1) * TH])
        maps.append(m)
    return maps


_NC_CACHE = {}


def kernel(**inputs):
    x = np.asarray(inputs["x"])
    B, T, _ = x.shape
    NBH = T // 512
    key = (NBH,)
    if key not in _NC_CACHE:
        _NC_CACHE[key] = build(NBH)
    nc = _NC_CACHE[key]
    maps = make_in_maps(inputs, NBH, B)
    res = run_bass_kernel_spmd(nc, maps, core_ids=list(range(2 * B)))
    TH = NBH * 256
    out = np.empty((B, T, D), np.float32)
    for c in range(2 * B):
        b, s = c // 2, c % 2
        out[b, s * TH:(s + 1) * TH] = np.asarray(res.results[c]["out"], dtype=np.float32)
    return out
```
